# Optimizing a Trainium2 kernel written in Bass

```python
import math
import jax, jax.numpy as jnp
from jax import lax
import numpy as np

D_MODEL = 1024
BATCH = 8
SEQ = 4096
DEPTH = 2

N_MIXERS = 2
ROPE_THETA = 10000.0
NORM_EPS = 1e-6
Q_BLOCK = 128
D_FF = 2816
FFN_RESIDUAL_WEIGHT = 0.5

DIFF_HEADS = 8
DIFF_HEAD_DIM = D_MODEL // DIFF_HEADS // 2

MLA_HEADS = 8
MLA_NOPE = 128
MLA_ROPE = 64
MLA_V = 128
MLA_Q_RANK = 384
MLA_KV_RANK = 256

kernel_name = "hybrid_diffattn_mla_macaron"


def rmsnorm(x, gain):
    x32 = x.astype(jnp.float32)
    y = x32 * lax.rsqrt(jnp.mean(x32 * x32, axis=-1, keepdims=True) + NORM_EPS)
    return (y * gain.astype(jnp.float32)).astype(x.dtype)


def rope_tables(positions, dim):
    inv_freq = ROPE_THETA ** (-jnp.arange(0, dim, 2, dtype=jnp.float32) / dim)
    ang = positions.astype(jnp.float32)[..., None] * inv_freq
    return jnp.cos(ang), jnp.sin(ang)


def rope(x, cos, sin):
    shape = cos.shape[:2] + (1,) * (x.ndim - 3) + cos.shape[2:]
    c, s = cos.reshape(shape), sin.reshape(shape)
    x1, x2 = jnp.split(x.astype(jnp.float32), 2, axis=-1)
    return jnp.concatenate([x1 * c - x2 * s, x2 * c + x1 * s], axis=-1).astype(x.dtype)


def swiglu(h, w_gate, w_up, w_down):
    return (jax.nn.silu(h @ w_gate) * (h @ w_up)) @ w_down


def causal_softmax(scores, q_start, seq):
    q_pos = q_start + jnp.arange(Q_BLOCK)
    mask = jnp.arange(seq)[None, :] <= q_pos[:, None]
    return jax.nn.softmax(jnp.where(mask, scores, -jnp.inf), axis=-1)


def sweep_query_blocks(block_fn, q, seq):
    n_blocks = seq // Q_BLOCK
    def to_blocks(a):
        a = a.reshape((a.shape[0], n_blocks, Q_BLOCK) + a.shape[2:])
        return jnp.moveaxis(a, 1, 0)
    q_blocks = jax.tree_util.tree_map(to_blocks, q)
    starts = jnp.arange(n_blocks) * Q_BLOCK
    out = lax.map(lambda args: block_fn(args[0], args[1]), (q_blocks, starts))
    out = jnp.moveaxis(out, 0, 1)
    return out.reshape((out.shape[0], seq) + out.shape[3:])


def diff_attention(h, w_in, lq1, lk1, lq2, lk2, sub_gain, w_out, cos, sin, lambda_init):
    b, s, _ = h.shape
    q, k, v = jnp.split(h @ w_in, 3, axis=-1)
    q = rope(q.reshape(b, s, DIFF_HEADS, 2, DIFF_HEAD_DIM), cos, sin)
    k = rope(k.reshape(b, s, DIFF_HEADS, 2, DIFF_HEAD_DIM), cos, sin)
    v = v.reshape(b, s, DIFF_HEADS, 2 * DIFF_HEAD_DIM)
    f32 = jnp.float32
    lam = (jnp.exp(jnp.sum(lq1.astype(f32) * lk1.astype(f32)))
           - jnp.exp(jnp.sum(lq2.astype(f32) * lk2.astype(f32))) + lambda_init)
    scale = DIFF_HEAD_DIM ** -0.5

    def block(q_blk, start):
        sc = jnp.einsum('bqhcd,bkhcd->bhcqk', q_blk, k).astype(f32) * scale
        p = causal_softmax(sc, start, s)
        a = p[:, :, 0] - lam * p[:, :, 1]
        return jnp.einsum('bhqk,bkhe->bqhe', a.astype(v.dtype), v)

    o = sweep_query_blocks(block, q, s)
    o = rmsnorm(o, sub_gain) * (1.0 - lambda_init)
    return o.reshape(b, s, DIFF_HEADS * 2 * DIFF_HEAD_DIM) @ w_out


def mla_attention(h, w_in, q_norm, w_q_up, kv_norm, w_kv_up, w_out, cos, sin):
    b, s, _ = h.shape
    c = h @ w_in
    cq = c[..., :MLA_Q_RANK]
    ckv = c[..., MLA_Q_RANK:MLA_Q_RANK + MLA_KV_RANK]
    k_rope = rope(c[..., MLA_Q_RANK + MLA_KV_RANK:], cos, sin)
    q = (rmsnorm(cq, q_norm) @ w_q_up).reshape(b, s, MLA_HEADS, MLA_NOPE + MLA_ROPE)
    q_nope, q_rope = q[..., :MLA_NOPE], rope(q[..., MLA_NOPE:], cos, sin)
    kv = (rmsnorm(ckv, kv_norm) @ w_kv_up).reshape(b, s, MLA_HEADS, MLA_NOPE + MLA_V)
    k_nope, v = kv[..., :MLA_NOPE], kv[..., MLA_NOPE:]
    scale = (MLA_NOPE + MLA_ROPE) ** -0.5

    def block(q_blk, start):
        qn, qr = q_blk
        sc = (jnp.einsum('bqhd,bkhd->bhqk', qn, k_nope)
              + jnp.einsum('bqhr,bkr->bhqk', qr, k_rope)).astype(jnp.float32) * scale
        p = causal_softmax(sc, start, s)
        return jnp.einsum('bhqk,bkhd->bqhd', p.astype(v.dtype), v)

    o = sweep_query_blocks(block, (q_nope, q_rope), s)
    return o.reshape(b, s, MLA_HEADS * MLA_V) @ w_out


def setup_inputs(seed: int = 0) -> dict:
    key = jax.random.key(seed)
    ks = iter(jax.random.split(key, 40))
    n_diff = (DEPTH + 1) // 2
    n_mla = DEPTH // 2

    def w(shape, fan_in):
        return jax.random.normal(next(ks), shape, jnp.float32) * (fan_in ** -0.5)

    def gain(shape):
        return 1.0 + 0.02 * jax.random.normal(next(ks), shape, jnp.float32)

    x = jax.random.normal(next(ks), (BATCH, SEQ, D_MODEL), jnp.float32)
    offset = jax.random.randint(next(ks), (BATCH, 1), 0, 1024, dtype=jnp.int32)
    positions = offset + jnp.arange(SEQ, dtype=jnp.int32)[None, :]
    return {
        "x": x,
        "positions": positions,
        "ffn1_norm": gain((DEPTH, D_MODEL)),
        "ffn1_w_gate": w((DEPTH, D_MODEL, D_FF), D_MODEL),
        "ffn1_w_up": w((DEPTH, D_MODEL, D_FF), D_MODEL),
        "ffn1_w_down": w((DEPTH, D_FF, D_MODEL), D_FF),
        "mix_norm": gain((DEPTH, D_MODEL)),
        "ffn2_norm": gain((DEPTH, D_MODEL)),
        "ffn2_w_gate": w((DEPTH, D_MODEL, D_FF), D_MODEL),
        "ffn2_w_up": w((DEPTH, D_MODEL, D_FF), D_MODEL),
        "ffn2_w_down": w((DEPTH, D_FF, D_MODEL), D_FF),
        "diff_w_in": w((n_diff, D_MODEL, 3 * D_MODEL), D_MODEL),
        "diff_lambda_q1": 0.1 * jax.random.normal(next(ks), (n_diff, DIFF_HEAD_DIM), jnp.float32),
        "diff_lambda_k1": 0.1 * jax.random.normal(next(ks), (n_diff, DIFF_HEAD_DIM), jnp.float32),
        "diff_lambda_q2": 0.1 * jax.random.normal(next(ks), (n_diff, DIFF_HEAD_DIM), jnp.float32),
        "diff_lambda_k2": 0.1 * jax.random.normal(next(ks), (n_diff, DIFF_HEAD_DIM), jnp.float32),
        "diff_sub_norm": gain((n_diff, 2 * DIFF_HEAD_DIM)),
        "diff_w_out": w((n_diff, D_MODEL, D_MODEL), D_MODEL),
        "mla_w_in": w((n_mla, D_MODEL, MLA_Q_RANK + MLA_KV_RANK + MLA_ROPE), D_MODEL),
        "mla_q_norm": gain((n_mla, MLA_Q_RANK)),
        "mla_w_q_up": w((n_mla, MLA_Q_RANK, MLA_HEADS * (MLA_NOPE + MLA_ROPE)), MLA_Q_RANK),
        "mla_kv_norm": gain((n_mla, MLA_KV_RANK)),
        "mla_w_kv_up": w((n_mla, MLA_KV_RANK, MLA_HEADS * (MLA_NOPE + MLA_V)), MLA_KV_RANK),
        "mla_w_out": w((n_mla, MLA_HEADS * MLA_V, D_MODEL), MLA_HEADS * MLA_V),
        "final_norm": gain((D_MODEL,)),
    }


def reference(x, positions, ffn1_norm, ffn1_w_gate, ffn1_w_up, ffn1_w_down, mix_norm,
              ffn2_norm, ffn2_w_gate, ffn2_w_up, ffn2_w_down,
              diff_w_in, diff_lambda_q1, diff_lambda_k1, diff_lambda_q2, diff_lambda_k2,
              diff_sub_norm, diff_w_out,
              mla_w_in, mla_q_norm, mla_w_q_up, mla_kv_norm, mla_w_kv_up, mla_w_out,
              final_norm):
    cos_d, sin_d = rope_tables(positions, DIFF_HEAD_DIM)
    cos_m, sin_m = rope_tables(positions, MLA_ROPE)
    for i in range(DEPTH):
        x = x + FFN_RESIDUAL_WEIGHT * swiglu(rmsnorm(x, ffn1_norm[i]),
                                             ffn1_w_gate[i], ffn1_w_up[i], ffn1_w_down[i])
        h = rmsnorm(x, mix_norm[i])
        j = i // N_MIXERS
        if i % N_MIXERS == 0:
            lambda_init = 0.8 - 0.6 * math.exp(-0.3 * i)
            x = x + diff_attention(h, diff_w_in[j], diff_lambda_q1[j], diff_lambda_k1[j],
                                   diff_lambda_q2[j], diff_lambda_k2[j], diff_sub_norm[j],
                                   diff_w_out[j], cos_d, sin_d, lambda_init)
        else:
            x = x + mla_attention(h, mla_w_in[j], mla_q_norm[j], mla_w_q_up[j], mla_kv_norm[j],
                                  mla_w_kv_up[j], mla_w_out[j], cos_m, sin_m)
        x = x + FFN_RESIDUAL_WEIGHT * swiglu(rmsnorm(x, ffn2_norm[i]),
                                             ffn2_w_gate[i], ffn2_w_up[i], ffn2_w_down[i])
    return rmsnorm(x, final_norm)
```

```python
import math
from contextlib import ExitStack

import numpy as np
import concourse.bass as bass
import concourse.mybir as mybir
from concourse.bass_utils import run_bass_kernel_spmd

F32 = mybir.dt.float32
BF16 = mybir.dt.bfloat16
I32 = mybir.dt.int32
AF = mybir.ActivationFunctionType
ALU = mybir.AluOpType
AX = mybir.AxisListType

D = 1024
S = 4096
DFF = 2816
NCH = 8
TT = 512
NT = S // TT
EPS = 1e-6
TWO_PI = 2.0 * math.pi

SAME_ENGINE_SYNC = True


class Buf:
    __slots__ = ("name", "w", "r")

    def __init__(self, name=""):
        self.name = name
        self.w = None
        self.r = {}


class DmaSem:
    def __init__(self, handle, sid):
        self.h = handle
        self.sid = sid
        self.count = 0


class Prog:
    ENGS = ("pe", "act", "dve", "pool", "sp")

    def __init__(self, nc, es):
        self.nc = nc
        self.sem = {}
        self.cnt = {}
        self.seen = {}
        self.lists = {}
        self._sid = 0
        for e in self.ENGS:
            h = es.enter_context(nc.semaphore("prog_" + e))
            self.sem[e] = (h, self._next_sid())
            self.cnt[e] = 0
            self.seen[e] = {}
            self.lists[e] = []
        self.es = es
        self.n_ops = 0
        self.pending_dma = {}

    def _next_sid(self):
        self._sid += 1
        return self._sid

    def dma_sem(self, name):
        _UNIQ[0] += 1
        h = self.es.enter_context(self.nc.semaphore(f"{name}_u{_UNIQ[0]}"))
        return DmaSem(h, self._next_sid())

    def _waits(self, eng, reads, writes):
        evs = []
        for b in reads:
            if b.w is not None:
                evs.append(b.w)
        for b in writes:
            if b.w is not None:
                evs.append(b.w)
            evs.extend(b.r.values())
        need = {}
        seen = self.seen[eng]
        for (sid, h, val, src) in evs:
            if src == eng and (eng in ("pe", "sp") or not SAME_ENGINE_SYNC):
                continue
            if seen.get(sid, 0) >= val:
                continue
            if sid not in need or need[sid][1] < val:
                need[sid] = (h, val)
        for sid, (h, val) in need.items():
            self.lists[eng].append(("w", h, val))
            seen[sid] = val

    def _commit(self, ev, reads, writes):
        for b in writes:
            b.w = ev
            b.r = {}
        for b in reads:
            b.r[ev[0]] = ev

    def op(self, eng, fn, reads=(), writes=()):
        self._waits(eng, reads, writes)
        h, sid = self.sem[eng]
        self.cnt[eng] += 1
        ev = (sid, h, self.cnt[eng], eng)
        self.lists[eng].append(("o", fn, h, 1))
        self._commit(ev, reads, writes)
        self.n_ops += 1
        return ev

    def dma(self, eng, out, in_, dsem, reads=(), writes=(), sbuf=True):
        return self.dma_group(eng, [(out, in_)], dsem, reads, writes, sbuf=sbuf)

    def dma_group(self, eng, pairs, dsem, reads=(), writes=(), sbuf=True):
        self._waits(eng, reads, writes)
        for (out, in_) in pairs:
            dsem.count += 16
            self.lists[eng].append(("o", lambda e, o=out, i=in_: e.dma_start(out=o, in_=i), dsem.h, 16))
        ev = (dsem.sid, dsem.h, dsem.count, "dma")
        self._commit(ev, reads, writes)
        if sbuf:
            self.pending_dma[dsem.sid] = ev
        return ev

    def wait_event(self, eng, ev):
        sid, h, val, _ = ev
        if self.seen[eng].get(sid, 0) < val:
            self.lists[eng].append(("w", h, val))
            self.seen[eng][sid] = val

    def barrier(self):
        for e in self.ENGS:
            for ev in self.pending_dma.values():
                self.wait_event(e, ev)
        self.pending_dma = {}
        for e in self.ENGS:
            for o in self.ENGS:
                if o == e or self.cnt[o] == 0:
                    continue
                h, sid = self.sem[o]
                if self.seen[e].get(sid, 0) < self.cnt[o]:
                    self.lists[e].append(("w", h, self.cnt[o]))
                    self.seen[e][sid] = self.cnt[o]

    def flush(self):
        lists = self.lists
        self.lists = {e: [] for e in self.ENGS}

        def replay(handle, items):
            for it in items:
                if it[0] == "w":
                    handle.wait_ge(it[1], it[2])
                else:
                    ins = it[1](handle)
                    ins.then_inc(it[2], it[3])

        with self.nc.Block(no_gpsimd_drain=True) as block:
            if lists["pe"]:
                @block.tensor
                def _(eng):
                    replay(eng, lists["pe"])
            if lists["act"]:
                @block.scalar
                def _(eng):
                    replay(eng, lists["act"])
            if lists["dve"]:
                @block.vector
                def _(eng):
                    replay(eng, lists["dve"])
            if lists["pool"]:
                @block.gpsimd
                def _(eng):
                    replay(eng, lists["pool"])
            if lists["sp"]:
                @block.sync
                def _(eng):
                    replay(eng, lists["sp"])


_UNIQ = [0]


def SBT(nc, name, shape, dt):
    _UNIQ[0] += 1
    return nc.sbuf_tensor(f"{name}_u{_UNIQ[0]}", shape, dt)


class Ring:
    def __init__(self, items):
        self.items = items
        self.i = 0

    def next(self):
        it = self.items[self.i % len(self.items)]
        self.i += 1
        return it


class Ctx:
    pass


def mk_ring(es, nc, name, n, shape, dt, P=None):
    items = []
    for i in range(n):
        t = es.enter_context(SBT(nc, f"{name}{i}", shape, dt))
        if P is None:
            items.append((t, Buf(f"{name}{i}")))
        else:
            items.append((t, Buf(f"{name}{i}"), P.dma_sem(f"{name}_s{i}")))
    return Ring(items)


def declare_inputs(nc, C):
    def inp(name, shape, dt=F32):
        return nc.dram_tensor(name, list(shape), dt, kind="ExternalInput")

    C.xT = inp("xT", [D, S])
    C.pos = inp("pos", [128, S], I32)
    C.gains = inp("gains", [128, 64])
    C.lamv = inp("lamv", [128, 256])
    C.ones_in = inp("ones_c", [128, 128])
    C.rot_in = inp("rot_c", [128, 128])
    C.mask_in = inp("mask_c", [128, 128])
    C.ident_in = inp("ident_c", [128, 128])
    C.af_in = inp("af_c", [128, 1])
    C.w = {}
    for f in ("ffn1", "ffn2"):
        C.w[f + "_g"] = inp(f + "_w_gate", [2, D, DFF])
        C.w[f + "_u"] = inp(f + "_w_up", [2, D, DFF])
        C.w[f + "_d"] = inp(f + "_w_down", [2, DFF, D])
    C.w["diff_in"] = inp("diff_w_in", [1, D, 3 * D])
    C.w["diff_out"] = inp("diff_w_out", [1, D, D])
    C.w["mla_in"] = inp("mla_w_in", [1, D, 704])
    C.w["mla_q"] = inp("mla_w_q_up", [1, 384, 1536])
    C.w["mla_kv"] = inp("mla_w_kv_up", [1, 256, 2048])
    C.w["mla_out"] = inp("mla_w_out", [1, D, D])
    C.outT = nc.dram_tensor("outT", [D, S], F32, kind="ExternalOutput")
    C.dbg = nc.dram_tensor("dbg", [16, 128, TT], F32, kind="ExternalOutput") if getattr(C, "debug", False) else None
    C.s = {}
    C.sbuf_ = {}
    C.ssem = {}
    C.grp_bufs = {}

    def scr(key, shape):
        C.s[key] = nc.dram_tensor("s_" + key, list(shape), BF16)
        C.sbuf_[key] = Buf("s_" + key)

    for l in range(2):
        for f in ("ffn1", "ffn2"):
            scr(f"{f}_g{l}", [D, DFF])
            scr(f"{f}_u{l}", [D, DFF])
            scr(f"{f}_d{l}", [DFF, D])
    scr("diff_in", [D, 3 * D])
    scr("diff_out", [D, D])
    scr("mla_in", [D, 704])
    scr("mla_q", [384, 1536])
    scr("mla_kv", [256, 2048])
    scr("mla_out", [D, D])
    C.cs_s = nc.dram_tensor("s_cs", [2, 128, S], F32)
    C.cs_sb = [Buf(f"s_cs{t}") for t in range(NT)]
    C.cq_s = nc.dram_tensor("s_cq", [128, 3, S], BF16)
    C.cq_sb = [Buf(f"s_cq{t}") for t in range(NT)]
    C.xn_s = nc.dram_tensor("s_xn", [128, NCH, S], BF16)
    C.xn_sb = [Buf(f"s_xn{t}") for t in range(NT)]


FFN_CHUNKS = [(0, 2), (2, 5), (5, 8), (8, 11)]


def convert_weights(P, C, keys, chunked=False, which=None):
    if chunked:
        f, l = keys[0][:4], int(keys[0][6])
        kg, ku, kd = f"{f}_g{l}", f"{f}_u{l}", f"{f}_d{l}"
        bufs = C.grp_bufs.setdefault((f, l), [None] * 11)
        for ci, (g0, g1) in enumerate(FFN_CHUNKS):
            if which is not None and ci not in which:
                continue
            ds = P.dma_sem(f"cvc_{f}{l}_{g0}")
            b = Buf(f"cvc_{f}{l}_{g0}")
            pairs = []
            c0, c1 = g0 * 256, g1 * 256
            for kk, kind in ((kg, "g"), (ku, "u")):
                src = C.w[f"{f}_{kind}"][l]
                for r0 in range(0, D, 512):
                    pairs.append((C.s[kk][r0:r0 + 512, c0:c1], src[r0:r0 + 512, c0:c1]))
            srcd = C.w[f"{f}_d"][l]
            for r0 in range(c0, c1, 128):
                pairs.append((C.s[kd][r0:r0 + 128, :], srcd[r0:r0 + 128, :]))
            P.dma_group("pool", pairs, ds, reads=(), writes=(b,), sbuf=False)
            for g in range(g0, g1):
                bufs[g] = b
        for k in keys:
            C.ssem[k] = None
        return
    for key in keys:
        if key in C.ssem:
            continue
        ds = P.dma_sem("cv_" + key)
        C.ssem[key] = ds
        dst = C.s[key]
        if key[:3] == "ffn":
            f, kind, l = key[:4], key[5], int(key[6])
            src = C.w[f"{f}_{kind}"][l]
        else:
            src = C.w[key][0]
        rows = dst.shape[0]
        step = 128
        pairs = []
        for r0 in range(0, rows, step):
            r1 = min(rows, r0 + step)
            pairs.append((dst[r0:r1, :], src[r0:r1, :]))
        for b0 in range(0, len(pairs), 2):
            P.dma_group("pool", pairs[b0:b0 + 2], ds, reads=(), writes=(C.sbuf_[key],), sbuf=False)


def setup_persistent(P, C, es):
    nc = C.nc
    C.X = es.enter_context(SBT(nc, "X", [128, NCH, S], F32))
    C.Xb = [[Buf(f"X{c}_{t}") for t in range(NT)] for c in range(NCH)]
    C.gains_sb = es.enter_context(SBT(nc, "gains_sb", [128, 64], F32))
    C.gains_b = Buf("gains")
    C.ones_bf = es.enter_context(SBT(nc, "ones_bf", [128, 128], BF16))
    C.ones_b = Buf("ones")
    C.rot32 = es.enter_context(SBT(nc, "rot32", [128, 128], F32))
    C.rot_b = Buf("rot")
    C.mask_bf = es.enter_context(SBT(nc, "mask_bf", [128, 128], BF16))
    C.mask_b = Buf("mask")
    C.ident_bf = es.enter_context(SBT(nc, "ident_bf", [128, 128], BF16))
    C.ident_b = Buf("ident")
    C.negm_bf = es.enter_context(SBT(nc, "negm_bf", [128, 128], BF16))
    C.negm_b = Buf("negm")
    C.af = es.enter_context(SBT(nc, "af", [128, 1], F32))
    C.af_b = Buf("af")
    C.cst = es.enter_context(SBT(nc, "cst", [128, 4], F32))
    C.cst_b = Buf("cst")
    C.psum = []
    for i in range(8):
        t = es.enter_context(nc.psum_tensor(f"ps{i}", [128, TT], F32))
        C.psum.append((t, Buf(f"ps{i}")))
    C.ld_sem = P.dma_sem("ld_const")
    C.x_sems = [P.dma_sem(f"ld_x{i}") for i in range(8)]
    C.out_sem = P.dma_sem("st_out")


def load_constants(P, C, es):
    nc = C.nc
    tmp = es.enter_context(SBT(nc, "ctmp", [128, 384], F32))
    tb = Buf("ctmp")
    P.dma_group("sp", [(C.gains_sb[:, :], C.gains[:, :]), (C.rot32[:, :], C.rot_in[:, :]), (C.af[:, :], C.af_in[:, :]),
                       (tmp[:, 0:128], C.ones_in[:, :]), (tmp[:, 128:256], C.mask_in[:, :]),
                       (tmp[:, 256:384], C.ident_in[:, :])], C.ld_sem,
                writes=(C.gains_b, C.rot_b, C.af_b, tb))
    P.op("dve", lambda e: e.tensor_copy(out=C.ident_bf[:, :], in_=tmp[:, 256:384]), reads=(tb,), writes=(C.ident_b,))
    P.op("dve", lambda e: e.tensor_scalar(out=C.negm_bf[:, :], in0=tmp[:, 128:256], scalar1=-1.0, scalar2=30000.0,
                                          op0=ALU.add, op1=ALU.mult), reads=(tb,), writes=(C.negm_b,))
    P.op("dve", lambda e: e.tensor_copy(out=C.ones_bf[:, :], in_=tmp[:, 0:128]), reads=(tb,), writes=(C.ones_b,))
    P.op("dve", lambda e: e.tensor_copy(out=C.mask_bf[:, :], in_=tmp[:, 128:256]), reads=(tb,), writes=(C.mask_b,))

    def cfn(e):
        e.memset(C.cst[:, 0:1], -math.pi)
        return e.memset(C.cst[:, 1:2], EPS)
    P.op("dve", cfn, writes=(C.cst_b,))


def load_x(P, C):
    src = C.xT.rearrange("(c p) t -> p c t", p=128)
    for hf in range(2):
        t0 = hf * (S // 2)
        for cg in range(4):
            pairs = [(C.X[:, c, t0:t0 + S // 2], src[:, c, t0:t0 + S // 2]) for c in (2 * cg, 2 * cg + 1)]
            wr = tuple(C.Xb[c][t] for c in (2 * cg, 2 * cg + 1) for t in range(hf * 4, hf * 4 + 4))
            P.dma_group("sp", pairs, C.x_sems[hf * 4 + cg], writes=wr)


def norm_stats(P, C, srcs, src_bufs, nfeat, sq_ring, f32_ring, parts=128, sq_eng="act"):
    ps, psb = C.ps_misc.next()
    n = len(srcs)
    for i, (a, b) in enumerate(zip(srcs, src_bufs)):
        sq, sqb = sq_ring.next()
        if sq_eng == "act":
            P.op("act", lambda e, a=a, sq=sq: e.activation(out=sq[0:parts, :], in_=a, func=AF.Square),
                 reads=(b,), writes=(sqb,))
        else:
            P.op(sq_eng, lambda e, a=a, sq=sq: e.tensor_tensor(out=sq[0:parts, :], in0=a, in1=a, op=ALU.mult),
                 reads=(b,), writes=(sqb,))
        P.op("pe", lambda e, sq=sq, i=i, ps=ps: e.matmul(ps[:, :], lhsT=C.ones_bf[0:parts, :], rhs=sq[0:parts, :],
                                                          start=(i == 0), stop=(i == n - 1)),
             reads=(sqb, C.ones_b), writes=(psb,))
    r, rb = f32_ring.next()
    P.op("act", lambda e: e.activation(out=r[:, :], in_=ps[:, :], func=AF.Ln, bias=C.cst[:, 1:2], scale=1.0 / nfeat),
         reads=(psb, C.cst_b), writes=(rb,))
    P.op("act", lambda e: e.activation(out=r[:, :], in_=r[:, :], func=AF.Exp, scale=-0.5),
         reads=(rb,), writes=(rb,))
    return r, rb


def norm_tile(P, C, t, gcol, outs, out_bufs, sq_ring, f32_ring, eng="dve", sq_eng="act"):
    sl = slice(t * TT, (t + 1) * TT)
    srcs = [C.X[:, c, sl] for c in range(NCH)]
    r, rb = norm_stats(P, C, srcs, [C.Xb[c][t] for c in range(NCH)], D, sq_ring, f32_ring, sq_eng=sq_eng)
    for c in range(NCH):
        P.op(eng, lambda e, c=c: e.scalar_tensor_tensor(out=outs[c], in0=C.X[:, c, sl],
                                                        scalar=C.gains_sb[:, gcol + c:gcol + c + 1],
                                                        in1=r[:, :], op0=ALU.mult, op1=ALU.mult),
             reads=(C.Xb[c][t], rb, C.gains_b), writes=(out_bufs[c],))


def ffn_phase(P, C, l, f):
    nc = C.nc
    G = 2
    NG = DFF // (128 * G)
    ST = 4
    gcol = (l * 3 + (0 if f == "ffn1" else 2)) * 8
    kg, ku, kd = f"{f}_g{l}", f"{f}_u{l}", f"{f}_d{l}"
    with ExitStack() as es:
        XN = es.enter_context(SBT(nc, "ffn_xn", [128, ST, NCH, TT], BF16))
        XNb = [[Buf(f"xn{t}_{c}") for c in range(NCH)] for t in range(ST)]
        Wg = [es.enter_context(SBT(nc, f"ffn_wg{i}", [128, NCH, G * 128], BF16)) for i in range(2)]
        Wu = [es.enter_context(SBT(nc, f"ffn_wu{i}", [128, NCH, G * 128], BF16)) for i in range(2)]
        Wd = [es.enter_context(SBT(nc, f"ffn_wd{i}", [128, G, D], BF16)) for i in range(2)]
        Wb = [Buf(f"ffn_w{i}") for i in range(2)]
        wsem = [P.dma_sem(f"ffn_ws{l}{f}{i}") for i in range(2)]
        sq_ring = mk_ring(es, nc, "ffn_sq", 2, [128, TT], BF16)
        f32_ring = mk_ring(es, nc, "ffn_f32", 2, [128, TT], F32)
        sg_ring = mk_ring(es, nc, "ffn_sg", 2, [128, TT], BF16)
        h_ring = mk_ring(es, nc, "ffn_h", 4, [128, TT], BF16)
        C.ps_misc = Ring(C.psum[4:8])
        psA = C.psum[0:4]
        psB = Ring(C.psum[4:8])

        sg_src = C.s[kg].rearrange("(c p) f -> p c f", p=128)
        su_src = C.s[ku].rearrange("(c p) f -> p c f", p=128)
        sd_src = C.s[kd].rearrange("(g p) d -> p g d", p=128)

        def load_group(g):
            sl = g % 2
            P.dma_group("sp", [(Wg[sl][:, :, :], sg_src[:, :, g * 256:(g + 1) * 256]),
                               (Wu[sl][:, :, :], su_src[:, :, g * 256:(g + 1) * 256]),
                               (Wd[sl][:, :, :], sd_src[:, g * G:(g + 1) * G, :])], wsem[sl],
                        reads=((C.grp_bufs[(f, l)][g],) if (f, l) in C.grp_bufs else (C.sbuf_[kg], C.sbuf_[ku], C.sbuf_[kd])),
                        writes=(Wb[sl],))

        for st in range(NT // ST):
            load_group(0)
            for tl in range(ST):
                t = st * ST + tl
                norm_tile(P, C, t, gcol, [XN[:, tl, c, :] for c in range(NCH)], XNb[tl], sq_ring, f32_ring)
            iters = [(g, tl) for g in range(NG) for tl in range(ST)]

            def GU(it):
                g, tl = it
                sl = g % 2
                hs = []
                for fi in range(G):
                    pg, pgb = psA[2 * fi]
                    pu, pub = psA[2 * fi + 1]

                    def mm(e, w, ps, fi=fi, tl=tl):
                        for c in range(NCH):
                            ins = e.matmul(ps[:, :], lhsT=w[:, c, fi * 128:(fi + 1) * 128], rhs=XN[:, tl, c, :],
                                           start=(c == 0), stop=(c == NCH - 1))
                        return ins
                    P.op("pe", lambda e, mm=mm, w=Wg[sl], ps=pg: mm(e, w, ps), reads=(Wb[sl], *XNb[tl]), writes=(pgb,))
                    P.op("pe", lambda e, mm=mm, w=Wu[sl], ps=pu: mm(e, w, ps), reads=(Wb[sl], *XNb[tl]), writes=(pub,))
                    sg, sgb = sg_ring.next()
                    P.op("act", lambda e, sg=sg, pg=pg: e.activation(out=sg[:, :], in_=pg[:, :], func=AF.Silu),
                         reads=(pgb,), writes=(sgb,))
                    h, hb = h_ring.next()
                    P.op("dve", lambda e, h=h, sg=sg, pu=pu: e.tensor_tensor(out=h[:, :], in0=sg[:, :], in1=pu[:, :],
                                                                             op=ALU.mult),
                         reads=(sgb, pub), writes=(hb,))
                    hs.append((h, hb))
                return hs

            def DOWN(it, hs):
                g, tl = it
                sl = g % 2
                t = st * ST + tl
                tsl = slice(t * TT, (t + 1) * TT)
                for dc in range(NCH):
                    pb, pbb = psB.next()

                    def mm(e, dc=dc, pb=pb):
                        for fi in range(G):
                            ins = e.matmul(pb[:, :], lhsT=Wd[sl][:, fi, dc * 128:(dc + 1) * 128], rhs=hs[fi][0][:, :],
                                           start=(fi == 0), stop=(fi == G - 1))
                        return ins
                    P.op("pe", mm, reads=(Wb[sl], hs[0][1], hs[1][1]), writes=(pbb,))
                    P.op("dve", lambda e, dc=dc, pb=pb: e.scalar_tensor_tensor(
                        out=C.X[:, dc, tsl], in0=pb[:, :], scalar=0.5, in1=C.X[:, dc, tsl],
                        op0=ALU.mult, op1=ALU.add),
                        reads=(pbb, C.Xb[dc][t]), writes=(C.Xb[dc][t],))

            prev = None
            for k, it in enumerate(iters):
                hs = GU(it)
                if prev is not None:
                    DOWN(*prev)
                if it[1] == 0 and it[0] + 1 < NG:
                    load_group(it[0] + 1)
                prev = (it, hs)
            DOWN(*prev)
        P.barrier()
        P.flush()


def rope_tables(P, C, j, pos_ring, f32_ring):
    pt, ptb, psem = pos_ring.next()
    P.dma("sp", pt[:, :], C.pos[:, j * TT:(j + 1) * TT], psem, writes=(ptb,))
    outs = []
    for off in (0.25, 0.0):
        u, ub = f32_ring.next()
        ii, iib = C.i32_ring.next()
        P.op("dve", lambda e, u=u: e.tensor_copy(out=u[:, :], in_=pt[:, :]), reads=(ptb,), writes=(ub,))
        P.op("dve", lambda e, u=u, off=off: e.tensor_scalar(out=u[:, :], in0=u[:, :], scalar1=C.af[:, 0:1],
                                                            scalar2=off, op0=ALU.mult, op1=ALU.add),
             reads=(ub, C.af_b), writes=(ub,))
        P.op("dve", lambda e, u=u, ii=ii: e.tensor_copy(out=ii[:, :], in_=u[:, :]), reads=(ub,), writes=(iib,))
        P.op("dve", lambda e, u=u, ii=ii: e.tensor_tensor(out=u[:, :], in0=u[:, :], in1=ii[:, :], op=ALU.subtract),
             reads=(ub, iib), writes=(ub,))
        P.op("act", lambda e, u=u: e.activation(out=u[:, :], in_=u[:, :], func=AF.Sin, scale=TWO_PI),
             reads=(ub,), writes=(ub,))
        outs.append((u, ub))
    return outs[0], outs[1]


def tables_phase(P, C, es):
    nc = C.nc
    if True:
        f32_ring = mk_ring(es, nc, "tb_f32", 16, [128, TT], F32)
        C.i32_ring = mk_ring(es, nc, "tb_i32", 4, [128, TT], I32)
        pos_ring = mk_ring(es, nc, "tb_pos", 8, [128, TT], I32, P=P)
        st_sems = [P.dma_sem(f"tb_st{i}") for i in range(16)]
        for j in range(NT):
            cosT, sinT = rope_tables(P, C, j, pos_ring, f32_ring)
            sl = slice(j * TT, (j + 1) * TT)
            P.dma("sp", C.cs_s[0, :, sl], cosT[0][:, :], st_sems[2 * j], reads=(cosT[1],), writes=(C.cs_sb[j],))
            P.dma("sp", C.cs_s[1, :, sl], sinT[0][:, :], st_sems[2 * j + 1], reads=(sinT[1],), writes=(C.cs_sb[j],))


def apply_rope(P, C, ps, psb, parts, cosT, sinT, out_ap, out_buf, f32_ring, scale_ap=None, scale_buf=None):
    (cs, csb), (sn, snb) = cosT, sinT
    q32, q32b = f32_ring.next()
    if scale_ap is None:
        P.op("act", lambda e: e.activation(out=q32[0:parts, :], in_=ps[0:parts, :], func=AF.Copy),
             reads=(psb,), writes=(q32b,))
    else:
        P.op("dve", lambda e: e.tensor_tensor(out=q32[0:parts, :], in0=ps[0:parts, :], in1=scale_ap[0:parts, :],
                                              op=ALU.mult),
             reads=(psb, scale_buf), writes=(q32b,))
    pr, prb = C.ps_misc.next()
    P.op("pe", lambda e: e.matmul(pr[0:parts, :], lhsT=C.rot32[0:parts, 0:parts], rhs=q32[0:parts, :],
                                  start=True, stop=True),
         reads=(q32b, C.rot_b), writes=(prb,))
    t1, t1b = f32_ring.next()
    P.op("dve", lambda e: e.tensor_tensor(out=t1[0:parts, :], in0=q32[0:parts, :], in1=cs[0:parts, :], op=ALU.mult),
         reads=(q32b, csb), writes=(t1b,))
    t2, t2b = f32_ring.next()
    P.op("dve", lambda e: e.tensor_tensor(out=t2[0:parts, :], in0=pr[0:parts, :], in1=sn[0:parts, :], op=ALU.mult),
         reads=(prb, snb), writes=(t2b,))
    P.op("dve", lambda e: e.tensor_tensor(out=out_ap, in0=t1[0:parts, :], in1=t2[0:parts, :], op=ALU.add),
         reads=(t1b, t2b), writes=(out_buf,))


def final_phase(P, C):
    nc = C.nc
    gcol = 48
    dst = C.outT.rearrange("(c p) t -> p c t", p=128)
    with ExitStack() as es:
        sq_ring = mk_ring(es, nc, "fin_sq", 2, [128, TT], BF16)
        f32_ring = mk_ring(es, nc, "fin_f32", 2, [128, TT], F32)
        o_ring = mk_ring(es, nc, "fin_o", 6, [128, TT], F32, P=P)
        C.ps_misc = Ring(C.psum[0:4])
        evs = {}
        for t in range(NT):
            sl = slice(t * TT, (t + 1) * TT)
            srcs = [C.X[:, c, sl] for c in range(NCH)]
            r, rb = norm_stats(P, C, srcs, [C.Xb[c][t] for c in range(NCH)], D, sq_ring, f32_ring)
            for c in range(NCH):
                o, ob, osem = o_ring.next()
                P.op("dve", lambda e, c=c, o=o, sl=sl, r=r: e.scalar_tensor_tensor(
                    out=o[:, :], in0=C.X[:, c, sl], scalar=C.gains_sb[:, gcol + c:gcol + c + 1], in1=r[:, :],
                    op0=ALU.mult, op1=ALU.mult),
                    reads=(C.Xb[c][t], rb, C.gains_b), writes=(ob,))
                evs[osem.sid] = P.dma("sp", dst[:, c, sl], o[:, :], osem, reads=(ob,))
        for ev in evs.values():
            P.wait_event("sp", ev)
        P.barrier()
        P.flush()


def store_x_raw(P, C):
    dst = C.outT.rearrange("(c p) t -> p c t", p=128)
    pairs = []
    rd = []
    for c in range(NCH):
        for hf in range(2):
            t0 = hf * (S // 2)
            pairs.append((dst[:, c, t0:t0 + S // 2], C.X[:, c, t0:t0 + S // 2]))
        rd.extend(C.Xb[c])
    ev = P.dma_group("sp", pairs, C.out_sem, reads=tuple(rd))
    P.wait_event("sp", ev)
    P.barrier()
    P.flush()


WEIGHT_KEYS = {
    "ffn1_0": ["ffn1_g0", "ffn1_u0", "ffn1_d0"],
    "diff": ["diff_in", "diff_out"],
    "ffn2_0": ["ffn2_g0", "ffn2_u0", "ffn2_d0"],
    "ffn1_1": ["ffn1_g1", "ffn1_u1", "ffn1_d1"],
    "mla": ["mla_in", "mla_q", "mla_kv", "mla_out"],
    "ffn2_1": ["ffn2_g1", "ffn2_u1", "ffn2_d1"],
}
ALL_PHASES = ["ffn1_0", "diff", "ffn2_0", "ffn1_1", "mla", "ffn2_1", "final"]


def build(phases, debug=False, nheads=8, ntiles=NT, head0=0):
    nc = bass.Bass("TRN2", target_bir_lowering=False)
    C = Ctx()
    C.nc = nc
    C.debug = debug
    C.nheads = nheads
    C.ntiles = ntiles
    C.head0 = head0
    C.pe_mask = True
    declare_inputs(nc, C)
    with ExitStack() as es:
        P = Prog(nc, es)
        setup_persistent(P, C, es)
        wphases = [ph for ph in phases if ph in WEIGHT_KEYS]
        with ExitStack() as es0:
            first_chunked = bool(wphases) and wphases[0].startswith("ffn")
            if wphases:
                convert_weights(P, C, WEIGHT_KEYS[wphases[0]], chunked=first_chunked,
                                which=(0, 1) if first_chunked else None)
            load_constants(P, C, es0)
            load_x(P, C)
            if "diff" in phases or "mla" in phases:
                tables_phase(P, C, es0)
            P.barrier()
            P.flush()
        tables_done = True
        for ph in phases:
            if ph in WEIGHT_KEYS:
                k = wphases.index(ph)
                if k == 0 and first_chunked:
                    convert_weights(P, C, WEIGHT_KEYS[ph], chunked=True, which=(2, 3))
                hook = None
                if ph in ("diff", "mla"):
                    def hook(k=k):
                        pieces = []
                        kk = k + 1
                        while kk < len(wphases):
                            for key in WEIGHT_KEYS[wphases[kk]]:
                                pieces.append(lambda key=key: convert_weights(P, C, [key]))
                            if wphases[kk] in ("diff", "mla"):
                                break
                            kk += 1
                        return pieces
                elif k + 1 < len(wphases):
                    convert_weights(P, C, WEIGHT_KEYS[wphases[k + 1]])
            if ph in ("diff", "mla") and not tables_done:
                tables_phase(P, C)
                tables_done = True
            if ph.startswith("ffn"):
                ffn_phase(P, C, int(ph[5]), ph[:4])
            elif ph == "diff":
                from_diff(P, C, hook)
            elif ph == "mla":
                from_mla(P, C, hook)
            elif ph == "final":
                final_phase(P, C)
            elif ph == "store":
                store_x_raw(P, C)
    return nc


def dbg_dump(P, C, slot, ap, buf, parts=128, cols=TT):
    if not getattr(C, "debug", False):
        return
    if not hasattr(C, "dbg_sem"):
        C.dbg_sem = P.dma_sem("dbg_sem")
        C.dbg_b = Buf("dbg")
    ev = P.dma("pool", C.dbg[slot, 0:parts, 0:cols], ap, C.dbg_sem, reads=(buf,), writes=(C.dbg_b,))
    P.wait_event("pool", ev)


def attention_tile(P, C, j, nk_comp, score_fn, V, Vb_of, PT_ring, O, L, scale, mask_eng="pool", inject=None):
    nk = 4 * j + 4
    inject = inject or {}

    def S_stage(i):
        q0 = max(0, i - 4 * j) * 128
        res = []
        for c in range(nk_comp):
            ps, psb = C.ps_misc.next()
            fn, rd = score_fn(c, i, q0, ps)
            if i >= 4 * j and C.pe_mask:
                def fn_m(e, fn=fn, ps=ps, q0=q0):
                    fn(e)
                    return e.matmul(ps[:, q0:q0 + 128], lhsT=C.ident_bf[:, :], rhs=C.negm_bf[:, :], start=False, stop=True,
                                    skip_group_check=True)
                P.op("pe", fn_m, reads=(*rd, C.ident_b, C.negm_b), writes=(psb,))
            else:
                P.op("pe", fn, reads=rd, writes=(psb,))
            pt, ptb = PT_ring.next()
            P.op("act", lambda e, pt=pt, ps=ps, q0=q0: e.activation(out=pt[:, q0:TT], in_=ps[:, q0:TT], func=AF.Exp,
                                                                     scale=scale),
                 reads=(psb,), writes=(ptb,))
            if i >= 4 * j and not C.pe_mask:
                P.op(mask_eng, lambda e, pt=pt, q0=q0: e.tensor_tensor(out=pt[:, q0:q0 + 128], in0=pt[:, q0:q0 + 128],
                                                                       in1=C.mask_bf[:, :], op=ALU.mult),
                     reads=(ptb, C.mask_b), writes=(ptb,))
            res.append((pt, ptb))
        return res, q0

    def PV_stage(i, res, q0):
        sb0 = q0 // 128
        for c in range(nk_comp):
            pt, ptb = res[c]

            def fn(e, pt=pt, c=c):
                for sb in range(sb0, 4):
                    bank = O[c][sb // 2][0]
                    col = (sb % 2) * 129
                    ins = e.matmul(bank[:, col:col + 129], lhsT=pt[:, sb * 128:(sb + 1) * 128], rhs=V[:, i, :],
                                   start=(i == 0 and sb % 2 == 0), stop=(i == nk - 1 and sb == 3),
                                   skip_group_check=True)
                return ins
            P.op("pe", fn, reads=(ptb, Vb_of(i), C.vones_b), writes=(O[c][0][1], O[c][1][1]))

    depth = 2 if nk_comp == 1 else 1
    pend = []
    for i in range(nk):
        pend.append((i, S_stage(i)))
        for f in inject.get(i, ()):
            f()
        if len(pend) > depth:
            ii, st = pend.pop(0)
            PV_stage(ii, *st)
    for ii, st in pend:
        PV_stage(ii, *st)


def out_proj_add(P, C, j, on, onb, WO, WOb, dcs=tuple(range(NCH))):
    tsl = slice(j * TT, (j + 1) * TT)
    for dc in dcs:
        ps, psb = C.ps_misc.next()
        P.op("pe", lambda e, dc=dc, ps=ps: e.matmul(ps[:, :], lhsT=WO[:, dc * 128:(dc + 1) * 128], rhs=on[:, :],
                                                    start=True, stop=True),
             reads=(WOb, onb), writes=(psb,))
        P.op("dve", lambda e, dc=dc, ps=ps: e.tensor_tensor(out=C.X[:, dc, tsl], in0=ps[:, :], in1=C.X[:, dc, tsl],
                                                            op=ALU.add),
             reads=(psb, C.Xb[dc][j]), writes=(C.Xb[dc][j],))


def from_diff(P, C, hook=None):
    nc = C.nc
    lambda_init = 0.8 - 0.6 * math.exp(-0.3 * 0)
    gcol = 8
    with ExitStack() as es:
        xr = []
        for i in range(4):
            t = es.enter_context(SBT(nc, f"da_xn{i}", [128, NCH, TT], BF16))
            xr.append((t, [Buf(f"da_xn{i}_{c}") for c in range(NCH)], P.dma_sem(f"da_xs{i}")))
        xr = Ring(xr)
        sq_ring = mk_ring(es, nc, "da_sq", 4, [128, TT], BF16)
        f32_ring = mk_ring(es, nc, "da_f32", 3, [128, TT], F32)
        C.ps_misc = Ring(C.psum[0:4])
        for j in range(NT):
            xn, xnb, xsem = xr.next()
            norm_tile(P, C, j, gcol, [xn[:, c, :] for c in range(NCH)], xnb, sq_ring, f32_ring)
            P.dma("sp", C.xn_s[:, :, j * TT:(j + 1) * TT], xn[:, :, :], xsem, reads=tuple(xnb), writes=(C.xn_sb[j],))
        P.barrier()
        P.flush()
    with ExitStack() as es:
        XN = es.enter_context(SBT(nc, "d_xn", [128, NCH, TT], BF16))
        XNb = Buf("d_xn")
        W = es.enter_context(SBT(nc, "d_w", [128, NCH, 384], BF16))
        Wb = Buf("d_w")
        wsem = P.dma_sem("d_wsem")
        WOs = [(es.enter_context(SBT(nc, f"d_wo{i}", [128, D], BF16)), Buf(f"d_wo{i}"), P.dma_sem(f"d_wos{i}"))
               for i in range(2)]
        KT = es.enter_context(SBT(nc, "d_kt", [128, S], BF16))
        KTb = [Buf(f"d_kt{t}") for t in range(NT)]
        V = es.enter_context(SBT(nc, "d_v", [128, S // 128, 129], BF16))
        Vb = [Buf(f"d_v{t}") for t in range(NT)]
        C.vones_b = Buf("d_vones")
        P.op("pool", lambda e: e.memset(V[:, :, 128:129], 1.0), writes=(C.vones_b,))
        conv_pieces = hook() if hook is not None else []
        qt_ring = mk_ring(es, nc, "d_qt", 2, [128, TT], BF16)
        pt_ring = mk_ring(es, nc, "d_pt", 4, [128, TT], BF16)
        rope_ring = mk_ring(es, nc, "d_rope", 4, [128, TT], F32)
        def fine_ring(name, n, shape, dt, mk):
            return Ring([(es.enter_context(SBT(nc, f"{name}{i}", shape, dt)), mk(i)) for i in range(n)])
        fin_ring = fine_ring("d_fin", 3, [128, TT], F32, lambda i: [Buf(f"d_fin{i}_{sb}") for sb in range(4)])
        onq_ring = fine_ring("d_onq", 2, [128, TT], BF16, lambda i: [Buf(f"d_onq{i}_{sb}") for sb in range(4)])
        junk_ring = fine_ring("d_junk", 2, [128, TT], BF16, lambda i: [Buf(f"d_junk{i}_{sb}") for sb in range(4)])
        small_ring = fine_ring("d_small", 3, [128, 16], F32, lambda i: {
            "rb": [[Buf(f"d_r{i}_{c}{hb}") for hb in range(2)] for c in range(2)],
            "ssb": [Buf(f"d_ss{i}_{sb}") for sb in range(4)], "rsb": Buf(f"d_rs{i}")})
        CS = es.enter_context(SBT(nc, "d_cs", [128, 2, TT], F32))
        CSb = Buf("d_cs")
        cssem = P.dma_sem("d_cssem")
        on_ring = mk_ring(es, nc, "d_on", 2, [128, TT], BF16)
        lam_sb = es.enter_context(SBT(nc, "d_lam", [128, 256], F32))
        lam_b = Buf("d_lam")
        sm = es.enter_context(SBT(nc, "d_sm", [128, 8], F32))
        sm_b = Buf("d_sm")
        lsem = P.dma_sem("d_lsem")
        xnsem = P.dma_sem("d_xnsem")
        C.ps_misc = Ring(C.psum[0:4])
        O = [[C.psum[4], C.psum[5]], [C.psum[6], C.psum[7]]]
        L = None

        P.dma("sp", lam_sb[:, :], C.lamv[:, :], lsem, writes=(lam_b,))
        P.op("dve", lambda e: e.tensor_tensor(out=lam_sb[:, 0:64], in0=lam_sb[:, 0:64], in1=lam_sb[:, 64:128],
                                              op=ALU.mult), reads=(lam_b,), writes=(lam_b,))
        P.op("dve", lambda e: e.tensor_tensor(out=lam_sb[:, 128:192], in0=lam_sb[:, 128:192], in1=lam_sb[:, 192:256],
                                              op=ALU.mult), reads=(lam_b,), writes=(lam_b,))
        P.op("dve", lambda e: e.reduce_sum(out=sm[:, 0:1], in_=lam_sb[:, 0:64], axis=AX.X), reads=(lam_b,),
             writes=(sm_b,))
        P.op("dve", lambda e: e.reduce_sum(out=sm[:, 1:2], in_=lam_sb[:, 128:192], axis=AX.X), reads=(lam_b,),
             writes=(sm_b,))
        P.op("act", lambda e: e.activation(out=sm[:, 2:4], in_=sm[:, 0:2], func=AF.Exp), reads=(sm_b,), writes=(sm_b,))
        P.op("dve", lambda e: e.tensor_tensor(out=sm[:, 4:5], in0=sm[:, 3:4], in1=sm[:, 2:3], op=ALU.subtract),
             reads=(sm_b,), writes=(sm_b,))
        P.op("dve", lambda e: e.tensor_scalar(out=sm[:, 4:5], in0=sm[:, 4:5], scalar1=-lambda_init, scalar2=None,
                                              op0=ALU.add), reads=(sm_b,), writes=(sm_b,))
        P.op("dve", lambda e: e.tensor_scalar(out=sm[:, 5:6], in0=C.gains_sb[:, 56:57], scalar1=1.0 - lambda_init,
                                              scalar2=None, op0=ALU.mult), reads=(sm_b, C.gains_b), writes=(sm_b,))

        src_in = C.s["diff_in"].rearrange("(c p) f -> p c f", p=128)
        heads = list(range(C.head0, C.head0 + C.nheads))
        seq = [(h, j) for h in heads for j in range(C.ntiles)]
        QT_of, fin_of, on_of = {}, {}, {}

        def load_w(h):
            pairs = [(W[:, :, 0:128], src_in[:, :, h * 128:(h + 1) * 128]),
                     (W[:, :, 128:256], src_in[:, :, D + h * 128:D + (h + 1) * 128]),
                     (W[:, :, 256:384], src_in[:, :, 2 * D + h * 128:2 * D + (h + 1) * 128])]
            P.dma_group("sp", pairs, wsem, reads=(C.sbuf_["diff_in"],), writes=(Wb,))
            wo, wob, wosem = WOs[h % 2]
            P.dma("sp", wo[:, :], C.s["diff_out"][h * 128:(h + 1) * 128, :], wosem, reads=(C.sbuf_["diff_out"],),
                  writes=(wob,))

        def PROJ(h, j):
            tsl = slice(j * TT, (j + 1) * TT)
            st = {}

            def proj(e, ps, c0):
                for c in range(NCH):
                    ins = e.matmul(ps[:, :], lhsT=W[:, c, c0:c0 + 128], rhs=XN[:, c, :],
                                   start=(c == 0), stop=(c == NCH - 1))
                return ins

            def p_load():
                P.dma("sp", XN[:, :, :], C.xn_s[:, :, tsl], xnsem, reads=(C.xn_sb[j],), writes=(XNb,))
                P.dma_group("sp", [(CS[:, 0, :], C.cs_s[0, :, tsl]), (CS[:, 1, :], C.cs_s[1, :, tsl])], cssem,
                            reads=(C.cs_sb[j],), writes=(CSb,))

            def p_q():
                st["psq"] = C.ps_misc.next()
                P.op("pe", lambda e: proj(e, st["psq"][0], 0), reads=(Wb, XNb), writes=(st["psq"][1],))
                st["q32"] = rope_ring.next()
                P.op("dve", lambda e: e.tensor_copy(out=st["q32"][0][:, :], in_=st["psq"][0][:, :]),
                     reads=(st["psq"][1],), writes=(st["q32"][1],))

            def p_k():
                st["psk"] = C.ps_misc.next()
                P.op("pe", lambda e: proj(e, st["psk"][0], 128), reads=(Wb, XNb), writes=(st["psk"][1],))
                st["k32"] = rope_ring.next()
                P.op("dve", lambda e: e.tensor_copy(out=st["k32"][0][:, :], in_=st["psk"][0][:, :]),
                     reads=(st["psk"][1],), writes=(st["k32"][1],))

            def p_v():
                psv, psvb = C.ps_misc.next()

                def vproj(e):
                    for sblk in range(4):
                        for c in range(NCH):
                            ins = e.matmul(psv[:, sblk * 128:(sblk + 1) * 128], lhsT=XN[:, c, sblk * 128:(sblk + 1) * 128],
                                           rhs=W[:, c, 256:384], start=(c == 0), stop=(c == NCH - 1))
                    return ins
                P.op("pe", vproj, reads=(Wb, XNb), writes=(psvb,))
                P.op("dve", lambda e: e.tensor_copy(out=V[:, j * 4:(j + 1) * 4, 0:128],
                                                    in_=psv[:, :].rearrange("p (s e) -> p s e", s=4)),
                     reads=(psvb,), writes=(Vb[j],))

            def mk_rope(src, out_fn):
                def f():
                    x32, x32b = st[src]
                    out_ap, out_b = out_fn()
                    pr, prb = C.ps_misc.next()
                    P.op("pe", lambda e: e.matmul(pr[:, :], lhsT=C.rot32[:, :], rhs=x32[:, :], start=True, stop=True),
                         reads=(x32b, C.rot_b), writes=(prb,))
                    t2, t2b = rope_ring.next()
                    P.op("dve", lambda e: e.tensor_tensor(out=t2[:, :], in0=pr[:, :], in1=CS[:, 1, :], op=ALU.mult),
                         reads=(prb, CSb), writes=(t2b,))
                    P.op("dve", lambda e: e.tensor_tensor(out=x32[:, :], in0=x32[:, :], in1=CS[:, 0, :], op=ALU.mult),
                         reads=(x32b, CSb), writes=(x32b,))
                    P.op("dve", lambda e: e.tensor_tensor(out=out_ap, in0=x32[:, :], in1=t2[:, :], op=ALU.add),
                         reads=(x32b, t2b), writes=(out_b,))
                return f

            def q_out():
                QT, QTb = qt_ring.next()
                QT_of[(h, j)] = (QT, QTb)
                return QT[:, :], QTb

            return [p_load, p_q, p_k, p_v, mk_rope("q32", q_out), mk_rope("k32", lambda: (KT[:, tsl], KTb[j]))]

        def EVAC_FIN1(h, j):
            sm8, sb_ = small_ring.next()
            rb, ssb, rsb = sb_["rb"], sb_["ssb"], sb_["rsb"]
            for c in range(2):
                for hb in range(2):
                    bank, bb = O[c][hb]
                    P.op("dve", lambda e, bank=bank, c=c, hb=hb: e.reciprocal(out=sm8[:, c * 4 + hb * 2:c * 4 + hb * 2 + 2],
                                                                              in_=bank[:, 128:258:129]),
                         reads=(bb,), writes=(rb[c][hb],))
            P.op("dve", lambda e: e.tensor_scalar(out=sm8[:, 4:8], in0=sm8[:, 4:8], scalar1=sm[:, 4:5], scalar2=None,
                                                  op0=ALU.mult), reads=(rb[1][0], rb[1][1], sm_b), writes=(rb[1][0], rb[1][1]))
            o4, o4b = fin_ring.next()
            for sb in range(4):
                col = (sb % 2) * 129
                b0, b0b = O[0][sb // 2]
                osl = slice(sb * 128, (sb + 1) * 128)
                P.op("dve", lambda e, b0=b0, col=col, osl=osl, sb=sb: e.tensor_scalar(
                    out=o4[:, osl], in0=b0[:, col:col + 128], scalar1=sm8[:, sb:sb + 1], scalar2=None, op0=ALU.mult),
                    reads=(b0b, rb[0][sb // 2]), writes=(o4b[sb],))
            for sb in range(4):
                col = (sb % 2) * 129
                b1, b1b = O[1][sb // 2]
                osl = slice(sb * 128, (sb + 1) * 128)
                P.op("dve", lambda e, b1=b1, col=col, osl=osl, sb=sb: e.scalar_tensor_tensor(
                    out=o4[:, osl], in0=b1[:, col:col + 128], scalar=sm8[:, 4 + sb:5 + sb], in1=o4[:, osl],
                    op0=ALU.mult, op1=ALU.add),
                    reads=(b1b, rb[1][sb // 2], o4b[sb]), writes=(o4b[sb],))
            P.op("dve", lambda e: e.memset(sm8[:, 8:12], 0.0), reads=(), writes=tuple(ssb))
            junk, junkb = junk_ring.next()
            for sb in range(4):
                osl = slice(sb * 128, (sb + 1) * 128)
                P.op("act", lambda e, osl=osl, sb=sb: e.activation(out=junk[:, osl], in_=o4[:, osl], func=AF.Square,
                                                                   accum_out=sm8[:, 8 + sb:9 + sb]),
                     reads=(o4b[sb], ssb[sb]), writes=(junkb[sb], ssb[sb]))
            P.op("act", lambda e: e.activation(out=sm8[:, 12:16], in_=sm8[:, 8:12], func=AF.Ln, bias=C.cst[:, 1:2],
                                               scale=1.0 / 128), reads=(*ssb, C.cst_b), writes=(rsb,))
            P.op("act", lambda e: e.activation(out=sm8[:, 12:16], in_=sm8[:, 12:16], func=AF.Exp, scale=-0.5),
                 reads=(rsb,), writes=(rsb,))
            onq, onqb = onq_ring.next()
            for sb in range(4):
                osl = slice(sb * 128, (sb + 1) * 128)
                P.op("dve", lambda e, osl=osl, sb=sb: e.tensor_scalar(out=onq[:, osl], in0=o4[:, osl],
                                                                      scalar1=sm8[:, 12 + sb:13 + sb], scalar2=None,
                                                                      op0=ALU.mult),
                     reads=(o4b[sb], rsb), writes=(onqb[sb],))
            fin_of[(h, j)] = (onq, onqb)

        def FIN2(h, j):
            onq, onqb = fin_of.pop((h, j))
            ps, psb = C.ps_misc.next()

            def tr(e):
                for sb in range(4):
                    osl = slice(sb * 128, (sb + 1) * 128)
                    ins = e.matmul(ps[:, osl], lhsT=onq[:, osl], rhs=C.ident_bf[:, :], start=True, stop=True)
                return ins
            P.op("pe", tr, reads=(*onqb, C.ident_b), writes=(psb,))
            on, onb = on_ring.next()
            P.op("act", lambda e: e.activation(out=on[:, :], in_=ps[:, :], func=AF.Copy, scale=sm[:, 5:6]),
                 reads=(psb, sm_b), writes=(onb,))
            on_of[(h, j)] = (on, onb)

        def OUT(h, j):
            def piece(dc):
                def f():
                    on, onb = on_of[(h, j)]
                    wo, wob, _ = WOs[h % 2]
                    out_proj_add(P, C, j, on, onb, wo, wob, dcs=(dc,))
                return f
            return [piece(dc) for dc in range(NCH)]

        def spread(inject, pieces, start, nk):
            for k, f in enumerate(pieces):
                inject.setdefault(min(start + k, nk - 1), []).append(f)

        load_w(seq[0][0])
        for f in PROJ(*seq[0]):
            f()
        for idx, (h, j) in enumerate(seq):
            nk = 4 * j + 4
            inject = {}
            post = []
            if idx > 0:
                ph, pj = seq[idx - 1]
                inject.setdefault(1, []).append(lambda ph=ph, pj=pj: FIN2(ph, pj))
                spread(inject, OUT(ph, pj), 2, nk)
            if idx + 1 < len(seq):
                nh, nj = seq[idx + 1]
                pieces = PROJ(nh, nj)
                if nh != h:
                    pieces = [lambda nh=nh: load_w(nh)] + pieces
                if nh == h:
                    spread(inject, pieces, 1, nk)
                elif nk > 5:
                    spread(inject, pieces, 5, nk)
                else:
                    post.extend(pieces)
            QT, QTb = QT_of.pop((h, j))

            def score_fn(c, i, q0, ps, QT=QT, QTb=QTb):
                def fn(e):
                    return e.matmul(ps[:, q0:TT], lhsT=KT[c * 64:(c + 1) * 64, i * 128:(i + 1) * 128],
                                    rhs=QT[c * 64:(c + 1) * 64, q0:TT], start=True, stop=True)
                return fn, (KTb[i // 4], QTb)
            C.pe_mask = False
            attention_tile(P, C, j, 2, score_fn, V, lambda i: Vb[i // 4], pt_ring, O, L, 0.125, inject=inject,
                           mask_eng="dve")
            C.pe_mask = True
            EVAC_FIN1(h, j)
            for f in post:
                f()
            if conv_pieces and idx % 3 == 0:
                conv_pieces.pop(0)()
        FIN2(*seq[-1])
        for f in OUT(*seq[-1]):
            f()
        for f in conv_pieces:
            f()
        P.barrier()
        P.flush()


def from_mla(P, C, hook=None):
    nc = C.nc
    gcol = 32
    scale = 192.0 ** -0.5
    NQ = 1
    TQ = NT // NQ
    with ExitStack() as es_outer:
        CKVN = es_outer.enter_context(SBT(nc, "m_ckvn", [128, 2, S], BF16))
        CKVNb = [[Buf(f"m_ckvn{m}_{t}") for t in range(NT)] for m in range(2)]
        KR = es_outer.enter_context(SBT(nc, "m_kr", [128, S], BF16))
        KRb = [Buf(f"m_kr{t}") for t in range(NT)]
        wsem = P.dma_sem("m_wsem")
        cssem = P.dma_sem("m_cssem")
        win_src = C.s["mla_in"].rearrange("(c p) f -> p c f", p=128)
        wq_src = C.s["mla_q"].rearrange("(c p) f -> p c f", p=128)
        wkv_src = C.s["mla_kv"].rearrange("(c p) f -> p c f", p=128)

        def rope64(x32, x32b, CS, CSb, rope_ring, out_ap, out_b):
            pr, prb = C.ps_misc.next()
            P.op("pe", lambda e: e.matmul(pr[0:64, :], lhsT=C.rot32[0:64, 0:64], rhs=x32[0:64, :], start=True, stop=True),
                 reads=(x32b, C.rot_b), writes=(prb,))
            t2, t2b = rope_ring.next()
            P.op("dve", lambda e: e.tensor_tensor(out=t2[0:64, :], in0=pr[0:64, :], in1=CS[0:64, 1, :], op=ALU.mult),
                 reads=(prb, CSb), writes=(t2b,))
            P.op("dve", lambda e: e.tensor_tensor(out=x32[0:64, :], in0=x32[0:64, :], in1=CS[0:64, 0, :], op=ALU.mult),
                 reads=(x32b, CSb), writes=(x32b,))
            P.op("dve", lambda e: e.tensor_tensor(out=out_ap, in0=x32[0:64, :], in1=t2[0:64, :], op=ALU.add),
                 reads=(x32b, t2b), writes=(out_b,))

        for qp in range(NQ):
            with ExitStack() as es:
                WIN = es.enter_context(SBT(nc, "m_win", [128, NCH, 704], BF16))
                WINb = Buf("m_win")
                XN = es.enter_context(SBT(nc, "m_xn", [128, NCH, TT], BF16))
                XNb = [Buf(f"m_xn{c}") for c in range(NCH)]
                sq_ring = mk_ring(es, nc, "mc_sq", 2, [128, TT], BF16)
                f32_ring = mk_ring(es, nc, "mc_f32", 9, [128, TT], F32)
                CS = es.enter_context(SBT(nc, "mc_cs", [128, 2, TT], F32))
                CSb = Buf("mc_cs")
                cq_stage = Ring([(es.enter_context(SBT(nc, f"mc_cq{i}", [128, 3, TT], BF16)),
                                  [Buf(f"mc_cq{i}_{m}") for m in range(3)], P.dma_sem(f"mc_cqs{i}")) for i in range(2)])
                C.ps_misc = Ring(C.psum[0:8])
                P.dma("sp", WIN[:, :, :], win_src[:, :, :], wsem, reads=(C.sbuf_["mla_in"],), writes=(WINb,))
                for tl in range(TQ):
                    j = qp * TQ + tl
                    tsl = slice(j * TT, (j + 1) * TT)
                    norm_tile(P, C, j, gcol, [XN[:, c, :] for c in range(NCH)], XNb, sq_ring, f32_ring, sq_eng="pool")
                    P.dma_group("sp", [(CS[0:64, 0, :], C.cs_s[0, 0:64, tsl]), (CS[0:64, 1, :], C.cs_s[1, 0:64, tsl])], cssem,
                                reads=(C.cs_sb[j],), writes=(CSb,))
                    c32 = []
                    for m in (5, 0, 1, 2, 3, 4):
                        width = 128 if m < 5 else 64
                        ps, psb = C.ps_misc.next()

                        def proj(e, ps=ps, m=m, width=width):
                            for c in range(NCH):
                                ins = e.matmul(ps[0:width, :], lhsT=WIN[:, c, m * 128:m * 128 + width], rhs=XN[:, c, :],
                                               start=(c == 0), stop=(c == NCH - 1))
                            return ins
                        P.op("pe", proj, reads=(WINb, *XNb), writes=(psb,))
                        t32, t32b = f32_ring.next()
                        P.op("act", lambda e, t32=t32, ps=ps, width=width: e.activation(out=t32[0:width, :], in_=ps[0:width, :],
                                                                                       func=AF.Copy),
                             reads=(psb,), writes=(t32b,))
                        if m < 5:
                            c32.append((t32, t32b))
                        else:
                            rope64(t32, t32b, CS, CSb, f32_ring, KR[0:64, tsl], KRb[j])
                    rq, rqb = norm_stats(P, C, [c32[m][0][:, :] for m in range(3)], [c32[m][1] for m in range(3)], 384,
                                         sq_ring, f32_ring, sq_eng="pool")
                    cqt, cqtb, cqsem = cq_stage.next()
                    for m in range(3):
                        P.op("dve", lambda e, m=m, rq=rq, c32=c32, cqt=cqt: e.scalar_tensor_tensor(
                            out=cqt[:, m, :], in0=c32[m][0][:, :], scalar=C.gains_sb[:, 57 + m:58 + m],
                            in1=rq[:, :], op0=ALU.mult, op1=ALU.mult),
                            reads=(c32[m][1], rqb, C.gains_b), writes=(cqtb[m],))
                    P.dma("sp", C.cq_s[:, :, tsl], cqt[:, :, :], cqsem, reads=tuple(cqtb), writes=(C.cq_sb[j],))
                    rk, rkb = norm_stats(P, C, [c32[3 + m][0][:, :] for m in range(2)], [c32[3 + m][1] for m in range(2)], 256,
                                         sq_ring, f32_ring, sq_eng="pool")
                    for m in range(2):
                        P.op("dve", lambda e, m=m, j=j, rk=rk, c32=c32: e.scalar_tensor_tensor(
                            out=CKVN[:, m, j * TT:(j + 1) * TT], in0=c32[3 + m][0][:, :], scalar=C.gains_sb[:, 60 + m:61 + m],
                            in1=rk[:, :], op0=ALU.mult, op1=ALU.mult),
                            reads=(c32[3 + m][1], rkb, C.gains_b), writes=(CKVNb[m][j],))
                P.barrier()
                P.flush()
            with ExitStack() as es:
                WQ = es.enter_context(SBT(nc, "m_wq", [128, 3, 192], BF16))
                WKV = es.enter_context(SBT(nc, "m_wkv", [128, 2, 256], BF16))
                Wb = Buf("m_w")
                WOs = [(es.enter_context(SBT(nc, f"m_wo{i}", [128, D], BF16)), Buf(f"m_wo{i}"), P.dma_sem(f"m_wos{i}"))
                       for i in range(2)]
                KN = es.enter_context(SBT(nc, "m_kn", [128, S], BF16))
                KNb = [Buf(f"m_kn{t}") for t in range(NT)]
                V = es.enter_context(SBT(nc, "m_v", [128, S // 128, 129], BF16))
                Vb = [Buf(f"m_v{t}") for t in range(NT)]
                C.vones_b = Buf("m_vones")
                P.op("pool", lambda e: e.memset(V[:, :, 128:129], 1.0), writes=(C.vones_b,))
                conv_pieces = hook() if hook is not None else []
                cqt_ring = Ring([(es.enter_context(SBT(nc, f"md_cq{i}", [128, 3, TT], BF16)), Buf(f"md_cq{i}"),
                                  P.dma_sem(f"md_cqs{i}")) for i in range(2)])
                qn_ring = mk_ring(es, nc, "m_qn", 2, [128, TT], BF16)
                qr_ring = mk_ring(es, nc, "m_qr", 2, [128, TT], BF16)
                pt_ring = mk_ring(es, nc, "m_pt", 4, [128, TT], BF16)
                rope_ring = mk_ring(es, nc, "md_rope", 2, [128, TT], F32)
                onq_ring = Ring([(es.enter_context(SBT(nc, f"md_onq{i}", [128, TT], BF16)),
                                  [Buf(f"md_onq{i}_{sb}") for sb in range(4)]) for i in range(2)])
                small_ring = Ring([(es.enter_context(SBT(nc, f"md_small{i}", [128, 8], F32)),
                                    [Buf(f"md_r{i}_{hb}") for hb in range(2)]) for i in range(3)])
                CS = es.enter_context(SBT(nc, "md_cs", [128, 2, TT], F32))
                CSb = Buf("md_cs")
                on_ring = mk_ring(es, nc, "m_on", 2, [128, TT], BF16)
                C.ps_misc = Ring(C.psum[0:4])
                Oring = Ring([[C.psum[4], C.psum[5]], [C.psum[6], C.psum[7]]])
                nkt = (qp + 1) * TQ
                heads = list(range(C.head0, C.head0 + C.nheads))
                seq = [(h, tl) for h in heads for tl in range(TQ)]
                Q_of, on_of, fin_of = {}, {}, {}

                def load_w(h):
                    pairs = [(WQ[:, :, :], wq_src[:, :, h * 192:(h + 1) * 192]),
                             (WKV[:, :, :], wkv_src[:, :, h * 256:(h + 1) * 256])]
                    P.dma_group("sp", pairs, wsem, reads=(C.sbuf_["mla_q"], C.sbuf_["mla_kv"]), writes=(Wb,))
                    wo, wob, wosem = WOs[h % 2]
                    P.dma("sp", wo[:, :], C.s["mla_out"][h * 128:(h + 1) * 128, :], wosem, reads=(C.sbuf_["mla_out"],),
                          writes=(wob,))

                def KV(h, jjs=None):
                    for jj in (range(nkt) if jjs is None else jjs):
                        ps, psb = C.ps_misc.next()

                        def kproj(e, ps=ps, jj=jj):
                            for m in range(2):
                                ins = e.matmul(ps[:, :], lhsT=WKV[:, m, 0:128], rhs=CKVN[:, m, jj * TT:(jj + 1) * TT],
                                               start=(m == 0), stop=(m == 1))
                            return ins
                        P.op("pe", kproj, reads=(Wb, CKVNb[0][jj], CKVNb[1][jj]), writes=(psb,))
                        P.op("act", lambda e, ps=ps, jj=jj: e.activation(out=KN[:, jj * TT:(jj + 1) * TT], in_=ps[:, :],
                                                                         func=AF.Copy),
                             reads=(psb,), writes=(KNb[jj],))
                        ps, psb = C.ps_misc.next()

                        def vproj(e, ps=ps, jj=jj):
                            for sblk in range(4):
                                t0 = jj * TT + sblk * 128
                                for m in range(2):
                                    ins = e.matmul(ps[:, sblk * 128:(sblk + 1) * 128], lhsT=CKVN[:, m, t0:t0 + 128],
                                                   rhs=WKV[:, m, 128:256], start=(m == 0), stop=(m == 1))
                            return ins
                        P.op("pe", vproj, reads=(Wb, CKVNb[0][jj], CKVNb[1][jj]), writes=(psb,))
                        P.op("dve", lambda e, ps=ps, jj=jj: e.tensor_copy(out=V[:, jj * 4:(jj + 1) * 4, 0:128],
                                                                          in_=ps[:, :].rearrange("p (s e) -> p s e", s=4)),
                             reads=(psb,), writes=(Vb[jj],))

                def QPROJ(h, tl):
                    j = qp * TQ + tl
                    tsl = slice(j * TT, (j + 1) * TT)
                    st = {}
                    CQT, CQTb, cqsem = cqt_ring.next()
                    cq_rd = (CQTb,)

                    def qproj(e, ps, c0, width):
                        for m in range(3):
                            ins = e.matmul(ps[0:width, :], lhsT=WQ[:, m, c0:c0 + width], rhs=CQT[:, m, :],
                                           start=(m == 0), stop=(m == 2))
                        return ins

                    def p_load():
                        P.dma("sp", CQT[:, :, :], C.cq_s[:, :, tsl], cqsem, reads=(C.cq_sb[j],), writes=(CQTb,))
                        P.dma_group("sp", [(CS[0:64, 0, :], C.cs_s[0, 0:64, tsl]), (CS[0:64, 1, :], C.cs_s[1, 0:64, tsl])],
                                    cssem, reads=(C.cs_sb[j],), writes=(CSb,))

                    def p_n():
                        psn, psnb = C.ps_misc.next()
                        P.op("pe", lambda e: qproj(e, psn, 0, 128), reads=(Wb, *cq_rd), writes=(psnb,))
                        QN, QNb = qn_ring.next()
                        P.op("dve", lambda e: e.tensor_copy(out=QN[:, :], in_=psn[:, :]), reads=(psnb,), writes=(QNb,))
                        st["qn"] = (QN, QNb)

                    def p_r():
                        psr, psrb = C.ps_misc.next()
                        P.op("pe", lambda e: qproj(e, psr, 128, 64), reads=(Wb, *cq_rd), writes=(psrb,))
                        q32, q32b = rope_ring.next()
                        P.op("dve", lambda e: e.tensor_copy(out=q32[0:64, :], in_=psr[0:64, :]), reads=(psrb,),
                             writes=(q32b,))
                        st["q32"] = (q32, q32b)

                    def p_rope():
                        q32, q32b = st["q32"]
                        QR, QRb = qr_ring.next()
                        rope64(q32, q32b, CS, CSb, rope_ring, QR[0:64, :], QRb)
                        Q_of[(h, tl)] = (*st["qn"], QR, QRb)

                    return [p_load, p_n, p_r, p_rope]

                def EVAC(h, tl, O1):
                    sm8, rbs = small_ring.next()
                    for hb in range(2):
                        bank, bb = O1[hb]
                        P.op("dve", lambda e, bank=bank, hb=hb: e.reciprocal(out=sm8[:, hb * 2:hb * 2 + 2],
                                                                             in_=bank[:, 128:258:129]),
                             reads=(bb,), writes=(rbs[hb],))
                    onq, onqb = onq_ring.next()
                    for sb in range(4):
                        col = (sb % 2) * 129
                        b0, b0b = O1[sb // 2]
                        osl = slice(sb * 128, (sb + 1) * 128)
                        P.op("dve", lambda e, b0=b0, col=col, osl=osl, sb=sb: e.tensor_scalar(
                            out=onq[:, osl], in0=b0[:, col:col + 128], scalar1=sm8[:, sb:sb + 1], scalar2=None,
                            op0=ALU.mult),
                            reads=(b0b, rbs[sb // 2]), writes=(onqb[sb],))
                    fin_of[(h, tl)] = (onq, onqb)

                def FIN2(h, tl):
                    onq, onqb = fin_of.pop((h, tl))
                    ps, psb = C.ps_misc.next()

                    def tr(e):
                        for sb in range(4):
                            osl = slice(sb * 128, (sb + 1) * 128)
                            ins = e.matmul(ps[:, osl], lhsT=onq[:, osl], rhs=C.ident_bf[:, :], start=True, stop=True)
                        return ins
                    P.op("pe", tr, reads=(*onqb, C.ident_b), writes=(psb,))
                    on, onb = on_ring.next()
                    P.op("act", lambda e: e.activation(out=on[:, :], in_=ps[:, :], func=AF.Copy), reads=(psb,), writes=(onb,))
                    on_of[(h, tl)] = (on, onb)

                def OUT(h, tl):
                    def piece(dc):
                        def f():
                            on, onb = on_of[(h, tl)]
                            wo, wob, _ = WOs[h % 2]
                            out_proj_add(P, C, qp * TQ + tl, on, onb, wo, wob, dcs=(dc,))
                        return f
                    return [piece(dc) for dc in range(NCH)]

                def spread(inject, pieces, start, nk):
                    for k, f in enumerate(pieces):
                        inject.setdefault(min(start + k, nk - 1), []).append(f)

                load_w(seq[0][0])
                KV(seq[0][0])
                for f in QPROJ(*seq[0]):
                    f()
                for idx, (h, tl) in enumerate(seq):
                    j = qp * TQ + tl
                    nk = 4 * j + 4
                    inject = {}
                    post = []
                    if idx > 0:
                        ph, ptl = seq[idx - 1]
                        inject.setdefault(1, []).append(lambda ph=ph, ptl=ptl: FIN2(ph, ptl))
                        spread(inject, OUT(ph, ptl), 2, nk)
                    if idx + 1 < len(seq):
                        nh, ntl = seq[idx + 1]
                        if nh == h:
                            spread(inject, QPROJ(nh, ntl), 1, nk)
                        else:
                            inject.setdefault(1, []).append(lambda nh=nh: load_w(nh))
                            for jj in range(nkt):
                                at = 4 * jj + 6
                                if at <= nk - 1:
                                    inject.setdefault(at, []).append(lambda nh=nh, jj=jj: KV(nh, (jj,)))
                                else:
                                    post.append(lambda nh=nh, jj=jj: KV(nh, (jj,)))
                            spread(inject, QPROJ(nh, ntl), max(2, nk - 5), nk)
                    QN, QNb, QR, QRb = Q_of.pop((h, tl))

                    def score_fn(c, i, q0, ps, QN=QN, QNb=QNb, QR=QR, QRb=QRb):
                        def fn(e):
                            e.matmul(ps[:, q0:TT], lhsT=KN[:, i * 128:(i + 1) * 128], rhs=QN[:, q0:TT],
                                     start=True, stop=False)
                            return e.matmul(ps[:, q0:TT], lhsT=KR[0:64, i * 128:(i + 1) * 128], rhs=QR[0:64, q0:TT],
                                            start=False, stop=True)
                        return fn, (KNb[i // 4], KRb[i // 4], QNb, QRb)
                    O1 = Oring.next()
                    attention_tile(P, C, j, 1, score_fn, V, lambda i: Vb[i // 4], pt_ring, [O1], None, scale, inject=inject)
                    EVAC(h, tl, O1)
                    for f in post:
                        f()
                    if conv_pieces and idx % 3 == 0:
                        conv_pieces.pop(0)()
                FIN2(*seq[-1])
                for f in OUT(*seq[-1]):
                    f()
                for f in conv_pieces:
                    f()
                P.barrier()
                P.flush()


def host_constants():
    ones = np.ones((128, 128), np.float32)
    rot = np.zeros((128, 128), np.float32)
    for m in range(128):
        if (m % 64) < 32:
            rot[m + 32, m] = -1.0
        else:
            rot[m - 32, m] = 1.0
    mask = (np.arange(128)[:, None] <= np.arange(128)[None, :]).astype(np.float32)
    inv_freq = (10000.0 ** (-np.arange(0, 64, 2, dtype=np.float32) / np.float32(64))).astype(np.float32)
    af = (inv_freq.astype(np.float64) / (2.0 * math.pi)).astype(np.float32)
    af = np.tile(af, 4).reshape(128, 1)
    return ones, rot, mask, af


def pack_gains(inp):
    g = np.zeros((128, 64), np.float32)

    def put(col, vec):
        v = np.asarray(vec, np.float32).reshape(-1, 128)
        for i in range(v.shape[0]):
            g[:, col + i] = v[i]
    for l in range(2):
        put((l * 3 + 0) * 8, inp["ffn1_norm"][l])
        put((l * 3 + 1) * 8, inp["mix_norm"][l])
        put((l * 3 + 2) * 8, inp["ffn2_norm"][l])
    put(48, inp["final_norm"])
    put(56, inp["diff_sub_norm"][0])
    put(57, inp["mla_q_norm"][0])
    put(60, inp["mla_kv_norm"][0])
    return g


def make_in_maps(inp, xT_list):
    ones, rot, mask, af = host_constants()
    gains = pack_gains(inp)
    lamv = np.concatenate([np.asarray(inp[k], np.float32).reshape(-1) for k in
                           ("diff_lambda_q1", "diff_lambda_k1", "diff_lambda_q2", "diff_lambda_k2")])
    lamv = np.ascontiguousarray(np.broadcast_to(lamv[None, :], (128, 256)))
    pos = np.asarray(inp["positions"], np.int32)
    maps = []
    for b, xT in enumerate(xT_list):
        m = {
            "xT": xT,
            "pos": np.ascontiguousarray(np.broadcast_to(pos[b][None, :], (128, S))),
            "gains": gains, "lamv": lamv, "ones_c": ones, "rot_c": rot, "mask_c": mask, "af_c": af,
            "ident_c": np.eye(128, dtype=np.float32),
        }
        for k in ("ffn1_w_gate", "ffn1_w_up", "ffn1_w_down", "ffn2_w_gate", "ffn2_w_up", "ffn2_w_down",
                  "diff_w_in", "diff_w_out", "mla_w_in", "mla_w_q_up", "mla_w_kv_up", "mla_w_out"):
            m[k] = np.asarray(inp[k], np.float32)
        maps.append(m)
    return maps


_NC_CACHE = {}


def run_phases(inp, xT_list, phases, core_ids=None, **bkw):
    key = (tuple(phases), tuple(sorted(bkw.items())))
    if key not in _NC_CACHE:
        _NC_CACHE[key] = build(phases, **bkw)
    nc = _NC_CACHE[key]
    maps = make_in_maps(inp, xT_list)
    if core_ids is None:
        core_ids = list(range(len(xT_list)))
    res = run_bass_kernel_spmd(nc, maps, core_ids=core_ids)
    if bkw.get("debug"):
        return [r["outT"] for r in res.results], [r["dbg"] for r in res.results]
    return [r["outT"] for r in res.results]


LAUNCH_PLAN = [ALL_PHASES]


def kernel(**inputs):
    x = np.asarray(inputs["x"], np.float32)
    B = x.shape[0]
    xT = [np.ascontiguousarray(x[b].T) for b in range(B)]
    cur = xT
    for phases in LAUNCH_PLAN:
        cur = run_phases(inputs, cur, list(phases))
    out = np.stack([np.ascontiguousarray(o.T) for o in cur], axis=0)
    return out.astype(np.float32)
```

```python
import math
from contextlib import ExitStack

import numpy as np
import concourse.bass as bass
import concourse.mybir as mybir
from concourse.bass_utils import run_bass_kernel_spmd

F32 = mybir.dt.float32
BF16 = mybir.dt.bfloat16
I32 = mybir.dt.int32
AF = mybir.ActivationFunctionType
ALU = mybir.AluOpType
AX = mybir.AxisListType

D = 1024
S = 4096
DFF = 2816
NCH = 8
TT = 512
NT = S // TT
EPS = 1e-6
TWO_PI = 2.0 * math.pi

SAME_ENGINE_SYNC = True


class Buf:
    __slots__ = ("name", "w", "r")

    def __init__(self, name=""):
        self.name = name
        self.w = None
        self.r = {}


class DmaSem:
    def __init__(self, handle, sid):
        self.h = handle
        self.sid = sid
        self.count = 0


class Prog:
    ENGS = ("pe", "act", "dve", "pool", "sp")

    def __init__(self, nc, es):
        self.nc = nc
        self.sem = {}
        self.cnt = {}
        self.seen = {}
        self.lists = {}
        self._sid = 0
        for e in self.ENGS:
            h = es.enter_context(nc.semaphore("prog_" + e))
            self.sem[e] = (h, self._next_sid())
            self.cnt[e] = 0
            self.seen[e] = {}
            self.lists[e] = []
        self.es = es
        self.n_ops = 0
        self.pending_dma = {}

    def _next_sid(self):
        self._sid += 1
        return self._sid

    def dma_sem(self, name):
        _UNIQ[0] += 1
        h = self.es.enter_context(self.nc.semaphore(f"{name}_u{_UNIQ[0]}"))
        return DmaSem(h, self._next_sid())

    def _waits(self, eng, reads, writes):
        evs = []
        for b in reads:
            if b.w is not None:
                evs.append(b.w)
        for b in writes:
            if b.w is not None:
                evs.append(b.w)
            evs.extend(b.r.values())
        need = {}
        seen = self.seen[eng]
        for (sid, h, val, src) in evs:
            if src == eng and (eng in ("pe", "sp") or not SAME_ENGINE_SYNC):
                continue
            if seen.get(sid, 0) >= val:
                continue
            if sid not in need or need[sid][1] < val:
                need[sid] = (h, val)
        for sid, (h, val) in need.items():
            self.lists[eng].append(("w", h, val))
            seen[sid] = val

    def _commit(self, ev, reads, writes):
        for b in writes:
            b.w = ev
            b.r = {}
        for b in reads:
            b.r[ev[0]] = ev

    def op(self, eng, fn, reads=(), writes=()):
        self._waits(eng, reads, writes)
        h, sid = self.sem[eng]
        self.cnt[eng] += 1
        ev = (sid, h, self.cnt[eng], eng)
        self.lists[eng].append(("o", fn, h, 1))
        self._commit(ev, reads, writes)
        self.n_ops += 1
        return ev

    def dma(self, eng, out, in_, dsem, reads=(), writes=(), sbuf=True):
        return self.dma_group(eng, [(out, in_)], dsem, reads, writes, sbuf=sbuf)

    def dma_group(self, eng, pairs, dsem, reads=(), writes=(), sbuf=True):
        self._waits(eng, reads, writes)
        for (out, in_) in pairs:
            dsem.count += 16
            self.lists[eng].append(("o", lambda e, o=out, i=in_: e.dma_start(out=o, in_=i), dsem.h, 16))
        ev = (dsem.sid, dsem.h, dsem.count, "dma")
        self._commit(ev, reads, writes)
        if sbuf:
            self.pending_dma[dsem.sid] = ev
        return ev

    def wait_event(self, eng, ev):
        sid, h, val, _ = ev
        if self.seen[eng].get(sid, 0) < val:
            self.lists[eng].append(("w", h, val))
            self.seen[eng][sid] = val

    def barrier(self):
        for e in self.ENGS:
            for ev in self.pending_dma.values():
                self.wait_event(e, ev)
        self.pending_dma = {}
        for e in self.ENGS:
            for o in self.ENGS:
                if o == e or self.cnt[o] == 0:
                    continue
                h, sid = self.sem[o]
                if self.seen[e].get(sid, 0) < self.cnt[o]:
                    self.lists[e].append(("w", h, self.cnt[o]))
                    self.seen[e][sid] = self.cnt[o]

    def flush(self):
        lists = self.lists
        self.lists = {e: [] for e in self.ENGS}

        def replay(handle, items):
            for it in items:
                if it[0] == "w":
                    handle.wait_ge(it[1], it[2])
                else:
                    ins = it[1](handle)
                    ins.then_inc(it[2], it[3])

        with self.nc.Block(no_gpsimd_drain=True) as block:
            if lists["pe"]:
                @block.tensor
                def _(eng):
                    replay(eng, lists["pe"])
            if lists["act"]:
                @block.scalar
                def _(eng):
                    replay(eng, lists["act"])
            if lists["dve"]:
                @block.vector
                def _(eng):
                    replay(eng, lists["dve"])
            if lists["pool"]:
                @block.gpsimd
                def _(eng):
                    replay(eng, lists["pool"])
            if lists["sp"]:
                @block.sync
                def _(eng):
                    replay(eng, lists["sp"])


_UNIQ = [0]


def SBT(nc, name, shape, dt):
    _UNIQ[0] += 1
    return nc.sbuf_tensor(f"{name}_u{_UNIQ[0]}", shape, dt)


class Ring:
    def __init__(self, items):
        self.items = items
        self.i = 0

    def next(self):
        it = self.items[self.i % len(self.items)]
        self.i += 1
        return it


class Ctx:
    pass


def mk_ring(es, nc, name, n, shape, dt, P=None):
    items = []
    for i in range(n):
        t = es.enter_context(SBT(nc, f"{name}{i}", shape, dt))
        if P is None:
            items.append((t, Buf(f"{name}{i}")))
        else:
            items.append((t, Buf(f"{name}{i}"), P.dma_sem(f"{name}_s{i}")))
    return Ring(items)


def declare_inputs(nc, C):
    def inp(name, shape, dt=F32):
        return nc.dram_tensor(name, list(shape), dt, kind="ExternalInput")

    C.xT = inp("xT", [D, S])
    C.pos = inp("pos", [128, S], I32)
    C.gains = inp("gains", [128, 64])
    C.lamv = inp("lamv", [128, 256])
    C.ones_in = inp("ones_c", [128, 128])
    C.rot_in = inp("rot_c", [128, 128])
    C.mask_in = inp("mask_c", [128, 128])
    C.ident_in = inp("ident_c", [128, 128])
    C.af_in = inp("af_c", [128, 1])
    C.w = {}
    for f in ("ffn1", "ffn2"):
        C.w[f + "_g"] = inp(f + "_w_gate", [2, D, DFF])
        C.w[f + "_u"] = inp(f + "_w_up", [2, D, DFF])
        C.w[f + "_d"] = inp(f + "_w_down", [2, DFF, D])
    C.w["diff_in"] = inp("diff_w_in", [1, D, 3 * D])
    C.w["diff_out"] = inp("diff_w_out", [1, D, D])
    C.w["mla_in"] = inp("mla_w_in", [1, D, 704])
    C.w["mla_q"] = inp("mla_w_q_up", [1, 384, 1536])
    C.w["mla_kv"] = inp("mla_w_kv_up", [1, 256, 2048])
    C.w["mla_out"] = inp("mla_w_out", [1, D, D])
    C.outT = nc.dram_tensor("outT", [D, S], F32, kind="ExternalOutput")
    C.dbg = nc.dram_tensor("dbg", [16, 128, TT], F32, kind="ExternalOutput") if getattr(C, "debug", False) else None
    C.s = {}
    C.sbuf_ = {}
    C.ssem = {}
    C.grp_bufs = {}

    def scr(key, shape):
        C.s[key] = nc.dram_tensor("s_" + key, list(shape), BF16)
        C.sbuf_[key] = Buf("s_" + key)

    for l in range(2):
        for f in ("ffn1", "ffn2"):
            scr(f"{f}_g{l}", [D, DFF])
            scr(f"{f}_u{l}", [D, DFF])
            scr(f"{f}_d{l}", [DFF, D])
    scr("diff_in", [D, 3 * D])
    scr("diff_out", [D, D])
    scr("mla_in", [D, 704])
    scr("mla_q", [384, 1536])
    scr("mla_kv", [256, 2048])
    scr("mla_out", [D, D])
    C.cs_s = nc.dram_tensor("s_cs", [2, 128, S], F32)
    C.cs_sb = [Buf(f"s_cs{t}") for t in range(NT)]
    C.cq_s = nc.dram_tensor("s_cq", [128, 3, S], BF16)
    C.cq_sb = [Buf(f"s_cq{t}") for t in range(NT)]
    C.xn_s = nc.dram_tensor("s_xn", [128, NCH, S], BF16)
    C.xn_sb = [Buf(f"s_xn{t}") for t in range(NT)]


FFN_CHUNKS = [(0, 2), (2, 5), (5, 8), (8, 11)]


def convert_weights(P, C, keys, chunked=False, which=None, batch=2):
    if chunked:
        f, l = keys[0][:4], int(keys[0][6])
        kg, ku, kd = f"{f}_g{l}", f"{f}_u{l}", f"{f}_d{l}"
        bufs = C.grp_bufs.setdefault((f, l), [None] * 11)
        for ci, (g0, g1) in enumerate(FFN_CHUNKS):
            if which is not None and ci not in which:
                continue
            ds = P.dma_sem(f"cvc_{f}{l}_{g0}")
            b = Buf(f"cvc_{f}{l}_{g0}")
            pairs = []
            c0, c1 = g0 * 256, g1 * 256
            for kk, kind in ((kg, "g"), (ku, "u")):
                src = C.w[f"{f}_{kind}"][l]
                for r0 in range(0, D, 512):
                    pairs.append((C.s[kk][r0:r0 + 512, c0:c1], src[r0:r0 + 512, c0:c1]))
            srcd = C.w[f"{f}_d"][l]
            for r0 in range(c0, c1, 128):
                pairs.append((C.s[kd][r0:r0 + 128, :], srcd[r0:r0 + 128, :]))
            P.dma_group("pool", pairs, ds, reads=(), writes=(b,), sbuf=False)
            for g in range(g0, g1):
                bufs[g] = b
        for k in keys:
            C.ssem[k] = None
        return
    for key in keys:
        if key in C.ssem:
            continue
        ds = P.dma_sem("cv_" + key)
        C.ssem[key] = ds
        dst = C.s[key]
        if key[:3] == "ffn":
            f, kind, l = key[:4], key[5], int(key[6])
            src = C.w[f"{f}_{kind}"][l]
        else:
            src = C.w[key][0]
        rows = dst.shape[0]
        step = 128
        pairs = []
        for r0 in range(0, rows, step):
            r1 = min(rows, r0 + step)
            pairs.append((dst[r0:r1, :], src[r0:r1, :]))
        for b0 in range(0, len(pairs), batch):
            P.dma_group("pool", pairs[b0:b0 + batch], ds, reads=(), writes=(C.sbuf_[key],), sbuf=False)


def setup_persistent(P, C, es):
    nc = C.nc
    C.X = es.enter_context(SBT(nc, "X", [128, NCH, S], F32))
    C.Xb = [[Buf(f"X{c}_{t}") for t in range(NT)] for c in range(NCH)]
    C.gains_sb = es.enter_context(SBT(nc, "gains_sb", [128, 64], F32))
    C.gains_b = Buf("gains")
    C.ones_bf = es.enter_context(SBT(nc, "ones_bf", [128, 128], BF16))
    C.ones_b = Buf("ones")
    C.rot32 = es.enter_context(SBT(nc, "rot32", [128, 128], F32))
    C.rot_b = Buf("rot")
    C.mask_bf = es.enter_context(SBT(nc, "mask_bf", [128, 128], BF16))
    C.mask_b = Buf("mask")
    C.ident_bf = es.enter_context(SBT(nc, "ident_bf", [128, 128], BF16))
    C.ident_b = Buf("ident")
    C.negm_bf = es.enter_context(SBT(nc, "negm_bf", [128, 128], BF16))
    C.negm_b = Buf("negm")
    C.af = es.enter_context(SBT(nc, "af", [128, 1], F32))
    C.af_b = Buf("af")
    C.cst = es.enter_context(SBT(nc, "cst", [128, 4], F32))
    C.cst_b = Buf("cst")
    C.psum = []
    for i in range(8):
        t = es.enter_context(nc.psum_tensor(f"ps{i}", [128, TT], F32))
        C.psum.append((t, Buf(f"ps{i}")))
    C.ld_sem = P.dma_sem("ld_const")
    C.x_sems = [P.dma_sem(f"ld_x{i}") for i in range(8)]
    C.out_sem = P.dma_sem("st_out")


def load_constants(P, C, es):
    nc = C.nc
    tmp = es.enter_context(SBT(nc, "ctmp", [128, 384], F32))
    tb = Buf("ctmp")
    P.dma_group("sp", [(C.gains_sb[:, :], C.gains[:, :]), (C.rot32[:, :], C.rot_in[:, :]), (C.af[:, :], C.af_in[:, :]),
                       (tmp[:, 0:128], C.ones_in[:, :]), (tmp[:, 128:256], C.mask_in[:, :]),
                       (tmp[:, 256:384], C.ident_in[:, :])], C.ld_sem,
                writes=(C.gains_b, C.rot_b, C.af_b, tb))
    P.op("dve", lambda e: e.tensor_copy(out=C.ident_bf[:, :], in_=tmp[:, 256:384]), reads=(tb,), writes=(C.ident_b,))
    P.op("dve", lambda e: e.tensor_scalar(out=C.negm_bf[:, :], in0=tmp[:, 128:256], scalar1=-1.0, scalar2=30000.0,
                                          op0=ALU.add, op1=ALU.mult), reads=(tb,), writes=(C.negm_b,))
    P.op("dve", lambda e: e.tensor_copy(out=C.ones_bf[:, :], in_=tmp[:, 0:128]), reads=(tb,), writes=(C.ones_b,))
    P.op("dve", lambda e: e.tensor_copy(out=C.mask_bf[:, :], in_=tmp[:, 128:256]), reads=(tb,), writes=(C.mask_b,))

    def cfn(e):
        e.memset(C.cst[:, 0:1], -math.pi)
        return e.memset(C.cst[:, 1:2], EPS)
    P.op("dve", cfn, writes=(C.cst_b,))


def load_x(P, C):
    src = C.xT.rearrange("(c p) t -> p c t", p=128)
    for hf in range(2):
        t0 = hf * (S // 2)
        for cg in range(4):
            pairs = [(C.X[:, c, t0:t0 + S // 2], src[:, c, t0:t0 + S // 2]) for c in (2 * cg, 2 * cg + 1)]
            wr = tuple(C.Xb[c][t] for c in (2 * cg, 2 * cg + 1) for t in range(hf * 4, hf * 4 + 4))
            P.dma_group("sp", pairs, C.x_sems[hf * 4 + cg], writes=wr)


def norm_stats(P, C, srcs, src_bufs, nfeat, sq_ring, f32_ring, parts=128, sq_eng="act"):
    ps, psb = C.ps_misc.next()
    n = len(srcs)
    for i, (a, b) in enumerate(zip(srcs, src_bufs)):
        sq, sqb = sq_ring.next()
        if sq_eng == "act":
            P.op("act", lambda e, a=a, sq=sq: e.activation(out=sq[0:parts, :], in_=a, func=AF.Square),
                 reads=(b,), writes=(sqb,))
        else:
            P.op(sq_eng, lambda e, a=a, sq=sq: e.tensor_tensor(out=sq[0:parts, :], in0=a, in1=a, op=ALU.mult),
                 reads=(b,), writes=(sqb,))
        P.op("pe", lambda e, sq=sq, i=i, ps=ps: e.matmul(ps[:, :], lhsT=C.ones_bf[0:parts, :], rhs=sq[0:parts, :],
                                                          start=(i == 0), stop=(i == n - 1)),
             reads=(sqb, C.ones_b), writes=(psb,))
    r, rb = f32_ring.next()
    P.op("act", lambda e: e.activation(out=r[:, :], in_=ps[:, :], func=AF.Ln, bias=C.cst[:, 1:2], scale=1.0 / nfeat),
         reads=(psb, C.cst_b), writes=(rb,))
    P.op("act", lambda e: e.activation(out=r[:, :], in_=r[:, :], func=AF.Exp, scale=-0.5),
         reads=(rb,), writes=(rb,))
    return r, rb


def norm_tile(P, C, t, gcol, outs, out_bufs, sq_ring, f32_ring, eng="dve", sq_eng="act"):
    sl = slice(t * TT, (t + 1) * TT)
    srcs = [C.X[:, c, sl] for c in range(NCH)]
    r, rb = norm_stats(P, C, srcs, [C.Xb[c][t] for c in range(NCH)], D, sq_ring, f32_ring, sq_eng=sq_eng)
    for c in range(NCH):
        P.op(eng, lambda e, c=c: e.scalar_tensor_tensor(out=outs[c], in0=C.X[:, c, sl],
                                                        scalar=C.gains_sb[:, gcol + c:gcol + c + 1],
                                                        in1=r[:, :], op0=ALU.mult, op1=ALU.mult),
             reads=(C.Xb[c][t], rb, C.gains_b), writes=(out_bufs[c],))


def ffn_phase(P, C, l, f):
    nc = C.nc
    G = 2
    NG = DFF // (128 * G)
    ST = 4
    gcol = (l * 3 + (0 if f == "ffn1" else 2)) * 8
    kg, ku, kd = f"{f}_g{l}", f"{f}_u{l}", f"{f}_d{l}"
    with ExitStack() as es:
        XN = es.enter_context(SBT(nc, "ffn_xn", [128, ST, NCH, TT], BF16))
        XNb = [[Buf(f"xn{t}_{c}") for c in range(NCH)] for t in range(ST)]
        Wg = [es.enter_context(SBT(nc, f"ffn_wg{i}", [128, NCH, G * 128], BF16)) for i in range(2)]
        Wu = [es.enter_context(SBT(nc, f"ffn_wu{i}", [128, NCH, G * 128], BF16)) for i in range(2)]
        Wd = [es.enter_context(SBT(nc, f"ffn_wd{i}", [128, G, D], BF16)) for i in range(2)]
        Wb = [Buf(f"ffn_w{i}") for i in range(2)]
        wsem = [P.dma_sem(f"ffn_ws{l}{f}{i}") for i in range(2)]
        sq_ring = mk_ring(es, nc, "ffn_sq", 2, [128, TT], BF16)
        f32_ring = mk_ring(es, nc, "ffn_f32", 2, [128, TT], F32)
        sg_ring = mk_ring(es, nc, "ffn_sg", 2, [128, TT], BF16)
        h_ring = mk_ring(es, nc, "ffn_h", 4, [128, TT], BF16)
        C.ps_misc = Ring(C.psum[4:8])
        psA = C.psum[0:4]
        psB = Ring(C.psum[4:8])

        sg_src = C.s[kg].rearrange("(c p) f -> p c f", p=128)
        su_src = C.s[ku].rearrange("(c p) f -> p c f", p=128)
        sd_src = C.s[kd].rearrange("(g p) d -> p g d", p=128)

        def load_group(g):
            sl = g % 2
            P.dma_group("sp", [(Wg[sl][:, :, :], sg_src[:, :, g * 256:(g + 1) * 256]),
                               (Wu[sl][:, :, :], su_src[:, :, g * 256:(g + 1) * 256]),
                               (Wd[sl][:, :, :], sd_src[:, g * G:(g + 1) * G, :])], wsem[sl],
                        reads=((C.grp_bufs[(f, l)][g],) if (f, l) in C.grp_bufs else (C.sbuf_[kg], C.sbuf_[ku], C.sbuf_[kd])),
                        writes=(Wb[sl],))

        for st in range(NT // ST):
            load_group(0)
            for tl in range(ST):
                t = st * ST + tl
                norm_tile(P, C, t, gcol, [XN[:, tl, c, :] for c in range(NCH)], XNb[tl], sq_ring, f32_ring)
            iters = [(g, tl) for g in range(NG) for tl in range(ST)]

            def GU(it):
                g, tl = it
                sl = g % 2
                hs = []
                for fi in range(G):
                    pg, pgb = psA[2 * fi]
                    pu, pub = psA[2 * fi + 1]

                    def mm(e, w, ps, fi=fi, tl=tl):
                        for c in range(NCH):
                            ins = e.matmul(ps[:, :], lhsT=w[:, c, fi * 128:(fi + 1) * 128], rhs=XN[:, tl, c, :],
                                           start=(c == 0), stop=(c == NCH - 1))
                        return ins
                    P.op("pe", lambda e, mm=mm, w=Wg[sl], ps=pg: mm(e, w, ps), reads=(Wb[sl], *XNb[tl]), writes=(pgb,))
                    P.op("pe", lambda e, mm=mm, w=Wu[sl], ps=pu: mm(e, w, ps), reads=(Wb[sl], *XNb[tl]), writes=(pub,))
                    sg, sgb = sg_ring.next()
                    P.op("act", lambda e, sg=sg, pg=pg: e.activation(out=sg[:, :], in_=pg[:, :], func=AF.Silu),
                         reads=(pgb,), writes=(sgb,))
                    h, hb = h_ring.next()
                    P.op("dve", lambda e, h=h, sg=sg, pu=pu: e.tensor_tensor(out=h[:, :], in0=sg[:, :], in1=pu[:, :],
                                                                             op=ALU.mult),
                         reads=(sgb, pub), writes=(hb,))
                    hs.append((h, hb))
                return hs

            def DOWN(it, hs):
                g, tl = it
                sl = g % 2
                t = st * ST + tl
                tsl = slice(t * TT, (t + 1) * TT)
                for dc in range(NCH):
                    pb, pbb = psB.next()

                    def mm(e, dc=dc, pb=pb):
                        for fi in range(G):
                            ins = e.matmul(pb[:, :], lhsT=Wd[sl][:, fi, dc * 128:(dc + 1) * 128], rhs=hs[fi][0][:, :],
                                           start=(fi == 0), stop=(fi == G - 1))
                        return ins
                    P.op("pe", mm, reads=(Wb[sl], hs[0][1], hs[1][1]), writes=(pbb,))
                    P.op("dve", lambda e, dc=dc, pb=pb: e.scalar_tensor_tensor(
                        out=C.X[:, dc, tsl], in0=pb[:, :], scalar=0.5, in1=C.X[:, dc, tsl],
                        op0=ALU.mult, op1=ALU.add),
                        reads=(pbb, C.Xb[dc][t]), writes=(C.Xb[dc][t],))

            prev = None
            for k, it in enumerate(iters):
                hs = GU(it)
                if prev is not None:
                    DOWN(*prev)
                if it[1] == 0 and it[0] + 1 < NG:
                    load_group(it[0] + 1)
                prev = (it, hs)
            DOWN(*prev)
        P.barrier()
        P.flush()


def rope_tables(P, C, j, pos_ring, f32_ring):
    pt, ptb, psem = pos_ring.next()
    P.dma("sp", pt[:, :], C.pos[:, j * TT:(j + 1) * TT], psem, writes=(ptb,))
    outs = []
    for off in (0.25, 0.0):
        u, ub = f32_ring.next()
        ii, iib = C.i32_ring.next()
        P.op("dve", lambda e, u=u: e.tensor_copy(out=u[:, :], in_=pt[:, :]), reads=(ptb,), writes=(ub,))
        P.op("dve", lambda e, u=u, off=off: e.tensor_scalar(out=u[:, :], in0=u[:, :], scalar1=C.af[:, 0:1],
                                                            scalar2=off, op0=ALU.mult, op1=ALU.add),
             reads=(ub, C.af_b), writes=(ub,))
        P.op("dve", lambda e, u=u, ii=ii: e.tensor_copy(out=ii[:, :], in_=u[:, :]), reads=(ub,), writes=(iib,))
        P.op("dve", lambda e, u=u, ii=ii: e.tensor_tensor(out=u[:, :], in0=u[:, :], in1=ii[:, :], op=ALU.subtract),
             reads=(ub, iib), writes=(ub,))
        P.op("act", lambda e, u=u: e.activation(out=u[:, :], in_=u[:, :], func=AF.Sin, scale=TWO_PI),
             reads=(ub,), writes=(ub,))
        outs.append((u, ub))
    return outs[0], outs[1]


def tables_phase(P, C, es):
    nc = C.nc
    if True:
        f32_ring = mk_ring(es, nc, "tb_f32", 16, [128, TT], F32)
        C.i32_ring = mk_ring(es, nc, "tb_i32", 4, [128, TT], I32)
        pos_ring = mk_ring(es, nc, "tb_pos", 8, [128, TT], I32, P=P)
        st_sems = [P.dma_sem(f"tb_st{i}") for i in range(16)]
        for j in range(NT):
            cosT, sinT = rope_tables(P, C, j, pos_ring, f32_ring)
            sl = slice(j * TT, (j + 1) * TT)
            P.dma("sp", C.cs_s[0, :, sl], cosT[0][:, :], st_sems[2 * j], reads=(cosT[1],), writes=(C.cs_sb[j],))
            P.dma("sp", C.cs_s[1, :, sl], sinT[0][:, :], st_sems[2 * j + 1], reads=(sinT[1],), writes=(C.cs_sb[j],))


def apply_rope(P, C, ps, psb, parts, cosT, sinT, out_ap, out_buf, f32_ring, scale_ap=None, scale_buf=None):
    (cs, csb), (sn, snb) = cosT, sinT
    q32, q32b = f32_ring.next()
    if scale_ap is None:
        P.op("act", lambda e: e.activation(out=q32[0:parts, :], in_=ps[0:parts, :], func=AF.Copy),
             reads=(psb,), writes=(q32b,))
    else:
        P.op("dve", lambda e: e.tensor_tensor(out=q32[0:parts, :], in0=ps[0:parts, :], in1=scale_ap[0:parts, :],
                                              op=ALU.mult),
             reads=(psb, scale_buf), writes=(q32b,))
    pr, prb = C.ps_misc.next()
    P.op("pe", lambda e: e.matmul(pr[0:parts, :], lhsT=C.rot32[0:parts, 0:parts], rhs=q32[0:parts, :],
                                  start=True, stop=True),
         reads=(q32b, C.rot_b), writes=(prb,))
    t1, t1b = f32_ring.next()
    P.op("dve", lambda e: e.tensor_tensor(out=t1[0:parts, :], in0=q32[0:parts, :], in1=cs[0:parts, :], op=ALU.mult),
         reads=(q32b, csb), writes=(t1b,))
    t2, t2b = f32_ring.next()
    P.op("dve", lambda e: e.tensor_tensor(out=t2[0:parts, :], in0=pr[0:parts, :], in1=sn[0:parts, :], op=ALU.mult),
         reads=(prb, snb), writes=(t2b,))
    P.op("dve", lambda e: e.tensor_tensor(out=out_ap, in0=t1[0:parts, :], in1=t2[0:parts, :], op=ALU.add),
         reads=(t1b, t2b), writes=(out_buf,))


def final_phase(P, C):
    nc = C.nc
    gcol = 48
    dst = C.outT.rearrange("(c p) t -> p c t", p=128)
    with ExitStack() as es:
        sq_ring = mk_ring(es, nc, "fin_sq", 2, [128, TT], BF16)
        f32_ring = mk_ring(es, nc, "fin_f32", 2, [128, TT], F32)
        o_ring = mk_ring(es, nc, "fin_o", 6, [128, TT], F32, P=P)
        C.ps_misc = Ring(C.psum[0:4])
        evs = {}
        for t in range(NT):
            sl = slice(t * TT, (t + 1) * TT)
            srcs = [C.X[:, c, sl] for c in range(NCH)]
            r, rb = norm_stats(P, C, srcs, [C.Xb[c][t] for c in range(NCH)], D, sq_ring, f32_ring)
            for c in range(NCH):
                o, ob, osem = o_ring.next()
                P.op("dve", lambda e, c=c, o=o, sl=sl, r=r: e.scalar_tensor_tensor(
                    out=o[:, :], in0=C.X[:, c, sl], scalar=C.gains_sb[:, gcol + c:gcol + c + 1], in1=r[:, :],
                    op0=ALU.mult, op1=ALU.mult),
                    reads=(C.Xb[c][t], rb, C.gains_b), writes=(ob,))
                evs[osem.sid] = P.dma("sp", dst[:, c, sl], o[:, :], osem, reads=(ob,))
        for ev in evs.values():
            P.wait_event("sp", ev)
        P.barrier()
        P.flush()


def store_x_raw(P, C):
    dst = C.outT.rearrange("(c p) t -> p c t", p=128)
    pairs = []
    rd = []
    for c in range(NCH):
        for hf in range(2):
            t0 = hf * (S // 2)
            pairs.append((dst[:, c, t0:t0 + S // 2], C.X[:, c, t0:t0 + S // 2]))
        rd.extend(C.Xb[c])
    ev = P.dma_group("sp", pairs, C.out_sem, reads=tuple(rd))
    P.wait_event("sp", ev)
    P.barrier()
    P.flush()


WEIGHT_KEYS = {
    "ffn1_0": ["ffn1_g0", "ffn1_u0", "ffn1_d0"],
    "diff": ["diff_in", "diff_out"],
    "ffn2_0": ["ffn2_g0", "ffn2_u0", "ffn2_d0"],
    "ffn1_1": ["ffn1_g1", "ffn1_u1", "ffn1_d1"],
    "mla": ["mla_in", "mla_q", "mla_kv", "mla_out"],
    "ffn2_1": ["ffn2_g1", "ffn2_u1", "ffn2_d1"],
}
ALL_PHASES = ["ffn1_0", "diff", "ffn2_0", "ffn1_1", "mla", "ffn2_1", "final"]


def build(phases, debug=False, nheads=8, ntiles=NT, head0=0):
    nc = bass.Bass("TRN2", target_bir_lowering=False)
    C = Ctx()
    C.nc = nc
    C.debug = debug
    C.nheads = nheads
    C.ntiles = ntiles
    C.head0 = head0
    C.pe_mask = True
    declare_inputs(nc, C)
    with ExitStack() as es:
        P = Prog(nc, es)
        setup_persistent(P, C, es)
        wphases = [ph for ph in phases if ph in WEIGHT_KEYS]
        with ExitStack() as es0:
            first_chunked = bool(wphases) and wphases[0].startswith("ffn")
            if wphases:
                convert_weights(P, C, WEIGHT_KEYS[wphases[0]], chunked=first_chunked,
                                which=(0, 1) if first_chunked else None)
            load_constants(P, C, es0)
            load_x(P, C)
            if "diff" in phases or "mla" in phases:
                tables_phase(P, C, es0)
            P.barrier()
            P.flush()
        tables_done = True
        for ph in phases:
            if ph in WEIGHT_KEYS:
                k = wphases.index(ph)
                if k == 0 and first_chunked:
                    convert_weights(P, C, WEIGHT_KEYS[ph], chunked=True, which=(2, 3))
                hook = None
                if ph in ("diff", "mla"):
                    def hook(k=k, batch=(4 if ph == "mla" else 2)):
                        pieces = []
                        kk = k + 1
                        while kk < len(wphases):
                            for key in WEIGHT_KEYS[wphases[kk]]:
                                pieces.append(lambda key=key: convert_weights(P, C, [key], batch=batch))
                            if wphases[kk] in ("diff", "mla"):
                                break
                            kk += 1
                        return pieces
                elif k + 1 < len(wphases):
                    convert_weights(P, C, WEIGHT_KEYS[wphases[k + 1]])
            if ph in ("diff", "mla") and not tables_done:
                tables_phase(P, C)
                tables_done = True
            if ph.startswith("ffn"):
                ffn_phase(P, C, int(ph[5]), ph[:4])
            elif ph == "diff":
                from_diff(P, C, hook)
            elif ph == "mla":
                from_mla(P, C, hook)
            elif ph == "final":
                final_phase(P, C)
            elif ph == "store":
                store_x_raw(P, C)
    return nc


def dbg_dump(P, C, slot, ap, buf, parts=128, cols=TT):
    if not getattr(C, "debug", False):
        return
    if not hasattr(C, "dbg_sem"):
        C.dbg_sem = P.dma_sem("dbg_sem")
        C.dbg_b = Buf("dbg")
    ev = P.dma("pool", C.dbg[slot, 0:parts, 0:cols], ap, C.dbg_sem, reads=(buf,), writes=(C.dbg_b,))
    P.wait_event("pool", ev)


def attention_tile(P, C, j, nk_comp, score_fn, V, Vb_of, PT_ring, O, L, scale, mask_eng="pool", inject=None):
    nk = 4 * j + 4
    inject = inject or {}

    def S_stage(i):
        q0 = max(0, i - 4 * j) * 128
        res = []
        for c in range(nk_comp):
            ps, psb = C.ps_misc.next()
            fn, rd = score_fn(c, i, q0, ps)
            if i >= 4 * j and C.pe_mask:
                def fn_m(e, fn=fn, ps=ps, q0=q0):
                    fn(e)
                    return e.matmul(ps[:, q0:q0 + 128], lhsT=C.ident_bf[:, :], rhs=C.negm_bf[:, :], start=False, stop=True,
                                    skip_group_check=True)
                P.op("pe", fn_m, reads=(*rd, C.ident_b, C.negm_b), writes=(psb,))
            else:
                P.op("pe", fn, reads=rd, writes=(psb,))
            pt, ptb = PT_ring.next()
            P.op("act", lambda e, pt=pt, ps=ps, q0=q0: e.activation(out=pt[:, q0:TT], in_=ps[:, q0:TT], func=AF.Exp,
                                                                     scale=scale),
                 reads=(psb,), writes=(ptb,))
            if i >= 4 * j and not C.pe_mask:
                P.op(mask_eng, lambda e, pt=pt, q0=q0: e.tensor_tensor(out=pt[:, q0:q0 + 128], in0=pt[:, q0:q0 + 128],
                                                                       in1=C.mask_bf[:, :], op=ALU.mult),
                     reads=(ptb, C.mask_b), writes=(ptb,))
            res.append((pt, ptb))
        return res, q0

    def PV_stage(i, res, q0):
        sb0 = q0 // 128
        for c in range(nk_comp):
            pt, ptb = res[c]

            def fn(e, pt=pt, c=c):
                for sb in range(sb0, 4):
                    bank = O[c][sb // 2][0]
                    col = (sb % 2) * 129
                    ins = e.matmul(bank[:, col:col + 129], lhsT=pt[:, sb * 128:(sb + 1) * 128], rhs=V[:, i, :],
                                   start=(i == 0 and sb % 2 == 0), stop=(i == nk - 1 and sb == 3),
                                   skip_group_check=True)
                return ins
            P.op("pe", fn, reads=(ptb, Vb_of(i), C.vones_b), writes=(O[c][0][1], O[c][1][1]))

    depth = 2 if nk_comp == 1 else 1
    pend = []
    for i in range(nk):
        pend.append((i, S_stage(i)))
        for f in inject.get(i, ()):
            f()
        if len(pend) > depth:
            ii, st = pend.pop(0)
            PV_stage(ii, *st)
    for ii, st in pend:
        PV_stage(ii, *st)


def out_proj_add(P, C, j, on, onb, WO, WOb, dcs=tuple(range(NCH))):
    tsl = slice(j * TT, (j + 1) * TT)
    for dc in dcs:
        ps, psb = C.ps_misc.next()
        P.op("pe", lambda e, dc=dc, ps=ps: e.matmul(ps[:, :], lhsT=WO[:, dc * 128:(dc + 1) * 128], rhs=on[:, :],
                                                    start=True, stop=True),
             reads=(WOb, onb), writes=(psb,))
        P.op("dve", lambda e, dc=dc, ps=ps: e.tensor_tensor(out=C.X[:, dc, tsl], in0=ps[:, :], in1=C.X[:, dc, tsl],
                                                            op=ALU.add),
             reads=(psb, C.Xb[dc][j]), writes=(C.Xb[dc][j],))


def from_diff(P, C, hook=None):
    nc = C.nc
    lambda_init = 0.8 - 0.6 * math.exp(-0.3 * 0)
    gcol = 8
    with ExitStack() as es:
        xr = []
        for i in range(4):
            t = es.enter_context(SBT(nc, f"da_xn{i}", [128, NCH, TT], BF16))
            xr.append((t, [Buf(f"da_xn{i}_{c}") for c in range(NCH)], P.dma_sem(f"da_xs{i}")))
        xr = Ring(xr)
        sq_ring = mk_ring(es, nc, "da_sq", 4, [128, TT], BF16)
        f32_ring = mk_ring(es, nc, "da_f32", 3, [128, TT], F32)
        C.ps_misc = Ring(C.psum[0:4])
        for j in range(NT):
            xn, xnb, xsem = xr.next()
            norm_tile(P, C, j, gcol, [xn[:, c, :] for c in range(NCH)], xnb, sq_ring, f32_ring)
            P.dma("sp", C.xn_s[:, :, j * TT:(j + 1) * TT], xn[:, :, :], xsem, reads=tuple(xnb), writes=(C.xn_sb[j],))
        P.barrier()
        P.flush()
    with ExitStack() as es:
        XN = es.enter_context(SBT(nc, "d_xn", [128, NCH, TT], BF16))
        XNb = Buf("d_xn")
        W = es.enter_context(SBT(nc, "d_w", [128, NCH, 384], BF16))
        Wb = Buf("d_w")
        wsem = P.dma_sem("d_wsem")
        WOs = [(es.enter_context(SBT(nc, f"d_wo{i}", [128, D], BF16)), Buf(f"d_wo{i}"), P.dma_sem(f"d_wos{i}"))
               for i in range(2)]
        KT = es.enter_context(SBT(nc, "d_kt", [128, S], BF16))
        KTb = [Buf(f"d_kt{t}") for t in range(NT)]
        V = es.enter_context(SBT(nc, "d_v", [128, S // 128, 129], BF16))
        Vb = [Buf(f"d_v{t}") for t in range(NT)]
        C.vones_b = Buf("d_vones")
        P.op("pool", lambda e: e.memset(V[:, :, 128:129], 1.0), writes=(C.vones_b,))
        conv_pieces = hook() if hook is not None else []
        qt_ring = mk_ring(es, nc, "d_qt", 2, [128, TT], BF16)
        pt_ring = mk_ring(es, nc, "d_pt", 4, [128, TT], BF16)
        rope_ring = mk_ring(es, nc, "d_rope", 4, [128, TT], F32)
        def fine_ring(name, n, shape, dt, mk):
            return Ring([(es.enter_context(SBT(nc, f"{name}{i}", shape, dt)), mk(i)) for i in range(n)])
        fin_ring = fine_ring("d_fin", 3, [128, TT], F32, lambda i: [Buf(f"d_fin{i}_{sb}") for sb in range(4)])
        onq_ring = fine_ring("d_onq", 2, [128, TT], BF16, lambda i: [Buf(f"d_onq{i}_{sb}") for sb in range(4)])
        junk_ring = fine_ring("d_junk", 2, [128, TT], BF16, lambda i: [Buf(f"d_junk{i}_{sb}") for sb in range(4)])
        small_ring = fine_ring("d_small", 3, [128, 16], F32, lambda i: {
            "rb": [[Buf(f"d_r{i}_{c}{hb}") for hb in range(2)] for c in range(2)],
            "ssb": [Buf(f"d_ss{i}_{sb}") for sb in range(4)], "rsb": Buf(f"d_rs{i}")})
        CS = es.enter_context(SBT(nc, "d_cs", [128, 2, TT], F32))
        CSb = Buf("d_cs")
        cssem = P.dma_sem("d_cssem")
        on_ring = mk_ring(es, nc, "d_on", 2, [128, TT], BF16)
        lam_sb = es.enter_context(SBT(nc, "d_lam", [128, 256], F32))
        lam_b = Buf("d_lam")
        sm = es.enter_context(SBT(nc, "d_sm", [128, 8], F32))
        sm_b = Buf("d_sm")
        lsem = P.dma_sem("d_lsem")
        xnsem = P.dma_sem("d_xnsem")
        C.ps_misc = Ring(C.psum[0:4])
        O = [[C.psum[4], C.psum[5]], [C.psum[6], C.psum[7]]]
        L = None

        P.dma("sp", lam_sb[:, :], C.lamv[:, :], lsem, writes=(lam_b,))
        P.op("dve", lambda e: e.tensor_tensor(out=lam_sb[:, 0:64], in0=lam_sb[:, 0:64], in1=lam_sb[:, 64:128],
                                              op=ALU.mult), reads=(lam_b,), writes=(lam_b,))
        P.op("dve", lambda e: e.tensor_tensor(out=lam_sb[:, 128:192], in0=lam_sb[:, 128:192], in1=lam_sb[:, 192:256],
                                              op=ALU.mult), reads=(lam_b,), writes=(lam_b,))
        P.op("dve", lambda e: e.reduce_sum(out=sm[:, 0:1], in_=lam_sb[:, 0:64], axis=AX.X), reads=(lam_b,),
             writes=(sm_b,))
        P.op("dve", lambda e: e.reduce_sum(out=sm[:, 1:2], in_=lam_sb[:, 128:192], axis=AX.X), reads=(lam_b,),
             writes=(sm_b,))
        P.op("act", lambda e: e.activation(out=sm[:, 2:4], in_=sm[:, 0:2], func=AF.Exp), reads=(sm_b,), writes=(sm_b,))
        P.op("dve", lambda e: e.tensor_tensor(out=sm[:, 4:5], in0=sm[:, 3:4], in1=sm[:, 2:3], op=ALU.subtract),
             reads=(sm_b,), writes=(sm_b,))
        P.op("dve", lambda e: e.tensor_scalar(out=sm[:, 4:5], in0=sm[:, 4:5], scalar1=-lambda_init, scalar2=None,
                                              op0=ALU.add), reads=(sm_b,), writes=(sm_b,))
        P.op("dve", lambda e: e.tensor_scalar(out=sm[:, 5:6], in0=C.gains_sb[:, 56:57], scalar1=1.0 - lambda_init,
                                              scalar2=None, op0=ALU.mult), reads=(sm_b, C.gains_b), writes=(sm_b,))

        src_in = C.s["diff_in"].rearrange("(c p) f -> p c f", p=128)
        heads = list(range(C.head0, C.head0 + C.nheads))
        seq = [(h, j) for h in heads for j in range(C.ntiles)]
        QT_of, fin_of, on_of = {}, {}, {}

        def load_w(h):
            pairs = [(W[:, :, 0:128], src_in[:, :, h * 128:(h + 1) * 128]),
                     (W[:, :, 128:256], src_in[:, :, D + h * 128:D + (h + 1) * 128]),
                     (W[:, :, 256:384], src_in[:, :, 2 * D + h * 128:2 * D + (h + 1) * 128])]
            P.dma_group("sp", pairs, wsem, reads=(C.sbuf_["diff_in"],), writes=(Wb,))
            wo, wob, wosem = WOs[h % 2]
            P.dma("sp", wo[:, :], C.s["diff_out"][h * 128:(h + 1) * 128, :], wosem, reads=(C.sbuf_["diff_out"],),
                  writes=(wob,))

        def PROJ(h, j):
            tsl = slice(j * TT, (j + 1) * TT)
            st = {}

            def proj(e, ps, c0):
                for c in range(NCH):
                    ins = e.matmul(ps[:, :], lhsT=W[:, c, c0:c0 + 128], rhs=XN[:, c, :],
                                   start=(c == 0), stop=(c == NCH - 1))
                return ins

            def p_load():
                P.dma("sp", XN[:, :, :], C.xn_s[:, :, tsl], xnsem, reads=(C.xn_sb[j],), writes=(XNb,))
                P.dma_group("sp", [(CS[:, 0, :], C.cs_s[0, :, tsl]), (CS[:, 1, :], C.cs_s[1, :, tsl])], cssem,
                            reads=(C.cs_sb[j],), writes=(CSb,))

            def p_q():
                st["psq"] = C.ps_misc.next()
                P.op("pe", lambda e: proj(e, st["psq"][0], 0), reads=(Wb, XNb), writes=(st["psq"][1],))
                st["q32"] = rope_ring.next()
                P.op("dve", lambda e: e.tensor_copy(out=st["q32"][0][:, :], in_=st["psq"][0][:, :]),
                     reads=(st["psq"][1],), writes=(st["q32"][1],))

            def p_k():
                st["psk"] = C.ps_misc.next()
                P.op("pe", lambda e: proj(e, st["psk"][0], 128), reads=(Wb, XNb), writes=(st["psk"][1],))
                st["k32"] = rope_ring.next()
                P.op("dve", lambda e: e.tensor_copy(out=st["k32"][0][:, :], in_=st["psk"][0][:, :]),
                     reads=(st["psk"][1],), writes=(st["k32"][1],))

            def p_v():
                psv, psvb = C.ps_misc.next()

                def vproj(e):
                    for sblk in range(4):
                        for c in range(NCH):
                            ins = e.matmul(psv[:, sblk * 128:(sblk + 1) * 128], lhsT=XN[:, c, sblk * 128:(sblk + 1) * 128],
                                           rhs=W[:, c, 256:384], start=(c == 0), stop=(c == NCH - 1))
                    return ins
                P.op("pe", vproj, reads=(Wb, XNb), writes=(psvb,))
                P.op("dve", lambda e: e.tensor_copy(out=V[:, j * 4:(j + 1) * 4, 0:128],
                                                    in_=psv[:, :].rearrange("p (s e) -> p s e", s=4)),
                     reads=(psvb,), writes=(Vb[j],))

            def mk_rope(src, out_fn):
                def f():
                    x32, x32b = st[src]
                    out_ap, out_b = out_fn()
                    pr, prb = C.ps_misc.next()
                    P.op("pe", lambda e: e.matmul(pr[:, :], lhsT=C.rot32[:, :], rhs=x32[:, :], start=True, stop=True),
                         reads=(x32b, C.rot_b), writes=(prb,))
                    t2, t2b = rope_ring.next()
                    P.op("dve", lambda e: e.tensor_tensor(out=t2[:, :], in0=pr[:, :], in1=CS[:, 1, :], op=ALU.mult),
                         reads=(prb, CSb), writes=(t2b,))
                    P.op("dve", lambda e: e.tensor_tensor(out=x32[:, :], in0=x32[:, :], in1=CS[:, 0, :], op=ALU.mult),
                         reads=(x32b, CSb), writes=(x32b,))
                    P.op("dve", lambda e: e.tensor_tensor(out=out_ap, in0=x32[:, :], in1=t2[:, :], op=ALU.add),
                         reads=(x32b, t2b), writes=(out_b,))
                return f

            def q_out():
                QT, QTb = qt_ring.next()
                QT_of[(h, j)] = (QT, QTb)
                return QT[:, :], QTb

            return [p_load, p_q, p_k, p_v, mk_rope("q32", q_out), mk_rope("k32", lambda: (KT[:, tsl], KTb[j]))]

        def EVAC_FIN1(h, j):
            sm8, sb_ = small_ring.next()
            rb, ssb, rsb = sb_["rb"], sb_["ssb"], sb_["rsb"]
            for c in range(2):
                for hb in range(2):
                    bank, bb = O[c][hb]
                    P.op("dve", lambda e, bank=bank, c=c, hb=hb: e.reciprocal(out=sm8[:, c * 4 + hb * 2:c * 4 + hb * 2 + 2],
                                                                              in_=bank[:, 128:258:129]),
                         reads=(bb,), writes=(rb[c][hb],))
            P.op("dve", lambda e: e.tensor_scalar(out=sm8[:, 4:8], in0=sm8[:, 4:8], scalar1=sm[:, 4:5], scalar2=None,
                                                  op0=ALU.mult), reads=(rb[1][0], rb[1][1], sm_b), writes=(rb[1][0], rb[1][1]))
            o4, o4b = fin_ring.next()
            for sb in range(4):
                col = (sb % 2) * 129
                b0, b0b = O[0][sb // 2]
                osl = slice(sb * 128, (sb + 1) * 128)
                P.op("dve", lambda e, b0=b0, col=col, osl=osl, sb=sb: e.tensor_scalar(
                    out=o4[:, osl], in0=b0[:, col:col + 128], scalar1=sm8[:, sb:sb + 1], scalar2=None, op0=ALU.mult),
                    reads=(b0b, rb[0][sb // 2]), writes=(o4b[sb],))
            for sb in range(4):
                col = (sb % 2) * 129
                b1, b1b = O[1][sb // 2]
                osl = slice(sb * 128, (sb + 1) * 128)
                P.op("dve", lambda e, b1=b1, col=col, osl=osl, sb=sb: e.scalar_tensor_tensor(
                    out=o4[:, osl], in0=b1[:, col:col + 128], scalar=sm8[:, 4 + sb:5 + sb], in1=o4[:, osl],
                    op0=ALU.mult, op1=ALU.add),
                    reads=(b1b, rb[1][sb // 2], o4b[sb]), writes=(o4b[sb],))
            P.op("dve", lambda e: e.memset(sm8[:, 8:12], 0.0), reads=(), writes=tuple(ssb))
            junk, junkb = junk_ring.next()
            for sb in range(4):
                osl = slice(sb * 128, (sb + 1) * 128)
                P.op("act", lambda e, osl=osl, sb=sb: e.activation(out=junk[:, osl], in_=o4[:, osl], func=AF.Square,
                                                                   accum_out=sm8[:, 8 + sb:9 + sb]),
                     reads=(o4b[sb], ssb[sb]), writes=(junkb[sb], ssb[sb]))
            P.op("act", lambda e: e.activation(out=sm8[:, 12:16], in_=sm8[:, 8:12], func=AF.Ln, bias=C.cst[:, 1:2],
                                               scale=1.0 / 128), reads=(*ssb, C.cst_b), writes=(rsb,))
            P.op("act", lambda e: e.activation(out=sm8[:, 12:16], in_=sm8[:, 12:16], func=AF.Exp, scale=-0.5),
                 reads=(rsb,), writes=(rsb,))
            onq, onqb = onq_ring.next()
            for sb in range(4):
                osl = slice(sb * 128, (sb + 1) * 128)
                P.op("dve", lambda e, osl=osl, sb=sb: e.tensor_scalar(out=onq[:, osl], in0=o4[:, osl],
                                                                      scalar1=sm8[:, 12 + sb:13 + sb], scalar2=None,
                                                                      op0=ALU.mult),
                     reads=(o4b[sb], rsb), writes=(onqb[sb],))
            fin_of[(h, j)] = (onq, onqb)

        def FIN2(h, j):
            onq, onqb = fin_of.pop((h, j))
            ps, psb = C.ps_misc.next()

            def tr(e):
                for sb in range(4):
                    osl = slice(sb * 128, (sb + 1) * 128)
                    ins = e.matmul(ps[:, osl], lhsT=onq[:, osl], rhs=C.ident_bf[:, :], start=True, stop=True)
                return ins
            P.op("pe", tr, reads=(*onqb, C.ident_b), writes=(psb,))
            on, onb = on_ring.next()
            P.op("act", lambda e: e.activation(out=on[:, :], in_=ps[:, :], func=AF.Copy, scale=sm[:, 5:6]),
                 reads=(psb, sm_b), writes=(onb,))
            on_of[(h, j)] = (on, onb)

        def OUT(h, j):
            def piece(dc):
                def f():
                    on, onb = on_of[(h, j)]
                    wo, wob, _ = WOs[h % 2]
                    out_proj_add(P, C, j, on, onb, wo, wob, dcs=(dc,))
                return f
            return [piece(dc) for dc in range(NCH)]

        def spread(inject, pieces, start, nk):
            for k, f in enumerate(pieces):
                inject.setdefault(min(start + k, nk - 1), []).append(f)

        load_w(seq[0][0])
        for f in PROJ(*seq[0]):
            f()
        for idx, (h, j) in enumerate(seq):
            nk = 4 * j + 4
            inject = {}
            post = []
            if idx > 0:
                ph, pj = seq[idx - 1]
                inject.setdefault(1, []).append(lambda ph=ph, pj=pj: FIN2(ph, pj))
                spread(inject, OUT(ph, pj), 2, nk)
            if idx + 1 < len(seq):
                nh, nj = seq[idx + 1]
                pieces = PROJ(nh, nj)
                if nh != h:
                    pieces = [lambda nh=nh: load_w(nh)] + pieces
                if nh == h:
                    spread(inject, pieces, 1, nk)
                elif nk > 5:
                    spread(inject, pieces, 5, nk)
                else:
                    post.extend(pieces)
            QT, QTb = QT_of.pop((h, j))

            def score_fn(c, i, q0, ps, QT=QT, QTb=QTb):
                def fn(e):
                    return e.matmul(ps[:, q0:TT], lhsT=KT[c * 64:(c + 1) * 64, i * 128:(i + 1) * 128],
                                    rhs=QT[c * 64:(c + 1) * 64, q0:TT], start=True, stop=True)
                return fn, (KTb[i // 4], QTb)
            C.pe_mask = False
            attention_tile(P, C, j, 2, score_fn, V, lambda i: Vb[i // 4], pt_ring, O, L, 0.125, inject=inject,
                           mask_eng="dve")
            C.pe_mask = True
            EVAC_FIN1(h, j)
            for f in post:
                f()
            if conv_pieces and idx % 3 == 0:
                conv_pieces.pop(0)()
        FIN2(*seq[-1])
        for f in OUT(*seq[-1]):
            f()
        for f in conv_pieces:
            f()
        P.barrier()
        P.flush()


def from_mla(P, C, hook=None):
    nc = C.nc
    gcol = 32
    scale = 192.0 ** -0.5
    NQ = 1
    TQ = NT // NQ
    with ExitStack() as es_outer:
        CKVN = es_outer.enter_context(SBT(nc, "m_ckvn", [128, 2, S], BF16))
        CKVNb = [[Buf(f"m_ckvn{m}_{t}") for t in range(NT)] for m in range(2)]
        KR = es_outer.enter_context(SBT(nc, "m_kr", [128, S], BF16))
        KRb = [Buf(f"m_kr{t}") for t in range(NT)]
        wsem = P.dma_sem("m_wsem")
        cssem = P.dma_sem("m_cssem")
        win_src = C.s["mla_in"].rearrange("(c p) f -> p c f", p=128)
        wq_src = C.s["mla_q"].rearrange("(c p) f -> p c f", p=128)
        wkv_src = C.s["mla_kv"].rearrange("(c p) f -> p c f", p=128)

        def rope64(x32, x32b, CS, CSb, rope_ring, out_ap, out_b):
            pr, prb = C.ps_misc.next()
            P.op("pe", lambda e: e.matmul(pr[0:64, :], lhsT=C.rot32[0:64, 0:64], rhs=x32[0:64, :], start=True, stop=True),
                 reads=(x32b, C.rot_b), writes=(prb,))
            t2, t2b = rope_ring.next()
            P.op("dve", lambda e: e.tensor_tensor(out=t2[0:64, :], in0=pr[0:64, :], in1=CS[0:64, 1, :], op=ALU.mult),
                 reads=(prb, CSb), writes=(t2b,))
            P.op("dve", lambda e: e.tensor_tensor(out=x32[0:64, :], in0=x32[0:64, :], in1=CS[0:64, 0, :], op=ALU.mult),
                 reads=(x32b, CSb), writes=(x32b,))
            P.op("dve", lambda e: e.tensor_tensor(out=out_ap, in0=x32[0:64, :], in1=t2[0:64, :], op=ALU.add),
                 reads=(x32b, t2b), writes=(out_b,))

        for qp in range(NQ):
            with ExitStack() as es:
                WIN = es.enter_context(SBT(nc, "m_win", [128, NCH, 704], BF16))
                WINb = Buf("m_win")
                XN = es.enter_context(SBT(nc, "m_xn", [128, NCH, TT], BF16))
                XNb = [Buf(f"m_xn{c}") for c in range(NCH)]
                sq_ring = mk_ring(es, nc, "mc_sq", 2, [128, TT], BF16)
                f32_ring = mk_ring(es, nc, "mc_f32", 9, [128, TT], F32)
                CS = es.enter_context(SBT(nc, "mc_cs", [128, 2, TT], F32))
                CSb = Buf("mc_cs")
                cq_stage = Ring([(es.enter_context(SBT(nc, f"mc_cq{i}", [128, 3, TT], BF16)),
                                  [Buf(f"mc_cq{i}_{m}") for m in range(3)], P.dma_sem(f"mc_cqs{i}")) for i in range(2)])
                C.ps_misc = Ring(C.psum[0:8])
                P.dma("sp", WIN[:, :, :], win_src[:, :, :], wsem, reads=(C.sbuf_["mla_in"],), writes=(WINb,))
                for tl in range(TQ):
                    j = qp * TQ + tl
                    tsl = slice(j * TT, (j + 1) * TT)
                    norm_tile(P, C, j, gcol, [XN[:, c, :] for c in range(NCH)], XNb, sq_ring, f32_ring, sq_eng="pool")
                    P.dma_group("sp", [(CS[0:64, 0, :], C.cs_s[0, 0:64, tsl]), (CS[0:64, 1, :], C.cs_s[1, 0:64, tsl])], cssem,
                                reads=(C.cs_sb[j],), writes=(CSb,))
                    c32 = []
                    for m in (5, 0, 1, 2, 3, 4):
                        width = 128 if m < 5 else 64
                        ps, psb = C.ps_misc.next()

                        def proj(e, ps=ps, m=m, width=width):
                            for c in range(NCH):
                                ins = e.matmul(ps[0:width, :], lhsT=WIN[:, c, m * 128:m * 128 + width], rhs=XN[:, c, :],
                                               start=(c == 0), stop=(c == NCH - 1))
                            return ins
                        P.op("pe", proj, reads=(WINb, *XNb), writes=(psb,))
                        t32, t32b = f32_ring.next()
                        P.op("act", lambda e, t32=t32, ps=ps, width=width: e.activation(out=t32[0:width, :], in_=ps[0:width, :],
                                                                                       func=AF.Copy),
                             reads=(psb,), writes=(t32b,))
                        if m < 5:
                            c32.append((t32, t32b))
                        else:
                            rope64(t32, t32b, CS, CSb, f32_ring, KR[0:64, tsl], KRb[j])
                    rq, rqb = norm_stats(P, C, [c32[m][0][:, :] for m in range(3)], [c32[m][1] for m in range(3)], 384,
                                         sq_ring, f32_ring, sq_eng="pool")
                    cqt, cqtb, cqsem = cq_stage.next()
                    for m in range(3):
                        P.op("dve", lambda e, m=m, rq=rq, c32=c32, cqt=cqt: e.scalar_tensor_tensor(
                            out=cqt[:, m, :], in0=c32[m][0][:, :], scalar=C.gains_sb[:, 57 + m:58 + m],
                            in1=rq[:, :], op0=ALU.mult, op1=ALU.mult),
                            reads=(c32[m][1], rqb, C.gains_b), writes=(cqtb[m],))
                    P.dma("sp", C.cq_s[:, :, tsl], cqt[:, :, :], cqsem, reads=tuple(cqtb), writes=(C.cq_sb[j],))
                    rk, rkb = norm_stats(P, C, [c32[3 + m][0][:, :] for m in range(2)], [c32[3 + m][1] for m in range(2)], 256,
                                         sq_ring, f32_ring, sq_eng="pool")
                    for m in range(2):
                        P.op("dve", lambda e, m=m, j=j, rk=rk, c32=c32: e.scalar_tensor_tensor(
                            out=CKVN[:, m, j * TT:(j + 1) * TT], in0=c32[3 + m][0][:, :], scalar=C.gains_sb[:, 60 + m:61 + m],
                            in1=rk[:, :], op0=ALU.mult, op1=ALU.mult),
                            reads=(c32[3 + m][1], rkb, C.gains_b), writes=(CKVNb[m][j],))
                P.barrier()
                P.flush()
            with ExitStack() as es:
                WQ = es.enter_context(SBT(nc, "m_wq", [128, 3, 192], BF16))
                WKV = es.enter_context(SBT(nc, "m_wkv", [128, 2, 256], BF16))
                Wb = Buf("m_w")
                WOs = [(es.enter_context(SBT(nc, f"m_wo{i}", [128, D], BF16)), Buf(f"m_wo{i}"), P.dma_sem(f"m_wos{i}"))
                       for i in range(2)]
                KN = es.enter_context(SBT(nc, "m_kn", [128, S], BF16))
                KNb = [Buf(f"m_kn{t}") for t in range(NT)]
                V = es.enter_context(SBT(nc, "m_v", [128, S // 128, 129], BF16))
                Vb = [Buf(f"m_v{t}") for t in range(NT)]
                C.vones_b = Buf("m_vones")
                P.op("pool", lambda e: e.memset(V[:, :, 128:129], 1.0), writes=(C.vones_b,))
                conv_pieces = hook() if hook is not None else []
                cqt_ring = Ring([(es.enter_context(SBT(nc, f"md_cq{i}", [128, 3, TT], BF16)), Buf(f"md_cq{i}"),
                                  P.dma_sem(f"md_cqs{i}")) for i in range(2)])
                qn_ring = mk_ring(es, nc, "m_qn", 2, [128, TT], BF16)
                qr_ring = mk_ring(es, nc, "m_qr", 2, [128, TT], BF16)
                pt_ring = mk_ring(es, nc, "m_pt", 4, [128, TT], BF16)
                rope_ring = mk_ring(es, nc, "md_rope", 2, [128, TT], F32)
                onq_ring = Ring([(es.enter_context(SBT(nc, f"md_onq{i}", [128, TT], BF16)),
                                  [Buf(f"md_onq{i}_{sb}") for sb in range(4)]) for i in range(2)])
                small_ring = Ring([(es.enter_context(SBT(nc, f"md_small{i}", [128, 8], F32)),
                                    [Buf(f"md_r{i}_{hb}") for hb in range(2)]) for i in range(3)])
                CS = es.enter_context(SBT(nc, "md_cs", [128, 2, TT], F32))
                CSb = Buf("md_cs")
                on_ring = mk_ring(es, nc, "m_on", 2, [128, TT], BF16)
                C.ps_misc = Ring(C.psum[0:4])
                Oring = Ring([[C.psum[4], C.psum[5]], [C.psum[6], C.psum[7]]])
                nkt = (qp + 1) * TQ
                heads = list(range(C.head0, C.head0 + C.nheads))
                seq = [(h, tl) for h in heads for tl in range(TQ)]
                Q_of, on_of, fin_of = {}, {}, {}

                def load_w(h):
                    pairs = [(WQ[:, :, :], wq_src[:, :, h * 192:(h + 1) * 192]),
                             (WKV[:, :, :], wkv_src[:, :, h * 256:(h + 1) * 256])]
                    P.dma_group("sp", pairs, wsem, reads=(C.sbuf_["mla_q"], C.sbuf_["mla_kv"]), writes=(Wb,))
                    wo, wob, wosem = WOs[h % 2]
                    P.dma("sp", wo[:, :], C.s["mla_out"][h * 128:(h + 1) * 128, :], wosem, reads=(C.sbuf_["mla_out"],),
                          writes=(wob,))

                def KV(h, jjs=None):
                    for jj in (range(nkt) if jjs is None else jjs):
                        ps, psb = C.ps_misc.next()

                        def kproj(e, ps=ps, jj=jj):
                            for m in range(2):
                                ins = e.matmul(ps[:, :], lhsT=WKV[:, m, 0:128], rhs=CKVN[:, m, jj * TT:(jj + 1) * TT],
                                               start=(m == 0), stop=(m == 1))
                            return ins
                        P.op("pe", kproj, reads=(Wb, CKVNb[0][jj], CKVNb[1][jj]), writes=(psb,))
                        P.op("act", lambda e, ps=ps, jj=jj: e.activation(out=KN[:, jj * TT:(jj + 1) * TT], in_=ps[:, :],
                                                                         func=AF.Copy),
                             reads=(psb,), writes=(KNb[jj],))
                        ps, psb = C.ps_misc.next()

                        def vproj(e, ps=ps, jj=jj):
                            for sblk in range(4):
                                t0 = jj * TT + sblk * 128
                                for m in range(2):
                                    ins = e.matmul(ps[:, sblk * 128:(sblk + 1) * 128], lhsT=CKVN[:, m, t0:t0 + 128],
                                                   rhs=WKV[:, m, 128:256], start=(m == 0), stop=(m == 1))
                            return ins
                        P.op("pe", vproj, reads=(Wb, CKVNb[0][jj], CKVNb[1][jj]), writes=(psb,))
                        P.op("dve", lambda e, ps=ps, jj=jj: e.tensor_copy(out=V[:, jj * 4:(jj + 1) * 4, 0:128],
                                                                          in_=ps[:, :].rearrange("p (s e) -> p s e", s=4)),
                             reads=(psb,), writes=(Vb[jj],))

                def QPROJ(h, tl):
                    j = qp * TQ + tl
                    tsl = slice(j * TT, (j + 1) * TT)
                    st = {}
                    CQT, CQTb, cqsem = cqt_ring.next()
                    cq_rd = (CQTb,)

                    def qproj(e, ps, c0, width):
                        for m in range(3):
                            ins = e.matmul(ps[0:width, :], lhsT=WQ[:, m, c0:c0 + width], rhs=CQT[:, m, :],
                                           start=(m == 0), stop=(m == 2))
                        return ins

                    def p_load():
                        P.dma("sp", CQT[:, :, :], C.cq_s[:, :, tsl], cqsem, reads=(C.cq_sb[j],), writes=(CQTb,))
                        P.dma_group("sp", [(CS[0:64, 0, :], C.cs_s[0, 0:64, tsl]), (CS[0:64, 1, :], C.cs_s[1, 0:64, tsl])],
                                    cssem, reads=(C.cs_sb[j],), writes=(CSb,))

                    def p_n():
                        psn, psnb = C.ps_misc.next()
                        P.op("pe", lambda e: qproj(e, psn, 0, 128), reads=(Wb, *cq_rd), writes=(psnb,))
                        QN, QNb = qn_ring.next()
                        P.op("dve", lambda e: e.tensor_copy(out=QN[:, :], in_=psn[:, :]), reads=(psnb,), writes=(QNb,))
                        st["qn"] = (QN, QNb)

                    def p_r():
                        psr, psrb = C.ps_misc.next()
                        P.op("pe", lambda e: qproj(e, psr, 128, 64), reads=(Wb, *cq_rd), writes=(psrb,))
                        q32, q32b = rope_ring.next()
                        P.op("dve", lambda e: e.tensor_copy(out=q32[0:64, :], in_=psr[0:64, :]), reads=(psrb,),
                             writes=(q32b,))
                        st["q32"] = (q32, q32b)

                    def p_rope():
                        q32, q32b = st["q32"]
                        QR, QRb = qr_ring.next()
                        rope64(q32, q32b, CS, CSb, rope_ring, QR[0:64, :], QRb)
                        Q_of[(h, tl)] = (*st["qn"], QR, QRb)

                    return [p_load, p_n, p_r, p_rope]

                def EVAC(h, tl, O1):
                    sm8, rbs = small_ring.next()
                    for hb in range(2):
                        bank, bb = O1[hb]
                        P.op("dve", lambda e, bank=bank, hb=hb: e.reciprocal(out=sm8[:, hb * 2:hb * 2 + 2],
                                                                             in_=bank[:, 128:258:129]),
                             reads=(bb,), writes=(rbs[hb],))
                    onq, onqb = onq_ring.next()
                    for sb in range(4):
                        col = (sb % 2) * 129
                        b0, b0b = O1[sb // 2]
                        osl = slice(sb * 128, (sb + 1) * 128)
                        P.op("dve", lambda e, b0=b0, col=col, osl=osl, sb=sb: e.tensor_scalar(
                            out=onq[:, osl], in0=b0[:, col:col + 128], scalar1=sm8[:, sb:sb + 1], scalar2=None,
                            op0=ALU.mult),
                            reads=(b0b, rbs[sb // 2]), writes=(onqb[sb],))
                    fin_of[(h, tl)] = (onq, onqb)

                def FIN2(h, tl):
                    onq, onqb = fin_of.pop((h, tl))
                    ps, psb = C.ps_misc.next()

                    def tr(e):
                        for sb in range(4):
                            osl = slice(sb * 128, (sb + 1) * 128)
                            ins = e.matmul(ps[:, osl], lhsT=onq[:, osl], rhs=C.ident_bf[:, :], start=True, stop=True)
                        return ins
                    P.op("pe", tr, reads=(*onqb, C.ident_b), writes=(psb,))
                    on, onb = on_ring.next()
                    P.op("act", lambda e: e.activation(out=on[:, :], in_=ps[:, :], func=AF.Copy), reads=(psb,), writes=(onb,))
                    on_of[(h, tl)] = (on, onb)

                def OUT(h, tl):
                    def piece(dc):
                        def f():
                            on, onb = on_of[(h, tl)]
                            wo, wob, _ = WOs[h % 2]
                            out_proj_add(P, C, qp * TQ + tl, on, onb, wo, wob, dcs=(dc,))
                        return f
                    return [piece(dc) for dc in range(NCH)]

                def spread(inject, pieces, start, nk):
                    for k, f in enumerate(pieces):
                        inject.setdefault(min(start + k, nk - 1), []).append(f)

                load_w(seq[0][0])
                KV(seq[0][0])
                for f in QPROJ(*seq[0]):
                    f()
                for idx, (h, tl) in enumerate(seq):
                    j = qp * TQ + tl
                    nk = 4 * j + 4
                    inject = {}
                    post = []
                    if idx > 0:
                        ph, ptl = seq[idx - 1]
                        inject.setdefault(1, []).append(lambda ph=ph, ptl=ptl: FIN2(ph, ptl))
                        spread(inject, OUT(ph, ptl), 2, nk)
                    if idx + 1 < len(seq):
                        nh, ntl = seq[idx + 1]
                        if nh == h:
                            spread(inject, QPROJ(nh, ntl), 1, nk)
                        else:
                            inject.setdefault(1, []).append(lambda nh=nh: load_w(nh))
                            for jj in range(nkt):
                                at = 4 * jj + 6
                                if at <= nk - 1:
                                    inject.setdefault(at, []).append(lambda nh=nh, jj=jj: KV(nh, (jj,)))
                                else:
                                    post.append(lambda nh=nh, jj=jj: KV(nh, (jj,)))
                            spread(inject, QPROJ(nh, ntl), max(2, nk - 5), nk)
                    QN, QNb, QR, QRb = Q_of.pop((h, tl))

                    def score_fn(c, i, q0, ps, QN=QN, QNb=QNb, QR=QR, QRb=QRb):
                        def fn(e):
                            e.matmul(ps[:, q0:TT], lhsT=KN[:, i * 128:(i + 1) * 128], rhs=QN[:, q0:TT],
                                     start=True, stop=False)
                            return e.matmul(ps[:, q0:TT], lhsT=KR[0:64, i * 128:(i + 1) * 128], rhs=QR[0:64, q0:TT],
                                            start=False, stop=True)
                        return fn, (KNb[i // 4], KRb[i // 4], QNb, QRb)
                    O1 = Oring.next()
                    attention_tile(P, C, j, 1, score_fn, V, lambda i: Vb[i // 4], pt_ring, [O1], None, scale, inject=inject)
                    EVAC(h, tl, O1)
                    for f in post:
                        f()
                    if conv_pieces and idx % 3 == 0:
                        conv_pieces.pop(0)()
                FIN2(*seq[-1])
                for f in OUT(*seq[-1]):
                    f()
                for f in conv_pieces:
                    f()
                P.barrier()
                P.flush()


def host_constants():
    ones = np.ones((128, 128), np.float32)
    rot = np.zeros((128, 128), np.float32)
    for m in range(128):
        if (m % 64) < 32:
            rot[m + 32, m] = -1.0
        else:
            rot[m - 32, m] = 1.0
    mask = (np.arange(128)[:, None] <= np.arange(128)[None, :]).astype(np.float32)
    inv_freq = (10000.0 ** (-np.arange(0, 64, 2, dtype=np.float32) / np.float32(64))).astype(np.float32)
    af = (inv_freq.astype(np.float64) / (2.0 * math.pi)).astype(np.float32)
    af = np.tile(af, 4).reshape(128, 1)
    return ones, rot, mask, af


def pack_gains(inp):
    g = np.zeros((128, 64), np.float32)

    def put(col, vec):
        v = np.asarray(vec, np.float32).reshape(-1, 128)
        for i in range(v.shape[0]):
            g[:, col + i] = v[i]
    for l in range(2):
        put((l * 3 + 0) * 8, inp["ffn1_norm"][l])
        put((l * 3 + 1) * 8, inp["mix_norm"][l])
        put((l * 3 + 2) * 8, inp["ffn2_norm"][l])
    put(48, inp["final_norm"])
    put(56, inp["diff_sub_norm"][0])
    put(57, inp["mla_q_norm"][0])
    put(60, inp["mla_kv_norm"][0])
    return g


def make_in_maps(inp, xT_list):
    ones, rot, mask, af = host_constants()
    gains = pack_gains(inp)
    lamv = np.concatenate([np.asarray(inp[k], np.float32).reshape(-1) for k in
                           ("diff_lambda_q1", "diff_lambda_k1", "diff_lambda_q2", "diff_lambda_k2")])
    lamv = np.ascontiguousarray(np.broadcast_to(lamv[None, :], (128, 256)))
    pos = np.asarray(inp["positions"], np.int32)
    maps = []
    for b, xT in enumerate(xT_list):
        m = {
            "xT": xT,
            "pos": np.ascontiguousarray(np.broadcast_to(pos[b][None, :], (128, S))),
            "gains": gains, "lamv": lamv, "ones_c": ones, "rot_c": rot, "mask_c": mask, "af_c": af,
            "ident_c": np.eye(128, dtype=np.float32),
        }
        for k in ("ffn1_w_gate", "ffn1_w_up", "ffn1_w_down", "ffn2_w_gate", "ffn2_w_up", "ffn2_w_down",
                  "diff_w_in", "diff_w_out", "mla_w_in", "mla_w_q_up", "mla_w_kv_up", "mla_w_out"):
            m[k] = np.asarray(inp[k], np.float32)
        maps.append(m)
    return maps


_NC_CACHE = {}


def run_phases(inp, xT_list, phases, core_ids=None, **bkw):
    key = (tuple(phases), tuple(sorted(bkw.items())))
    if key not in _NC_CACHE:
        _NC_CACHE[key] = build(phases, **bkw)
    nc = _NC_CACHE[key]
    maps = make_in_maps(inp, xT_list)
    if core_ids is None:
        core_ids = list(range(len(xT_list)))
    res = run_bass_kernel_spmd(nc, maps, core_ids=core_ids)
    if bkw.get("debug"):
        return [r["outT"] for r in res.results], [r["dbg"] for r in res.results]
    return [r["outT"] for r in res.results]


LAUNCH_PLAN = [ALL_PHASES]


def kernel(**inputs):
    x = np.asarray(inputs["x"], np.float32)
    B = x.shape[0]
    xT = [np.ascontiguousarray(x[b].T) for b in range(B)]
    cur = xT
    for phases in LAUNCH_PLAN:
        cur = run_phases(inputs, cur, list(phases))
    out = np.stack([np.ascontiguousarray(o.T) for o in cur], axis=0)
    return out.astype(np.float32)
```

```python
import math
from contextlib import ExitStack

import numpy as np
import concourse.bass as bass
import concourse.mybir as mybir
from concourse.bass_utils import run_bass_kernel_spmd

F32 = mybir.dt.float32
BF16 = mybir.dt.bfloat16
I32 = mybir.dt.int32
AF = mybir.ActivationFunctionType
ALU = mybir.AluOpType
AX = mybir.AxisListType

D = 1024
S = 4096
DFF = 2816
NCH = 8
TT = 512
NT = S // TT
EPS = 1e-6
TWO_PI = 2.0 * math.pi

SAME_ENGINE_SYNC = True


class Buf:
    __slots__ = ("name", "w", "r")

    def __init__(self, name=""):
        self.name = name
        self.w = None
        self.r = {}


class DmaSem:
    def __init__(self, handle, sid):
        self.h = handle
        self.sid = sid
        self.count = 0


class Prog:
    ENGS = ("pe", "act", "dve", "pool", "sp")

    def __init__(self, nc, es):
        self.nc = nc
        self.sem = {}
        self.cnt = {}
        self.seen = {}
        self.lists = {}
        self._sid = 0
        for e in self.ENGS:
            h = es.enter_context(nc.semaphore("prog_" + e))
            self.sem[e] = (h, self._next_sid())
            self.cnt[e] = 0
            self.seen[e] = {}
            self.lists[e] = []
        self.es = es
        self.n_ops = 0
        self.pending_dma = {}

    def _next_sid(self):
        self._sid += 1
        return self._sid

    def dma_sem(self, name):
        _UNIQ[0] += 1
        h = self.es.enter_context(self.nc.semaphore(f"{name}_u{_UNIQ[0]}"))
        return DmaSem(h, self._next_sid())

    def _waits(self, eng, reads, writes):
        evs = []
        for b in reads:
            if b.w is not None:
                evs.append(b.w)
        for b in writes:
            if b.w is not None:
                evs.append(b.w)
            evs.extend(b.r.values())
        need = {}
        seen = self.seen[eng]
        for (sid, h, val, src) in evs:
            if src == eng and (eng in ("pe", "sp") or not SAME_ENGINE_SYNC):
                continue
            if seen.get(sid, 0) >= val:
                continue
            if sid not in need or need[sid][1] < val:
                need[sid] = (h, val)
        for sid, (h, val) in need.items():
            self.lists[eng].append(("w", h, val))
            seen[sid] = val

    def _commit(self, ev, reads, writes):
        for b in writes:
            b.w = ev
            b.r = {}
        for b in reads:
            b.r[ev[0]] = ev

    def op(self, eng, fn, reads=(), writes=()):
        self._waits(eng, reads, writes)
        h, sid = self.sem[eng]
        self.cnt[eng] += 1
        ev = (sid, h, self.cnt[eng], eng)
        self.lists[eng].append(("o", fn, h, 1))
        self._commit(ev, reads, writes)
        self.n_ops += 1
        return ev

    def dma(self, eng, out, in_, dsem, reads=(), writes=(), sbuf=True):
        return self.dma_group(eng, [(out, in_)], dsem, reads, writes, sbuf=sbuf)

    def dma_group(self, eng, pairs, dsem, reads=(), writes=(), sbuf=True):
        self._waits(eng, reads, writes)
        for (out, in_) in pairs:
            dsem.count += 16
            self.lists[eng].append(("o", lambda e, o=out, i=in_: e.dma_start(out=o, in_=i), dsem.h, 16))
        ev = (dsem.sid, dsem.h, dsem.count, "dma")
        self._commit(ev, reads, writes)
        if sbuf:
            self.pending_dma[dsem.sid] = ev
        return ev

    def wait_event(self, eng, ev):
        sid, h, val, _ = ev
        if self.seen[eng].get(sid, 0) < val:
            self.lists[eng].append(("w", h, val))
            self.seen[eng][sid] = val

    def barrier(self):
        for e in self.ENGS:
            for ev in self.pending_dma.values():
                self.wait_event(e, ev)
        self.pending_dma = {}
        for e in self.ENGS:
            for o in self.ENGS:
                if o == e or self.cnt[o] == 0:
                    continue
                h, sid = self.sem[o]
                if self.seen[e].get(sid, 0) < self.cnt[o]:
                    self.lists[e].append(("w", h, self.cnt[o]))
                    self.seen[e][sid] = self.cnt[o]

    def flush(self):
        lists = self.lists
        self.lists = {e: [] for e in self.ENGS}

        def replay(handle, items):
            for it in items:
                if it[0] == "w":
                    handle.wait_ge(it[1], it[2])
                else:
                    ins = it[1](handle)
                    ins.then_inc(it[2], it[3])

        with self.nc.Block(no_gpsimd_drain=True) as block:
            if lists["pe"]:
                @block.tensor
                def _(eng):
                    replay(eng, lists["pe"])
            if lists["act"]:
                @block.scalar
                def _(eng):
                    replay(eng, lists["act"])
            if lists["dve"]:
                @block.vector
                def _(eng):
                    replay(eng, lists["dve"])
            if lists["pool"]:
                @block.gpsimd
                def _(eng):
                    replay(eng, lists["pool"])
            if lists["sp"]:
                @block.sync
                def _(eng):
                    replay(eng, lists["sp"])


_UNIQ = [0]


def SBT(nc, name, shape, dt):
    _UNIQ[0] += 1
    return nc.sbuf_tensor(f"{name}_u{_UNIQ[0]}", shape, dt)


class Ring:
    def __init__(self, items):
        self.items = items
        self.i = 0

    def next(self):
        it = self.items[self.i % len(self.items)]
        self.i += 1
        return it


class Ctx:
    pass


def mk_ring(es, nc, name, n, shape, dt, P=None):
    items = []
    for i in range(n):
        t = es.enter_context(SBT(nc, f"{name}{i}", shape, dt))
        if P is None:
            items.append((t, Buf(f"{name}{i}")))
        else:
            items.append((t, Buf(f"{name}{i}"), P.dma_sem(f"{name}_s{i}")))
    return Ring(items)


def declare_inputs(nc, C):
    def inp(name, shape, dt=F32):
        return nc.dram_tensor(name, list(shape), dt, kind="ExternalInput")

    C.xT = inp("xT", [D, S])
    C.pos = inp("pos", [128, S], I32)
    C.gains = inp("gains", [128, 64])
    C.lamv = inp("lamv", [128, 256])
    C.ones_in = inp("ones_c", [128, 128])
    C.rot_in = inp("rot_c", [128, 128])
    C.mask_in = inp("mask_c", [128, 128])
    C.ident_in = inp("ident_c", [128, 128])
    C.af_in = inp("af_c", [128, 1])
    C.w = {}
    for f in ("ffn1", "ffn2"):
        C.w[f + "_g"] = inp(f + "_w_gate", [2, D, DFF])
        C.w[f + "_u"] = inp(f + "_w_up", [2, D, DFF])
        C.w[f + "_d"] = inp(f + "_w_down", [2, DFF, D])
    C.w["diff_in"] = inp("diff_w_in", [1, D, 3 * D])
    C.w["diff_out"] = inp("diff_w_out", [1, D, D])
    C.w["mla_in"] = inp("mla_w_in", [1, D, 704])
    C.w["mla_q"] = inp("mla_w_q_up", [1, 384, 1536])
    C.w["mla_kv"] = inp("mla_w_kv_up", [1, 256, 2048])
    C.w["mla_out"] = inp("mla_w_out", [1, D, D])
    C.outT = nc.dram_tensor("outT", [D, S], F32, kind="ExternalOutput")
    C.dbg = nc.dram_tensor("dbg", [16, 128, TT], F32, kind="ExternalOutput") if getattr(C, "debug", False) else None
    C.s = {}
    C.sbuf_ = {}
    C.ssem = {}
    C.grp_bufs = {}

    def scr(key, shape):
        C.s[key] = nc.dram_tensor("s_" + key, list(shape), BF16)
        C.sbuf_[key] = Buf("s_" + key)

    for l in range(2):
        for f in ("ffn1", "ffn2"):
            scr(f"{f}_g{l}", [D, DFF])
            scr(f"{f}_u{l}", [D, DFF])
            scr(f"{f}_d{l}", [DFF, D])
    scr("diff_in", [D, 3 * D])
    scr("diff_out", [D, D])
    scr("mla_in", [D, 704])
    scr("mla_q", [384, 1536])
    scr("mla_kv", [256, 2048])
    scr("mla_out", [D, D])
    C.cs_s = nc.dram_tensor("s_cs", [2, 128, S], F32)
    C.cs_sb = [Buf(f"s_cs{t}") for t in range(NT)]
    C.cq_s = nc.dram_tensor("s_cq", [128, 3, S], BF16)
    C.cq_sb = [Buf(f"s_cq{t}") for t in range(NT)]
    C.xn_s = nc.dram_tensor("s_xn", [128, NCH, S], BF16)
    C.xn_sb = [Buf(f"s_xn{t}") for t in range(NT)]


FFN_CHUNKS = [(0, 2), (2, 5), (5, 8), (8, 11)]


def convert_weights(P, C, keys, chunked=False, which=None, batch=2):
    if chunked:
        f, l = keys[0][:4], int(keys[0][6])
        kg, ku, kd = f"{f}_g{l}", f"{f}_u{l}", f"{f}_d{l}"
        bufs = C.grp_bufs.setdefault((f, l), [None] * 11)
        for ci, (g0, g1) in enumerate(FFN_CHUNKS):
            if which is not None and ci not in which:
                continue
            ds = P.dma_sem(f"cvc_{f}{l}_{g0}")
            b = Buf(f"cvc_{f}{l}_{g0}")
            pairs = []
            c0, c1 = g0 * 256, g1 * 256
            for kk, kind in ((kg, "g"), (ku, "u")):
                src = C.w[f"{f}_{kind}"][l]
                for r0 in range(0, D, 512):
                    pairs.append((C.s[kk][r0:r0 + 512, c0:c1], src[r0:r0 + 512, c0:c1]))
            srcd = C.w[f"{f}_d"][l]
            for r0 in range(c0, c1, 128):
                pairs.append((C.s[kd][r0:r0 + 128, :], srcd[r0:r0 + 128, :]))
            P.dma_group("pool", pairs, ds, reads=(), writes=(b,), sbuf=False)
            for g in range(g0, g1):
                bufs[g] = b
        for k in keys:
            C.ssem[k] = None
        return
    for key in keys:
        if key in C.ssem:
            continue
        ds = P.dma_sem("cv_" + key)
        C.ssem[key] = ds
        dst = C.s[key]
        if key[:3] == "ffn":
            f, kind, l = key[:4], key[5], int(key[6])
            src = C.w[f"{f}_{kind}"][l]
        else:
            src = C.w[key][0]
        rows = dst.shape[0]
        step = 128
        pairs = []
        for r0 in range(0, rows, step):
            r1 = min(rows, r0 + step)
            pairs.append((dst[r0:r1, :], src[r0:r1, :]))
        for b0 in range(0, len(pairs), batch):
            P.dma_group("pool", pairs[b0:b0 + batch], ds, reads=(), writes=(C.sbuf_[key],), sbuf=False)


def setup_persistent(P, C, es):
    nc = C.nc
    C.X = es.enter_context(SBT(nc, "X", [128, NCH, S], F32))
    C.Xb = [[Buf(f"X{c}_{t}") for t in range(NT)] for c in range(NCH)]
    C.gains_sb = es.enter_context(SBT(nc, "gains_sb", [128, 64], F32))
    C.gains_b = Buf("gains")
    C.ones_bf = es.enter_context(SBT(nc, "ones_bf", [128, 128], BF16))
    C.ones_b = Buf("ones")
    C.rot32 = es.enter_context(SBT(nc, "rot32", [128, 128], F32))
    C.rot_b = Buf("rot")
    C.mask_bf = es.enter_context(SBT(nc, "mask_bf", [128, 128], BF16))
    C.mask_b = Buf("mask")
    C.ident_bf = es.enter_context(SBT(nc, "ident_bf", [128, 128], BF16))
    C.ident_b = Buf("ident")
    C.negm_bf = es.enter_context(SBT(nc, "negm_bf", [128, 128], BF16))
    C.negm_b = Buf("negm")
    C.af = es.enter_context(SBT(nc, "af", [128, 1], F32))
    C.af_b = Buf("af")
    C.cst = es.enter_context(SBT(nc, "cst", [128, 4], F32))
    C.cst_b = Buf("cst")
    C.psum = []
    for i in range(8):
        t = es.enter_context(nc.psum_tensor(f"ps{i}", [128, TT], F32))
        C.psum.append((t, Buf(f"ps{i}")))
    C.ld_sem = P.dma_sem("ld_const")
    C.x_sems = [P.dma_sem(f"ld_x{i}") for i in range(8)]
    C.out_sem = P.dma_sem("st_out")


def load_constants(P, C, es):
    nc = C.nc
    tmp = es.enter_context(SBT(nc, "ctmp", [128, 384], F32))
    tb = Buf("ctmp")
    P.dma_group("sp", [(C.gains_sb[:, :], C.gains[:, :]), (C.rot32[:, :], C.rot_in[:, :]), (C.af[:, :], C.af_in[:, :]),
                       (tmp[:, 0:128], C.ones_in[:, :]), (tmp[:, 128:256], C.mask_in[:, :]),
                       (tmp[:, 256:384], C.ident_in[:, :])], C.ld_sem,
                writes=(C.gains_b, C.rot_b, C.af_b, tb))
    P.op("dve", lambda e: e.tensor_copy(out=C.ident_bf[:, :], in_=tmp[:, 256:384]), reads=(tb,), writes=(C.ident_b,))
    P.op("dve", lambda e: e.tensor_scalar(out=C.negm_bf[:, :], in0=tmp[:, 128:256], scalar1=-1.0, scalar2=30000.0,
                                          op0=ALU.add, op1=ALU.mult), reads=(tb,), writes=(C.negm_b,))
    P.op("dve", lambda e: e.tensor_copy(out=C.ones_bf[:, :], in_=tmp[:, 0:128]), reads=(tb,), writes=(C.ones_b,))
    P.op("dve", lambda e: e.tensor_copy(out=C.mask_bf[:, :], in_=tmp[:, 128:256]), reads=(tb,), writes=(C.mask_b,))

    def cfn(e):
        e.memset(C.cst[:, 0:1], -math.pi)
        return e.memset(C.cst[:, 1:2], EPS)
    P.op("dve", cfn, writes=(C.cst_b,))


def load_x(P, C):
    src = C.xT.rearrange("(c p) t -> p c t", p=128)
    for hf in range(2):
        t0 = hf * (S // 2)
        for cg in range(4):
            pairs = [(C.X[:, c, t0:t0 + S // 2], src[:, c, t0:t0 + S // 2]) for c in (2 * cg, 2 * cg + 1)]
            wr = tuple(C.Xb[c][t] for c in (2 * cg, 2 * cg + 1) for t in range(hf * 4, hf * 4 + 4))
            P.dma_group("sp", pairs, C.x_sems[hf * 4 + cg], writes=wr)


def norm_stats(P, C, srcs, src_bufs, nfeat, sq_ring, f32_ring, parts=128, sq_eng="act"):
    ps, psb = C.ps_misc.next()
    n = len(srcs)
    for i, (a, b) in enumerate(zip(srcs, src_bufs)):
        sq, sqb = sq_ring.next()
        if sq_eng == "act":
            P.op("act", lambda e, a=a, sq=sq: e.activation(out=sq[0:parts, :], in_=a, func=AF.Square),
                 reads=(b,), writes=(sqb,))
        else:
            P.op(sq_eng, lambda e, a=a, sq=sq: e.tensor_tensor(out=sq[0:parts, :], in0=a, in1=a, op=ALU.mult),
                 reads=(b,), writes=(sqb,))
        P.op("pe", lambda e, sq=sq, i=i, ps=ps: e.matmul(ps[:, :], lhsT=C.ones_bf[0:parts, :], rhs=sq[0:parts, :],
                                                          start=(i == 0), stop=(i == n - 1)),
             reads=(sqb, C.ones_b), writes=(psb,))
    r, rb = f32_ring.next()
    P.op("act", lambda e: e.activation(out=r[:, :], in_=ps[:, :], func=AF.Ln, bias=C.cst[:, 1:2], scale=1.0 / nfeat),
         reads=(psb, C.cst_b), writes=(rb,))
    P.op("act", lambda e: e.activation(out=r[:, :], in_=r[:, :], func=AF.Exp, scale=-0.5),
         reads=(rb,), writes=(rb,))
    return r, rb


def norm_tile(P, C, t, gcol, outs, out_bufs, sq_ring, f32_ring, eng="dve", sq_eng="act"):
    sl = slice(t * TT, (t + 1) * TT)
    srcs = [C.X[:, c, sl] for c in range(NCH)]
    r, rb = norm_stats(P, C, srcs, [C.Xb[c][t] for c in range(NCH)], D, sq_ring, f32_ring, sq_eng=sq_eng)
    for c in range(NCH):
        P.op(eng, lambda e, c=c: e.scalar_tensor_tensor(out=outs[c], in0=C.X[:, c, sl],
                                                        scalar=C.gains_sb[:, gcol + c:gcol + c + 1],
                                                        in1=r[:, :], op0=ALU.mult, op1=ALU.mult),
             reads=(C.Xb[c][t], rb, C.gains_b), writes=(out_bufs[c],))


def ffn_phase(P, C, l, f):
    nc = C.nc
    G = 2
    NG = DFF // (128 * G)
    ST = 4
    gcol = (l * 3 + (0 if f == "ffn1" else 2)) * 8
    kg, ku, kd = f"{f}_g{l}", f"{f}_u{l}", f"{f}_d{l}"
    with ExitStack() as es:
        XN = es.enter_context(SBT(nc, "ffn_xn", [128, ST, NCH, TT], BF16))
        XNb = [[Buf(f"xn{t}_{c}") for c in range(NCH)] for t in range(ST)]
        Wg = [es.enter_context(SBT(nc, f"ffn_wg{i}", [128, NCH, G * 128], BF16)) for i in range(2)]
        Wu = [es.enter_context(SBT(nc, f"ffn_wu{i}", [128, NCH, G * 128], BF16)) for i in range(2)]
        Wd = [es.enter_context(SBT(nc, f"ffn_wd{i}", [128, G, D], BF16)) for i in range(2)]
        Wb = [Buf(f"ffn_w{i}") for i in range(2)]
        wsem = [P.dma_sem(f"ffn_ws{l}{f}{i}") for i in range(2)]
        sq_ring = mk_ring(es, nc, "ffn_sq", 2, [128, TT], BF16)
        f32_ring = mk_ring(es, nc, "ffn_f32", 2, [128, TT], F32)
        sg_ring = mk_ring(es, nc, "ffn_sg", 2, [128, TT], BF16)
        h_ring = mk_ring(es, nc, "ffn_h", 4, [128, TT], BF16)
        C.ps_misc = Ring(C.psum[4:8])
        psA = C.psum[0:4]
        psB = Ring(C.psum[4:8])

        sg_src = C.s[kg].rearrange("(c p) f -> p c f", p=128)
        su_src = C.s[ku].rearrange("(c p) f -> p c f", p=128)
        sd_src = C.s[kd].rearrange("(g p) d -> p g d", p=128)

        def load_group(g):
            sl = g % 2
            P.dma_group("sp", [(Wg[sl][:, :, :], sg_src[:, :, g * 256:(g + 1) * 256]),
                               (Wu[sl][:, :, :], su_src[:, :, g * 256:(g + 1) * 256]),
                               (Wd[sl][:, :, :], sd_src[:, g * G:(g + 1) * G, :])], wsem[sl],
                        reads=((C.grp_bufs[(f, l)][g],) if (f, l) in C.grp_bufs else (C.sbuf_[kg], C.sbuf_[ku], C.sbuf_[kd])),
                        writes=(Wb[sl],))

        for st in range(NT // ST):
            load_group(0)
            for tl in range(ST):
                t = st * ST + tl
                norm_tile(P, C, t, gcol, [XN[:, tl, c, :] for c in range(NCH)], XNb[tl], sq_ring, f32_ring)
            iters = [(g, tl) for g in range(NG) for tl in range(ST)]

            def GU(it):
                g, tl = it
                sl = g % 2
                hs = []
                for fi in range(G):
                    pg, pgb = psA[2 * fi]
                    pu, pub = psA[2 * fi + 1]

                    def mm(e, w, ps, fi=fi, tl=tl):
                        for c in range(NCH):
                            ins = e.matmul(ps[:, :], lhsT=w[:, c, fi * 128:(fi + 1) * 128], rhs=XN[:, tl, c, :],
                                           start=(c == 0), stop=(c == NCH - 1))
                        return ins
                    P.op("pe", lambda e, mm=mm, w=Wg[sl], ps=pg: mm(e, w, ps), reads=(Wb[sl], *XNb[tl]), writes=(pgb,))
                    P.op("pe", lambda e, mm=mm, w=Wu[sl], ps=pu: mm(e, w, ps), reads=(Wb[sl], *XNb[tl]), writes=(pub,))
                    sg, sgb = sg_ring.next()
                    P.op("act", lambda e, sg=sg, pg=pg: e.activation(out=sg[:, :], in_=pg[:, :], func=AF.Silu),
                         reads=(pgb,), writes=(sgb,))
                    h, hb = h_ring.next()
                    P.op("dve", lambda e, h=h, sg=sg, pu=pu: e.tensor_tensor(out=h[:, :], in0=sg[:, :], in1=pu[:, :],
                                                                             op=ALU.mult),
                         reads=(sgb, pub), writes=(hb,))
                    hs.append((h, hb))
                return hs

            def DOWN(it, hs):
                g, tl = it
                sl = g % 2
                t = st * ST + tl
                tsl = slice(t * TT, (t + 1) * TT)
                for dc in range(NCH):
                    pb, pbb = psB.next()

                    def mm(e, dc=dc, pb=pb):
                        for fi in range(G):
                            ins = e.matmul(pb[:, :], lhsT=Wd[sl][:, fi, dc * 128:(dc + 1) * 128], rhs=hs[fi][0][:, :],
                                           start=(fi == 0), stop=(fi == G - 1))
                        return ins
                    P.op("pe", mm, reads=(Wb[sl], hs[0][1], hs[1][1]), writes=(pbb,))
                    P.op("dve", lambda e, dc=dc, pb=pb: e.scalar_tensor_tensor(
                        out=C.X[:, dc, tsl], in0=pb[:, :], scalar=0.5, in1=C.X[:, dc, tsl],
                        op0=ALU.mult, op1=ALU.add),
                        reads=(pbb, C.Xb[dc][t]), writes=(C.Xb[dc][t],))

            prev = None
            for k, it in enumerate(iters):
                hs = GU(it)
                if prev is not None:
                    DOWN(*prev)
                if it[1] == 0 and it[0] + 1 < NG:
                    load_group(it[0] + 1)
                prev = (it, hs)
            DOWN(*prev)
        P.barrier()
        P.flush()


def rope_tables(P, C, j, pos_ring, f32_ring):
    pt, ptb, psem = pos_ring.next()
    P.dma("sp", pt[:, :], C.pos[:, j * TT:(j + 1) * TT], psem, writes=(ptb,))
    outs = []
    for off in (0.25, 0.0):
        u, ub = f32_ring.next()
        ii, iib = C.i32_ring.next()
        P.op("dve", lambda e, u=u: e.tensor_copy(out=u[:, :], in_=pt[:, :]), reads=(ptb,), writes=(ub,))
        P.op("dve", lambda e, u=u, off=off: e.tensor_scalar(out=u[:, :], in0=u[:, :], scalar1=C.af[:, 0:1],
                                                            scalar2=off, op0=ALU.mult, op1=ALU.add),
             reads=(ub, C.af_b), writes=(ub,))
        P.op("dve", lambda e, u=u, ii=ii: e.tensor_copy(out=ii[:, :], in_=u[:, :]), reads=(ub,), writes=(iib,))
        P.op("dve", lambda e, u=u, ii=ii: e.tensor_tensor(out=u[:, :], in0=u[:, :], in1=ii[:, :], op=ALU.subtract),
             reads=(ub, iib), writes=(ub,))
        P.op("act", lambda e, u=u: e.activation(out=u[:, :], in_=u[:, :], func=AF.Sin, scale=TWO_PI),
             reads=(ub,), writes=(ub,))
        outs.append((u, ub))
    return outs[0], outs[1]


def tables_phase(P, C, es):
    nc = C.nc
    if True:
        f32_ring = mk_ring(es, nc, "tb_f32", 16, [128, TT], F32)
        C.i32_ring = mk_ring(es, nc, "tb_i32", 4, [128, TT], I32)
        pos_ring = mk_ring(es, nc, "tb_pos", 8, [128, TT], I32, P=P)
        st_sems = [P.dma_sem(f"tb_st{i}") for i in range(16)]
        for j in range(NT):
            cosT, sinT = rope_tables(P, C, j, pos_ring, f32_ring)
            sl = slice(j * TT, (j + 1) * TT)
            P.dma("sp", C.cs_s[0, :, sl], cosT[0][:, :], st_sems[2 * j], reads=(cosT[1],), writes=(C.cs_sb[j],))
            P.dma("sp", C.cs_s[1, :, sl], sinT[0][:, :], st_sems[2 * j + 1], reads=(sinT[1],), writes=(C.cs_sb[j],))


def apply_rope(P, C, ps, psb, parts, cosT, sinT, out_ap, out_buf, f32_ring, scale_ap=None, scale_buf=None):
    (cs, csb), (sn, snb) = cosT, sinT
    q32, q32b = f32_ring.next()
    if scale_ap is None:
        P.op("act", lambda e: e.activation(out=q32[0:parts, :], in_=ps[0:parts, :], func=AF.Copy),
             reads=(psb,), writes=(q32b,))
    else:
        P.op("dve", lambda e: e.tensor_tensor(out=q32[0:parts, :], in0=ps[0:parts, :], in1=scale_ap[0:parts, :],
                                              op=ALU.mult),
             reads=(psb, scale_buf), writes=(q32b,))
    pr, prb = C.ps_misc.next()
    P.op("pe", lambda e: e.matmul(pr[0:parts, :], lhsT=C.rot32[0:parts, 0:parts], rhs=q32[0:parts, :],
                                  start=True, stop=True),
         reads=(q32b, C.rot_b), writes=(prb,))
    t1, t1b = f32_ring.next()
    P.op("dve", lambda e: e.tensor_tensor(out=t1[0:parts, :], in0=q32[0:parts, :], in1=cs[0:parts, :], op=ALU.mult),
         reads=(q32b, csb), writes=(t1b,))
    t2, t2b = f32_ring.next()
    P.op("dve", lambda e: e.tensor_tensor(out=t2[0:parts, :], in0=pr[0:parts, :], in1=sn[0:parts, :], op=ALU.mult),
         reads=(prb, snb), writes=(t2b,))
    P.op("dve", lambda e: e.tensor_tensor(out=out_ap, in0=t1[0:parts, :], in1=t2[0:parts, :], op=ALU.add),
         reads=(t1b, t2b), writes=(out_buf,))


def final_phase(P, C):
    nc = C.nc
    gcol = 48
    dst = C.outT.rearrange("(c p) t -> p c t", p=128)
    with ExitStack() as es:
        sq_ring = mk_ring(es, nc, "fin_sq", 2, [128, TT], BF16)
        f32_ring = mk_ring(es, nc, "fin_f32", 2, [128, TT], F32)
        o_ring = mk_ring(es, nc, "fin_o", 6, [128, TT], F32, P=P)
        C.ps_misc = Ring(C.psum[0:4])
        evs = {}
        for t in range(NT):
            sl = slice(t * TT, (t + 1) * TT)
            srcs = [C.X[:, c, sl] for c in range(NCH)]
            r, rb = norm_stats(P, C, srcs, [C.Xb[c][t] for c in range(NCH)], D, sq_ring, f32_ring)
            for c in range(NCH):
                o, ob, osem = o_ring.next()
                P.op("dve", lambda e, c=c, o=o, sl=sl, r=r: e.scalar_tensor_tensor(
                    out=o[:, :], in0=C.X[:, c, sl], scalar=C.gains_sb[:, gcol + c:gcol + c + 1], in1=r[:, :],
                    op0=ALU.mult, op1=ALU.mult),
                    reads=(C.Xb[c][t], rb, C.gains_b), writes=(ob,))
                evs[osem.sid] = P.dma("sp", dst[:, c, sl], o[:, :], osem, reads=(ob,))
        for ev in evs.values():
            P.wait_event("sp", ev)
        P.barrier()
        P.flush()


def store_x_raw(P, C):
    dst = C.outT.rearrange("(c p) t -> p c t", p=128)
    pairs = []
    rd = []
    for c in range(NCH):
        for hf in range(2):
            t0 = hf * (S // 2)
            pairs.append((dst[:, c, t0:t0 + S // 2], C.X[:, c, t0:t0 + S // 2]))
        rd.extend(C.Xb[c])
    ev = P.dma_group("sp", pairs, C.out_sem, reads=tuple(rd))
    P.wait_event("sp", ev)
    P.barrier()
    P.flush()


WEIGHT_KEYS = {
    "ffn1_0": ["ffn1_g0", "ffn1_u0", "ffn1_d0"],
    "diff": ["diff_in", "diff_out"],
    "ffn2_0": ["ffn2_g0", "ffn2_u0", "ffn2_d0"],
    "ffn1_1": ["ffn1_g1", "ffn1_u1", "ffn1_d1"],
    "mla": ["mla_in", "mla_q", "mla_kv", "mla_out"],
    "ffn2_1": ["ffn2_g1", "ffn2_u1", "ffn2_d1"],
}
ALL_PHASES = ["ffn1_0", "diff", "ffn2_0", "ffn1_1", "mla", "ffn2_1", "final"]


def build(phases, debug=False, nheads=8, ntiles=NT, head0=0):
    nc = bass.Bass("TRN2", target_bir_lowering=False)
    C = Ctx()
    C.nc = nc
    C.debug = debug
    C.nheads = nheads
    C.ntiles = ntiles
    C.head0 = head0
    C.pe_mask = True
    declare_inputs(nc, C)
    with ExitStack() as es:
        P = Prog(nc, es)
        setup_persistent(P, C, es)
        wphases = [ph for ph in phases if ph in WEIGHT_KEYS]
        with ExitStack() as es0:
            first_chunked = bool(wphases) and wphases[0].startswith("ffn")
            if wphases:
                convert_weights(P, C, WEIGHT_KEYS[wphases[0]], chunked=first_chunked,
                                which=(0, 1) if first_chunked else None)
            load_constants(P, C, es0)
            load_x(P, C)
            if "diff" in phases or "mla" in phases:
                tables_phase(P, C, es0)
            P.barrier()
            P.flush()
        tables_done = True
        for ph in phases:
            if ph in WEIGHT_KEYS:
                k = wphases.index(ph)
                if k == 0 and first_chunked:
                    convert_weights(P, C, WEIGHT_KEYS[ph], chunked=True, which=(2, 3))
                hook = None
                if ph in ("diff", "mla"):
                    def hook(k=k, batch=(3 if ph == "mla" else 2)):
                        pieces = []
                        kk = k + 1
                        while kk < len(wphases):
                            for key in WEIGHT_KEYS[wphases[kk]]:
                                pieces.append(lambda key=key: convert_weights(P, C, [key], batch=batch))
                            if wphases[kk] in ("diff", "mla"):
                                break
                            kk += 1
                        return pieces
                elif k + 1 < len(wphases):
                    convert_weights(P, C, WEIGHT_KEYS[wphases[k + 1]])
            if ph in ("diff", "mla") and not tables_done:
                tables_phase(P, C)
                tables_done = True
            if ph.startswith("ffn"):
                ffn_phase(P, C, int(ph[5]), ph[:4])
            elif ph == "diff":
                from_diff(P, C, hook)
            elif ph == "mla":
                from_mla(P, C, hook)
            elif ph == "final":
                final_phase(P, C)
            elif ph == "store":
                store_x_raw(P, C)
    return nc


def dbg_dump(P, C, slot, ap, buf, parts=128, cols=TT):
    if not getattr(C, "debug", False):
        return
    if not hasattr(C, "dbg_sem"):
        C.dbg_sem = P.dma_sem("dbg_sem")
        C.dbg_b = Buf("dbg")
    ev = P.dma("pool", C.dbg[slot, 0:parts, 0:cols], ap, C.dbg_sem, reads=(buf,), writes=(C.dbg_b,))
    P.wait_event("pool", ev)


def attention_tile(P, C, j, nk_comp, score_fn, V, Vb_of, PT_ring, O, L, scale, mask_eng="pool", inject=None):
    nk = 4 * j + 4
    inject = inject or {}

    def S_stage(i):
        q0 = max(0, i - 4 * j) * 128
        res = []
        for c in range(nk_comp):
            ps, psb = C.ps_misc.next()
            fn, rd = score_fn(c, i, q0, ps)
            if i >= 4 * j and C.pe_mask:
                def fn_m(e, fn=fn, ps=ps, q0=q0):
                    fn(e)
                    return e.matmul(ps[:, q0:q0 + 128], lhsT=C.ident_bf[:, :], rhs=C.negm_bf[:, :], start=False, stop=True,
                                    skip_group_check=True)
                P.op("pe", fn_m, reads=(*rd, C.ident_b, C.negm_b), writes=(psb,))
            else:
                P.op("pe", fn, reads=rd, writes=(psb,))
            pt, ptb = PT_ring.next()
            P.op("act", lambda e, pt=pt, ps=ps, q0=q0: e.activation(out=pt[:, q0:TT], in_=ps[:, q0:TT], func=AF.Exp,
                                                                     scale=scale),
                 reads=(psb,), writes=(ptb,))
            if i >= 4 * j and not C.pe_mask:
                P.op(mask_eng, lambda e, pt=pt, q0=q0: e.tensor_tensor(out=pt[:, q0:q0 + 128], in0=pt[:, q0:q0 + 128],
                                                                       in1=C.mask_bf[:, :], op=ALU.mult),
                     reads=(ptb, C.mask_b), writes=(ptb,))
            res.append((pt, ptb))
        return res, q0

    def PV_stage(i, res, q0):
        sb0 = q0 // 128
        for c in range(nk_comp):
            pt, ptb = res[c]

            def fn(e, pt=pt, c=c):
                for sb in range(sb0, 4):
                    bank = O[c][sb // 2][0]
                    col = (sb % 2) * 129
                    ins = e.matmul(bank[:, col:col + 129], lhsT=pt[:, sb * 128:(sb + 1) * 128], rhs=V[:, i, :],
                                   start=(i == 0 and sb % 2 == 0), stop=(i == nk - 1 and sb == 3),
                                   skip_group_check=True)
                return ins
            P.op("pe", fn, reads=(ptb, Vb_of(i), C.vones_b), writes=(O[c][0][1], O[c][1][1]))

    depth = 2 if nk_comp == 1 else 1
    pend = []
    for i in range(nk):
        pend.append((i, S_stage(i)))
        for f in inject.get(i, ()):
            f()
        if len(pend) > depth:
            ii, st = pend.pop(0)
            PV_stage(ii, *st)
    for ii, st in pend:
        PV_stage(ii, *st)


def out_proj_add(P, C, j, on, onb, WO, WOb, dcs=tuple(range(NCH))):
    tsl = slice(j * TT, (j + 1) * TT)
    for dc in dcs:
        ps, psb = C.ps_misc.next()
        P.op("pe", lambda e, dc=dc, ps=ps: e.matmul(ps[:, :], lhsT=WO[:, dc * 128:(dc + 1) * 128], rhs=on[:, :],
                                                    start=True, stop=True),
             reads=(WOb, onb), writes=(psb,))
        P.op("dve", lambda e, dc=dc, ps=ps: e.tensor_tensor(out=C.X[:, dc, tsl], in0=ps[:, :], in1=C.X[:, dc, tsl],
                                                            op=ALU.add),
             reads=(psb, C.Xb[dc][j]), writes=(C.Xb[dc][j],))


def from_diff(P, C, hook=None):
    nc = C.nc
    lambda_init = 0.8 - 0.6 * math.exp(-0.3 * 0)
    gcol = 8
    with ExitStack() as es:
        xr = []
        for i in range(4):
            t = es.enter_context(SBT(nc, f"da_xn{i}", [128, NCH, TT], BF16))
            xr.append((t, [Buf(f"da_xn{i}_{c}") for c in range(NCH)], P.dma_sem(f"da_xs{i}")))
        xr = Ring(xr)
        sq_ring = mk_ring(es, nc, "da_sq", 4, [128, TT], BF16)
        f32_ring = mk_ring(es, nc, "da_f32", 3, [128, TT], F32)
        C.ps_misc = Ring(C.psum[0:4])
        for j in range(NT):
            xn, xnb, xsem = xr.next()
            norm_tile(P, C, j, gcol, [xn[:, c, :] for c in range(NCH)], xnb, sq_ring, f32_ring)
            P.dma("sp", C.xn_s[:, :, j * TT:(j + 1) * TT], xn[:, :, :], xsem, reads=tuple(xnb), writes=(C.xn_sb[j],))
        P.barrier()
        P.flush()
    with ExitStack() as es:
        XN = es.enter_context(SBT(nc, "d_xn", [128, NCH, TT], BF16))
        XNb = Buf("d_xn")
        W = es.enter_context(SBT(nc, "d_w", [128, NCH, 384], BF16))
        Wb = Buf("d_w")
        wsem = P.dma_sem("d_wsem")
        WOs = [(es.enter_context(SBT(nc, f"d_wo{i}", [128, D], BF16)), Buf(f"d_wo{i}"), P.dma_sem(f"d_wos{i}"))
               for i in range(2)]
        KT = es.enter_context(SBT(nc, "d_kt", [128, S], BF16))
        KTb = [Buf(f"d_kt{t}") for t in range(NT)]
        V = es.enter_context(SBT(nc, "d_v", [128, S // 128, 129], BF16))
        Vb = [Buf(f"d_v{t}") for t in range(NT)]
        C.vones_b = Buf("d_vones")
        P.op("pool", lambda e: e.memset(V[:, :, 128:129], 1.0), writes=(C.vones_b,))
        conv_pieces = hook() if hook is not None else []
        qt_ring = mk_ring(es, nc, "d_qt", 2, [128, TT], BF16)
        pt_ring = mk_ring(es, nc, "d_pt", 4, [128, TT], BF16)
        rope_ring = mk_ring(es, nc, "d_rope", 4, [128, TT], F32)
        def fine_ring(name, n, shape, dt, mk):
            return Ring([(es.enter_context(SBT(nc, f"{name}{i}", shape, dt)), mk(i)) for i in range(n)])
        fin_ring = fine_ring("d_fin", 3, [128, TT], F32, lambda i: [Buf(f"d_fin{i}_{sb}") for sb in range(4)])
        onq_ring = fine_ring("d_onq", 2, [128, TT], BF16, lambda i: [Buf(f"d_onq{i}_{sb}") for sb in range(4)])
        junk_ring = fine_ring("d_junk", 2, [128, TT], BF16, lambda i: [Buf(f"d_junk{i}_{sb}") for sb in range(4)])
        small_ring = fine_ring("d_small", 3, [128, 16], F32, lambda i: {
            "rb": [[Buf(f"d_r{i}_{c}{hb}") for hb in range(2)] for c in range(2)],
            "ssb": [Buf(f"d_ss{i}_{sb}") for sb in range(4)], "rsb": Buf(f"d_rs{i}")})
        CS = es.enter_context(SBT(nc, "d_cs", [128, 2, TT], F32))
        CSb = Buf("d_cs")
        cssem = P.dma_sem("d_cssem")
        on_ring = mk_ring(es, nc, "d_on", 2, [128, TT], BF16)
        lam_sb = es.enter_context(SBT(nc, "d_lam", [128, 256], F32))
        lam_b = Buf("d_lam")
        sm = es.enter_context(SBT(nc, "d_sm", [128, 8], F32))
        sm_b = Buf("d_sm")
        lsem = P.dma_sem("d_lsem")
        xnsem = P.dma_sem("d_xnsem")
        C.ps_misc = Ring(C.psum[0:4])
        O = [[C.psum[4], C.psum[5]], [C.psum[6], C.psum[7]]]
        L = None

        P.dma("sp", lam_sb[:, :], C.lamv[:, :], lsem, writes=(lam_b,))
        P.op("dve", lambda e: e.tensor_tensor(out=lam_sb[:, 0:64], in0=lam_sb[:, 0:64], in1=lam_sb[:, 64:128],
                                              op=ALU.mult), reads=(lam_b,), writes=(lam_b,))
        P.op("dve", lambda e: e.tensor_tensor(out=lam_sb[:, 128:192], in0=lam_sb[:, 128:192], in1=lam_sb[:, 192:256],
                                              op=ALU.mult), reads=(lam_b,), writes=(lam_b,))
        P.op("dve", lambda e: e.reduce_sum(out=sm[:, 0:1], in_=lam_sb[:, 0:64], axis=AX.X), reads=(lam_b,),
             writes=(sm_b,))
        P.op("dve", lambda e: e.reduce_sum(out=sm[:, 1:2], in_=lam_sb[:, 128:192], axis=AX.X), reads=(lam_b,),
             writes=(sm_b,))
        P.op("act", lambda e: e.activation(out=sm[:, 2:4], in_=sm[:, 0:2], func=AF.Exp), reads=(sm_b,), writes=(sm_b,))
        P.op("dve", lambda e: e.tensor_tensor(out=sm[:, 4:5], in0=sm[:, 3:4], in1=sm[:, 2:3], op=ALU.subtract),
             reads=(sm_b,), writes=(sm_b,))
        P.op("dve", lambda e: e.tensor_scalar(out=sm[:, 4:5], in0=sm[:, 4:5], scalar1=-lambda_init, scalar2=None,
                                              op0=ALU.add), reads=(sm_b,), writes=(sm_b,))
        P.op("dve", lambda e: e.tensor_scalar(out=sm[:, 5:6], in0=C.gains_sb[:, 56:57], scalar1=1.0 - lambda_init,
                                              scalar2=None, op0=ALU.mult), reads=(sm_b, C.gains_b), writes=(sm_b,))

        src_in = C.s["diff_in"].rearrange("(c p) f -> p c f", p=128)
        heads = list(range(C.head0, C.head0 + C.nheads))
        seq = [(h, j) for h in heads for j in range(C.ntiles)]
        QT_of, fin_of, on_of = {}, {}, {}

        def load_w(h):
            pairs = [(W[:, :, 0:128], src_in[:, :, h * 128:(h + 1) * 128]),
                     (W[:, :, 128:256], src_in[:, :, D + h * 128:D + (h + 1) * 128]),
                     (W[:, :, 256:384], src_in[:, :, 2 * D + h * 128:2 * D + (h + 1) * 128])]
            P.dma_group("sp", pairs, wsem, reads=(C.sbuf_["diff_in"],), writes=(Wb,))
            wo, wob, wosem = WOs[h % 2]
            P.dma("sp", wo[:, :], C.s["diff_out"][h * 128:(h + 1) * 128, :], wosem, reads=(C.sbuf_["diff_out"],),
                  writes=(wob,))

        def PROJ(h, j):
            tsl = slice(j * TT, (j + 1) * TT)
            st = {}

            def proj(e, ps, c0):
                for c in range(NCH):
                    ins = e.matmul(ps[:, :], lhsT=W[:, c, c0:c0 + 128], rhs=XN[:, c, :],
                                   start=(c == 0), stop=(c == NCH - 1))
                return ins

            def p_load():
                P.dma("sp", XN[:, :, :], C.xn_s[:, :, tsl], xnsem, reads=(C.xn_sb[j],), writes=(XNb,))
                P.dma_group("sp", [(CS[:, 0, :], C.cs_s[0, :, tsl]), (CS[:, 1, :], C.cs_s[1, :, tsl])], cssem,
                            reads=(C.cs_sb[j],), writes=(CSb,))

            def p_q():
                st["psq"] = C.ps_misc.next()
                P.op("pe", lambda e: proj(e, st["psq"][0], 0), reads=(Wb, XNb), writes=(st["psq"][1],))
                st["q32"] = rope_ring.next()
                P.op("dve", lambda e: e.tensor_copy(out=st["q32"][0][:, :], in_=st["psq"][0][:, :]),
                     reads=(st["psq"][1],), writes=(st["q32"][1],))

            def p_k():
                st["psk"] = C.ps_misc.next()
                P.op("pe", lambda e: proj(e, st["psk"][0], 128), reads=(Wb, XNb), writes=(st["psk"][1],))
                st["k32"] = rope_ring.next()
                P.op("dve", lambda e: e.tensor_copy(out=st["k32"][0][:, :], in_=st["psk"][0][:, :]),
                     reads=(st["psk"][1],), writes=(st["k32"][1],))

            def p_v():
                psv, psvb = C.ps_misc.next()

                def vproj(e):
                    for sblk in range(4):
                        for c in range(NCH):
                            ins = e.matmul(psv[:, sblk * 128:(sblk + 1) * 128], lhsT=XN[:, c, sblk * 128:(sblk + 1) * 128],
                                           rhs=W[:, c, 256:384], start=(c == 0), stop=(c == NCH - 1))
                    return ins
                P.op("pe", vproj, reads=(Wb, XNb), writes=(psvb,))
                P.op("dve", lambda e: e.tensor_copy(out=V[:, j * 4:(j + 1) * 4, 0:128],
                                                    in_=psv[:, :].rearrange("p (s e) -> p s e", s=4)),
                     reads=(psvb,), writes=(Vb[j],))

            def mk_rope(src, out_fn):
                def f():
                    x32, x32b = st[src]
                    out_ap, out_b = out_fn()
                    pr, prb = C.ps_misc.next()
                    P.op("pe", lambda e: e.matmul(pr[:, :], lhsT=C.rot32[:, :], rhs=x32[:, :], start=True, stop=True),
                         reads=(x32b, C.rot_b), writes=(prb,))
                    t2, t2b = rope_ring.next()
                    P.op("dve", lambda e: e.tensor_tensor(out=t2[:, :], in0=pr[:, :], in1=CS[:, 1, :], op=ALU.mult),
                         reads=(prb, CSb), writes=(t2b,))
                    P.op("dve", lambda e: e.tensor_tensor(out=x32[:, :], in0=x32[:, :], in1=CS[:, 0, :], op=ALU.mult),
                         reads=(x32b, CSb), writes=(x32b,))
                    P.op("dve", lambda e: e.tensor_tensor(out=out_ap, in0=x32[:, :], in1=t2[:, :], op=ALU.add),
                         reads=(x32b, t2b), writes=(out_b,))
                return f

            def q_out():
                QT, QTb = qt_ring.next()
                QT_of[(h, j)] = (QT, QTb)
                return QT[:, :], QTb

            return [p_load, p_q, p_k, p_v, mk_rope("q32", q_out), mk_rope("k32", lambda: (KT[:, tsl], KTb[j]))]

        def EVAC_FIN1(h, j):
            sm8, sb_ = small_ring.next()
            rb, ssb, rsb = sb_["rb"], sb_["ssb"], sb_["rsb"]
            for c in range(2):
                for hb in range(2):
                    bank, bb = O[c][hb]
                    P.op("dve", lambda e, bank=bank, c=c, hb=hb: e.reciprocal(out=sm8[:, c * 4 + hb * 2:c * 4 + hb * 2 + 2],
                                                                              in_=bank[:, 128:258:129]),
                         reads=(bb,), writes=(rb[c][hb],))
            P.op("dve", lambda e: e.tensor_scalar(out=sm8[:, 4:8], in0=sm8[:, 4:8], scalar1=sm[:, 4:5], scalar2=None,
                                                  op0=ALU.mult), reads=(rb[1][0], rb[1][1], sm_b), writes=(rb[1][0], rb[1][1]))
            o4, o4b = fin_ring.next()
            for sb in range(4):
                col = (sb % 2) * 129
                b0, b0b = O[0][sb // 2]
                osl = slice(sb * 128, (sb + 1) * 128)
                P.op("dve", lambda e, b0=b0, col=col, osl=osl, sb=sb: e.tensor_scalar(
                    out=o4[:, osl], in0=b0[:, col:col + 128], scalar1=sm8[:, sb:sb + 1], scalar2=None, op0=ALU.mult),
                    reads=(b0b, rb[0][sb // 2]), writes=(o4b[sb],))
            for sb in range(4):
                col = (sb % 2) * 129
                b1, b1b = O[1][sb // 2]
                osl = slice(sb * 128, (sb + 1) * 128)
                P.op("dve", lambda e, b1=b1, col=col, osl=osl, sb=sb: e.scalar_tensor_tensor(
                    out=o4[:, osl], in0=b1[:, col:col + 128], scalar=sm8[:, 4 + sb:5 + sb], in1=o4[:, osl],
                    op0=ALU.mult, op1=ALU.add),
                    reads=(b1b, rb[1][sb // 2], o4b[sb]), writes=(o4b[sb],))
            P.op("dve", lambda e: e.memset(sm8[:, 8:12], 0.0), reads=(), writes=tuple(ssb))
            junk, junkb = junk_ring.next()
            for sb in range(4):
                osl = slice(sb * 128, (sb + 1) * 128)
                P.op("act", lambda e, osl=osl, sb=sb: e.activation(out=junk[:, osl], in_=o4[:, osl], func=AF.Square,
                                                                   accum_out=sm8[:, 8 + sb:9 + sb]),
                     reads=(o4b[sb], ssb[sb]), writes=(junkb[sb], ssb[sb]))
            P.op("act", lambda e: e.activation(out=sm8[:, 12:16], in_=sm8[:, 8:12], func=AF.Ln, bias=C.cst[:, 1:2],
                                               scale=1.0 / 128), reads=(*ssb, C.cst_b), writes=(rsb,))
            P.op("act", lambda e: e.activation(out=sm8[:, 12:16], in_=sm8[:, 12:16], func=AF.Exp, scale=-0.5),
                 reads=(rsb,), writes=(rsb,))
            onq, onqb = onq_ring.next()
            for sb in range(4):
                osl = slice(sb * 128, (sb + 1) * 128)
                P.op("dve", lambda e, osl=osl, sb=sb: e.tensor_scalar(out=onq[:, osl], in0=o4[:, osl],
                                                                      scalar1=sm8[:, 12 + sb:13 + sb], scalar2=None,
                                                                      op0=ALU.mult),
                     reads=(o4b[sb], rsb), writes=(onqb[sb],))
            fin_of[(h, j)] = (onq, onqb)

        def FIN2(h, j):
            onq, onqb = fin_of.pop((h, j))
            ps, psb = C.ps_misc.next()

            def tr(e):
                for sb in range(4):
                    osl = slice(sb * 128, (sb + 1) * 128)
                    ins = e.matmul(ps[:, osl], lhsT=onq[:, osl], rhs=C.ident_bf[:, :], start=True, stop=True)
                return ins
            P.op("pe", tr, reads=(*onqb, C.ident_b), writes=(psb,))
            on, onb = on_ring.next()
            P.op("act", lambda e: e.activation(out=on[:, :], in_=ps[:, :], func=AF.Copy, scale=sm[:, 5:6]),
                 reads=(psb, sm_b), writes=(onb,))
            on_of[(h, j)] = (on, onb)

        def OUT(h, j):
            def piece(dc):
                def f():
                    on, onb = on_of[(h, j)]
                    wo, wob, _ = WOs[h % 2]
                    out_proj_add(P, C, j, on, onb, wo, wob, dcs=(dc,))
                return f
            return [piece(dc) for dc in range(NCH)]

        def spread(inject, pieces, start, nk):
            for k, f in enumerate(pieces):
                inject.setdefault(min(start + k, nk - 1), []).append(f)

        load_w(seq[0][0])
        for f in PROJ(*seq[0]):
            f()
        for idx, (h, j) in enumerate(seq):
            nk = 4 * j + 4
            inject = {}
            post = []
            if idx > 0:
                ph, pj = seq[idx - 1]
                inject.setdefault(1, []).append(lambda ph=ph, pj=pj: FIN2(ph, pj))
                spread(inject, OUT(ph, pj), 2, nk)
            if idx + 1 < len(seq):
                nh, nj = seq[idx + 1]
                pieces = PROJ(nh, nj)
                if nh != h:
                    pieces = [lambda nh=nh: load_w(nh)] + pieces
                if nh == h:
                    spread(inject, pieces, 1, nk)
                elif nk > 5:
                    spread(inject, pieces, 5, nk)
                else:
                    post.extend(pieces)
            QT, QTb = QT_of.pop((h, j))

            def score_fn(c, i, q0, ps, QT=QT, QTb=QTb):
                def fn(e):
                    return e.matmul(ps[:, q0:TT], lhsT=KT[c * 64:(c + 1) * 64, i * 128:(i + 1) * 128],
                                    rhs=QT[c * 64:(c + 1) * 64, q0:TT], start=True, stop=True)
                return fn, (KTb[i // 4], QTb)
            C.pe_mask = False
            attention_tile(P, C, j, 2, score_fn, V, lambda i: Vb[i // 4], pt_ring, O, L, 0.125, inject=inject,
                           mask_eng="dve")
            C.pe_mask = True
            EVAC_FIN1(h, j)
            for f in post:
                f()
            if conv_pieces and idx % 3 == 0:
                conv_pieces.pop(0)()
        FIN2(*seq[-1])
        for f in OUT(*seq[-1]):
            f()
        for f in conv_pieces:
            f()
        P.barrier()
        P.flush()


def from_mla(P, C, hook=None):
    nc = C.nc
    gcol = 32
    scale = 192.0 ** -0.5
    NQ = 1
    TQ = NT // NQ
    with ExitStack() as es_outer:
        CKVN = es_outer.enter_context(SBT(nc, "m_ckvn", [128, 2, S], BF16))
        CKVNb = [[Buf(f"m_ckvn{m}_{t}") for t in range(NT)] for m in range(2)]
        KR = es_outer.enter_context(SBT(nc, "m_kr", [128, S], BF16))
        KRb = [Buf(f"m_kr{t}") for t in range(NT)]
        wsem = P.dma_sem("m_wsem")
        cssem = P.dma_sem("m_cssem")
        win_src = C.s["mla_in"].rearrange("(c p) f -> p c f", p=128)
        wq_src = C.s["mla_q"].rearrange("(c p) f -> p c f", p=128)
        wkv_src = C.s["mla_kv"].rearrange("(c p) f -> p c f", p=128)

        def rope64(x32, x32b, CS, CSb, rope_ring, out_ap, out_b):
            pr, prb = C.ps_misc.next()
            P.op("pe", lambda e: e.matmul(pr[0:64, :], lhsT=C.rot32[0:64, 0:64], rhs=x32[0:64, :], start=True, stop=True),
                 reads=(x32b, C.rot_b), writes=(prb,))
            t2, t2b = rope_ring.next()
            P.op("dve", lambda e: e.tensor_tensor(out=t2[0:64, :], in0=pr[0:64, :], in1=CS[0:64, 1, :], op=ALU.mult),
                 reads=(prb, CSb), writes=(t2b,))
            P.op("dve", lambda e: e.tensor_tensor(out=x32[0:64, :], in0=x32[0:64, :], in1=CS[0:64, 0, :], op=ALU.mult),
                 reads=(x32b, CSb), writes=(x32b,))
            P.op("dve", lambda e: e.tensor_tensor(out=out_ap, in0=x32[0:64, :], in1=t2[0:64, :], op=ALU.add),
                 reads=(x32b, t2b), writes=(out_b,))

        for qp in range(NQ):
            with ExitStack() as es:
                WIN = es.enter_context(SBT(nc, "m_win", [128, NCH, 704], BF16))
                WINb = Buf("m_win")
                XN = es.enter_context(SBT(nc, "m_xn", [128, NCH, TT], BF16))
                XNb = [Buf(f"m_xn{c}") for c in range(NCH)]
                sq_ring = mk_ring(es, nc, "mc_sq", 2, [128, TT], BF16)
                f32_ring = mk_ring(es, nc, "mc_f32", 9, [128, TT], F32)
                CS = es.enter_context(SBT(nc, "mc_cs", [128, 2, TT], F32))
                CSb = Buf("mc_cs")
                cq_stage = Ring([(es.enter_context(SBT(nc, f"mc_cq{i}", [128, 3, TT], BF16)),
                                  [Buf(f"mc_cq{i}_{m}") for m in range(3)], P.dma_sem(f"mc_cqs{i}")) for i in range(2)])
                C.ps_misc = Ring(C.psum[0:8])
                P.dma("sp", WIN[:, :, :], win_src[:, :, :], wsem, reads=(C.sbuf_["mla_in"],), writes=(WINb,))
                for tl in range(TQ):
                    j = qp * TQ + tl
                    tsl = slice(j * TT, (j + 1) * TT)
                    norm_tile(P, C, j, gcol, [XN[:, c, :] for c in range(NCH)], XNb, sq_ring, f32_ring, sq_eng="pool")
                    P.dma_group("sp", [(CS[0:64, 0, :], C.cs_s[0, 0:64, tsl]), (CS[0:64, 1, :], C.cs_s[1, 0:64, tsl])], cssem,
                                reads=(C.cs_sb[j],), writes=(CSb,))
                    c32 = []
                    for m in (5, 0, 1, 2, 3, 4):
                        width = 128 if m < 5 else 64
                        ps, psb = C.ps_misc.next()

                        def proj(e, ps=ps, m=m, width=width):
                            for c in range(NCH):
                                ins = e.matmul(ps[0:width, :], lhsT=WIN[:, c, m * 128:m * 128 + width], rhs=XN[:, c, :],
                                               start=(c == 0), stop=(c == NCH - 1))
                            return ins
                        P.op("pe", proj, reads=(WINb, *XNb), writes=(psb,))
                        t32, t32b = f32_ring.next()
                        P.op("act", lambda e, t32=t32, ps=ps, width=width: e.activation(out=t32[0:width, :], in_=ps[0:width, :],
                                                                                       func=AF.Copy),
                             reads=(psb,), writes=(t32b,))
                        if m < 5:
                            c32.append((t32, t32b))
                        else:
                            rope64(t32, t32b, CS, CSb, f32_ring, KR[0:64, tsl], KRb[j])
                    rq, rqb = norm_stats(P, C, [c32[m][0][:, :] for m in range(3)], [c32[m][1] for m in range(3)], 384,
                                         sq_ring, f32_ring, sq_eng="pool")
                    cqt, cqtb, cqsem = cq_stage.next()
                    for m in range(3):
                        P.op("dve", lambda e, m=m, rq=rq, c32=c32, cqt=cqt: e.scalar_tensor_tensor(
                            out=cqt[:, m, :], in0=c32[m][0][:, :], scalar=C.gains_sb[:, 57 + m:58 + m],
                            in1=rq[:, :], op0=ALU.mult, op1=ALU.mult),
                            reads=(c32[m][1], rqb, C.gains_b), writes=(cqtb[m],))
                    P.dma("sp", C.cq_s[:, :, tsl], cqt[:, :, :], cqsem, reads=tuple(cqtb), writes=(C.cq_sb[j],))
                    rk, rkb = norm_stats(P, C, [c32[3 + m][0][:, :] for m in range(2)], [c32[3 + m][1] for m in range(2)], 256,
                                         sq_ring, f32_ring, sq_eng="pool")
                    for m in range(2):
                        P.op("dve", lambda e, m=m, j=j, rk=rk, c32=c32: e.scalar_tensor_tensor(
                            out=CKVN[:, m, j * TT:(j + 1) * TT], in0=c32[3 + m][0][:, :], scalar=C.gains_sb[:, 60 + m:61 + m],
                            in1=rk[:, :], op0=ALU.mult, op1=ALU.mult),
                            reads=(c32[3 + m][1], rkb, C.gains_b), writes=(CKVNb[m][j],))
                P.barrier()
                P.flush()
            with ExitStack() as es:
                WQ = es.enter_context(SBT(nc, "m_wq", [128, 3, 192], BF16))
                WKV = es.enter_context(SBT(nc, "m_wkv", [128, 2, 256], BF16))
                Wb = Buf("m_w")
                WOs = [(es.enter_context(SBT(nc, f"m_wo{i}", [128, D], BF16)), Buf(f"m_wo{i}"), P.dma_sem(f"m_wos{i}"))
                       for i in range(2)]
                KN = es.enter_context(SBT(nc, "m_kn", [128, S], BF16))
                KNb = [Buf(f"m_kn{t}") for t in range(NT)]
                V = es.enter_context(SBT(nc, "m_v", [128, S // 128, 129], BF16))
                Vb = [Buf(f"m_v{t}") for t in range(NT)]
                C.vones_b = Buf("m_vones")
                P.op("pool", lambda e: e.memset(V[:, :, 128:129], 1.0), writes=(C.vones_b,))
                conv_pieces = hook() if hook is not None else []
                cqt_ring = Ring([(es.enter_context(SBT(nc, f"md_cq{i}", [128, 3, TT], BF16)), Buf(f"md_cq{i}"),
                                  P.dma_sem(f"md_cqs{i}")) for i in range(2)])
                qn_ring = mk_ring(es, nc, "m_qn", 2, [128, TT], BF16)
                qr_ring = mk_ring(es, nc, "m_qr", 2, [128, TT], BF16)
                pt_ring = mk_ring(es, nc, "m_pt", 4, [128, TT], BF16)
                rope_ring = mk_ring(es, nc, "md_rope", 2, [128, TT], F32)
                onq_ring = Ring([(es.enter_context(SBT(nc, f"md_onq{i}", [128, TT], BF16)),
                                  [Buf(f"md_onq{i}_{sb}") for sb in range(4)]) for i in range(2)])
                small_ring = Ring([(es.enter_context(SBT(nc, f"md_small{i}", [128, 8], F32)),
                                    [Buf(f"md_r{i}_{hb}") for hb in range(2)]) for i in range(3)])
                CS = es.enter_context(SBT(nc, "md_cs", [128, 2, TT], F32))
                CSb = Buf("md_cs")
                on_ring = mk_ring(es, nc, "m_on", 2, [128, TT], BF16)
                C.ps_misc = Ring(C.psum[0:4])
                Oring = Ring([[C.psum[4], C.psum[5]], [C.psum[6], C.psum[7]]])
                nkt = (qp + 1) * TQ
                heads = list(range(C.head0, C.head0 + C.nheads))
                seq = [(h, tl) for h in heads for tl in range(TQ)]
                Q_of, on_of, fin_of = {}, {}, {}

                def load_w(h):
                    pairs = [(WQ[:, :, :], wq_src[:, :, h * 192:(h + 1) * 192]),
                             (WKV[:, :, :], wkv_src[:, :, h * 256:(h + 1) * 256])]
                    P.dma_group("sp", pairs, wsem, reads=(C.sbuf_["mla_q"], C.sbuf_["mla_kv"]), writes=(Wb,))
                    wo, wob, wosem = WOs[h % 2]
                    P.dma("sp", wo[:, :], C.s["mla_out"][h * 128:(h + 1) * 128, :], wosem, reads=(C.sbuf_["mla_out"],),
                          writes=(wob,))

                def KV(h, jjs=None):
                    for jj in (range(nkt) if jjs is None else jjs):
                        ps, psb = C.ps_misc.next()

                        def kproj(e, ps=ps, jj=jj):
                            for m in range(2):
                                ins = e.matmul(ps[:, :], lhsT=WKV[:, m, 0:128], rhs=CKVN[:, m, jj * TT:(jj + 1) * TT],
                                               start=(m == 0), stop=(m == 1))
                            return ins
                        P.op("pe", kproj, reads=(Wb, CKVNb[0][jj], CKVNb[1][jj]), writes=(psb,))
                        P.op("act", lambda e, ps=ps, jj=jj: e.activation(out=KN[:, jj * TT:(jj + 1) * TT], in_=ps[:, :],
                                                                         func=AF.Copy),
                             reads=(psb,), writes=(KNb[jj],))
                        ps, psb = C.ps_misc.next()

                        def vproj(e, ps=ps, jj=jj):
                            for sblk in range(4):
                                t0 = jj * TT + sblk * 128
                                for m in range(2):
                                    ins = e.matmul(ps[:, sblk * 128:(sblk + 1) * 128], lhsT=CKVN[:, m, t0:t0 + 128],
                                                   rhs=WKV[:, m, 128:256], start=(m == 0), stop=(m == 1))
                            return ins
                        P.op("pe", vproj, reads=(Wb, CKVNb[0][jj], CKVNb[1][jj]), writes=(psb,))
                        P.op("dve", lambda e, ps=ps, jj=jj: e.tensor_copy(out=V[:, jj * 4:(jj + 1) * 4, 0:128],
                                                                          in_=ps[:, :].rearrange("p (s e) -> p s e", s=4)),
                             reads=(psb,), writes=(Vb[jj],))

                def QPROJ(h, tl):
                    j = qp * TQ + tl
                    tsl = slice(j * TT, (j + 1) * TT)
                    st = {}
                    CQT, CQTb, cqsem = cqt_ring.next()
                    cq_rd = (CQTb,)

                    def qproj(e, ps, c0, width):
                        for m in range(3):
                            ins = e.matmul(ps[0:width, :], lhsT=WQ[:, m, c0:c0 + width], rhs=CQT[:, m, :],
                                           start=(m == 0), stop=(m == 2))
                        return ins

                    def p_load():
                        P.dma("sp", CQT[:, :, :], C.cq_s[:, :, tsl], cqsem, reads=(C.cq_sb[j],), writes=(CQTb,))
                        P.dma_group("sp", [(CS[0:64, 0, :], C.cs_s[0, 0:64, tsl]), (CS[0:64, 1, :], C.cs_s[1, 0:64, tsl])],
                                    cssem, reads=(C.cs_sb[j],), writes=(CSb,))

                    def p_n():
                        psn, psnb = C.ps_misc.next()
                        P.op("pe", lambda e: qproj(e, psn, 0, 128), reads=(Wb, *cq_rd), writes=(psnb,))
                        QN, QNb = qn_ring.next()
                        P.op("dve", lambda e: e.tensor_copy(out=QN[:, :], in_=psn[:, :]), reads=(psnb,), writes=(QNb,))
                        st["qn"] = (QN, QNb)

                    def p_r():
                        psr, psrb = C.ps_misc.next()
                        P.op("pe", lambda e: qproj(e, psr, 128, 64), reads=(Wb, *cq_rd), writes=(psrb,))
                        q32, q32b = rope_ring.next()
                        P.op("dve", lambda e: e.tensor_copy(out=q32[0:64, :], in_=psr[0:64, :]), reads=(psrb,),
                             writes=(q32b,))
                        st["q32"] = (q32, q32b)

                    def p_rope():
                        q32, q32b = st["q32"]
                        QR, QRb = qr_ring.next()
                        rope64(q32, q32b, CS, CSb, rope_ring, QR[0:64, :], QRb)
                        Q_of[(h, tl)] = (*st["qn"], QR, QRb)

                    return [p_load, p_n, p_r, p_rope]

                def EVAC(h, tl, O1):
                    sm8, rbs = small_ring.next()
                    for hb in range(2):
                        bank, bb = O1[hb]
                        P.op("dve", lambda e, bank=bank, hb=hb: e.reciprocal(out=sm8[:, hb * 2:hb * 2 + 2],
                                                                             in_=bank[:, 128:258:129]),
                             reads=(bb,), writes=(rbs[hb],))
                    onq, onqb = onq_ring.next()
                    for sb in range(4):
                        col = (sb % 2) * 129
                        b0, b0b = O1[sb // 2]
                        osl = slice(sb * 128, (sb + 1) * 128)
                        P.op("dve", lambda e, b0=b0, col=col, osl=osl, sb=sb: e.tensor_scalar(
                            out=onq[:, osl], in0=b0[:, col:col + 128], scalar1=sm8[:, sb:sb + 1], scalar2=None,
                            op0=ALU.mult),
                            reads=(b0b, rbs[sb // 2]), writes=(onqb[sb],))
                    fin_of[(h, tl)] = (onq, onqb)

                def FIN2(h, tl):
                    onq, onqb = fin_of.pop((h, tl))
                    ps, psb = C.ps_misc.next()

                    def tr(e):
                        for sb in range(4):
                            osl = slice(sb * 128, (sb + 1) * 128)
                            ins = e.matmul(ps[:, osl], lhsT=onq[:, osl], rhs=C.ident_bf[:, :], start=True, stop=True)
                        return ins
                    P.op("pe", tr, reads=(*onqb, C.ident_b), writes=(psb,))
                    on, onb = on_ring.next()
                    P.op("act", lambda e: e.activation(out=on[:, :], in_=ps[:, :], func=AF.Copy), reads=(psb,), writes=(onb,))
                    on_of[(h, tl)] = (on, onb)

                def OUT(h, tl):
                    def piece(dc):
                        def f():
                            on, onb = on_of[(h, tl)]
                            wo, wob, _ = WOs[h % 2]
                            out_proj_add(P, C, qp * TQ + tl, on, onb, wo, wob, dcs=(dc,))
                        return f
                    return [piece(dc) for dc in range(NCH)]

                def spread(inject, pieces, start, nk):
                    for k, f in enumerate(pieces):
                        inject.setdefault(min(start + k, nk - 1), []).append(f)

                load_w(seq[0][0])
                KV(seq[0][0])
                for f in QPROJ(*seq[0]):
                    f()
                for idx, (h, tl) in enumerate(seq):
                    j = qp * TQ + tl
                    nk = 4 * j + 4
                    inject = {}
                    post = []
                    if idx > 0:
                        ph, ptl = seq[idx - 1]
                        inject.setdefault(1, []).append(lambda ph=ph, ptl=ptl: FIN2(ph, ptl))
                        spread(inject, OUT(ph, ptl), 2, nk)
                    if idx + 1 < len(seq):
                        nh, ntl = seq[idx + 1]
                        if nh == h:
                            spread(inject, QPROJ(nh, ntl), 1, nk)
                        else:
                            inject.setdefault(1, []).append(lambda nh=nh: load_w(nh))
                            for jj in range(nkt):
                                at = 4 * jj + 6
                                if at <= nk - 1:
                                    inject.setdefault(at, []).append(lambda nh=nh, jj=jj: KV(nh, (jj,)))
                                else:
                                    post.append(lambda nh=nh, jj=jj: KV(nh, (jj,)))
                            spread(inject, QPROJ(nh, ntl), max(2, nk - 5), nk)
                    QN, QNb, QR, QRb = Q_of.pop((h, tl))

                    def score_fn(c, i, q0, ps, QN=QN, QNb=QNb, QR=QR, QRb=QRb):
                        def fn(e):
                            e.matmul(ps[:, q0:TT], lhsT=KN[:, i * 128:(i + 1) * 128], rhs=QN[:, q0:TT],
                                     start=True, stop=False)
                            return e.matmul(ps[:, q0:TT], lhsT=KR[0:64, i * 128:(i + 1) * 128], rhs=QR[0:64, q0:TT],
                                            start=False, stop=True)
                        return fn, (KNb[i // 4], KRb[i // 4], QNb, QRb)
                    O1 = Oring.next()
                    attention_tile(P, C, j, 1, score_fn, V, lambda i: Vb[i // 4], pt_ring, [O1], None, scale, inject=inject)
                    EVAC(h, tl, O1)
                    for f in post:
                        f()
                    if conv_pieces and idx % 3 == 0:
                        conv_pieces.pop(0)()
                FIN2(*seq[-1])
                for f in OUT(*seq[-1]):
                    f()
                for f in conv_pieces:
                    f()
                P.barrier()
                P.flush()


def host_constants():
    ones = np.ones((128, 128), np.float32)
    rot = np.zeros((128, 128), np.float32)
    for m in range(128):
        if (m % 64) < 32:
            rot[m + 32, m] = -1.0
        else:
            rot[m - 32, m] = 1.0
    mask = (np.arange(128)[:, None] <= np.arange(128)[None, :]).astype(np.float32)
    inv_freq = (10000.0 ** (-np.arange(0, 64, 2, dtype=np.float32) / np.float32(64))).astype(np.float32)
    af = (inv_freq.astype(np.float64) / (2.0 * math.pi)).astype(np.float32)
    af = np.tile(af, 4).reshape(128, 1)
    return ones, rot, mask, af


def pack_gains(inp):
    g = np.zeros((128, 64), np.float32)

    def put(col, vec):
        v = np.asarray(vec, np.float32).reshape(-1, 128)
        for i in range(v.shape[0]):
            g[:, col + i] = v[i]
    for l in range(2):
        put((l * 3 + 0) * 8, inp["ffn1_norm"][l])
        put((l * 3 + 1) * 8, inp["mix_norm"][l])
        put((l * 3 + 2) * 8, inp["ffn2_norm"][l])
    put(48, inp["final_norm"])
    put(56, inp["diff_sub_norm"][0])
    put(57, inp["mla_q_norm"][0])
    put(60, inp["mla_kv_norm"][0])
    return g


def make_in_maps(inp, xT_list):
    ones, rot, mask, af = host_constants()
    gains = pack_gains(inp)
    lamv = np.concatenate([np.asarray(inp[k], np.float32).reshape(-1) for k in
                           ("diff_lambda_q1", "diff_lambda_k1", "diff_lambda_q2", "diff_lambda_k2")])
    lamv = np.ascontiguousarray(np.broadcast_to(lamv[None, :], (128, 256)))
    pos = np.asarray(inp["positions"], np.int32)
    maps = []
    for b, xT in enumerate(xT_list):
        m = {
            "xT": xT,
            "pos": np.ascontiguousarray(np.broadcast_to(pos[b][None, :], (128, S))),
            "gains": gains, "lamv": lamv, "ones_c": ones, "rot_c": rot, "mask_c": mask, "af_c": af,
            "ident_c": np.eye(128, dtype=np.float32),
        }
        for k in ("ffn1_w_gate", "ffn1_w_up", "ffn1_w_down", "ffn2_w_gate", "ffn2_w_up", "ffn2_w_down",
                  "diff_w_in", "diff_w_out", "mla_w_in", "mla_w_q_up", "mla_w_kv_up", "mla_w_out"):
            m[k] = np.asarray(inp[k], np.float32)
        maps.append(m)
    return maps


_NC_CACHE = {}


def run_phases(inp, xT_list, phases, core_ids=None, **bkw):
    key = (tuple(phases), tuple(sorted(bkw.items())))
    if key not in _NC_CACHE:
        _NC_CACHE[key] = build(phases, **bkw)
    nc = _NC_CACHE[key]
    maps = make_in_maps(inp, xT_list)
    if core_ids is None:
        core_ids = list(range(len(xT_list)))
    res = run_bass_kernel_spmd(nc, maps, core_ids=core_ids)
    if bkw.get("debug"):
        return [r["outT"] for r in res.results], [r["dbg"] for r in res.results]
    return [r["outT"] for r in res.results]


LAUNCH_PLAN = [ALL_PHASES]


def kernel(**inputs):
    x = np.asarray(inputs["x"], np.float32)
    B = x.shape[0]
    xT = [np.ascontiguousarray(x[b].T) for b in range(B)]
    cur = xT
    for phases in LAUNCH_PLAN:
        cur = run_phases(inputs, cur, list(phases))
    out = np.stack([np.ascontiguousarray(o.T) for o in cur], axis=0)
    return out.astype(np.float32)
```
